# Optimizing a Trainium2 kernel written in Bass

```python
import math
import jax, jax.numpy as jnp
from jax import lax
import numpy as np

D_MODEL = 1024
BATCH = 1
SEQ = 16384
DEPTH = 4
DEC_BATCH = 16
DEC_SEQ = 16
PAST_LEN = 4096

CHUNK = 64
N_PREV_CHUNKS = 8
BAND_PAST = N_PREV_CHUNKS * CHUNK
BAND = BAND_PAST + CHUNK
N_HEADS = 16
HEAD_DIM = D_MODEL // N_HEADS
D_FF = 4 * D_MODEL
CONV_W = 3
REL_MAX = 128
N_A = DEPTH // 2
N_B = DEPTH - N_A
ALPHA = (2.0 * DEPTH) ** 0.25
BETA = (8.0 * DEPTH) ** -0.25
LN_EPS = 1e-5

kernel_name = "yoco_shortconv_chunkband_deepnorm_step"


def _layer_norm(x, g, b):
    xf = x.astype(jnp.float32)
    mu = jnp.mean(xf, axis=-1, keepdims=True)
    var = jnp.mean(jnp.square(xf - mu), axis=-1, keepdims=True)
    y = (xf - mu) * lax.rsqrt(var + LN_EPS) * g.astype(jnp.float32) + b.astype(jnp.float32)
    return y.astype(x.dtype)


def _sq_relu_mlp(x, w_up, w_down):
    return jnp.square(jax.nn.relu(x @ w_up)) @ w_down


def _short_conv_mixer(x, prev, w_in, conv_w, w_out):
    t = x.shape[1]
    b_gate, c_gate, h = jnp.split(x @ w_in, 3, axis=-1)
    z = c_gate * h
    zp = jnp.concatenate([prev.astype(z.dtype), z], axis=1)
    y = conv_w[0] * zp[:, 0:t]
    for j in range(1, CONV_W):
        y = y + conv_w[j] * zp[:, j:j + t]
    return (b_gate * y) @ w_out, zp[:, -(CONV_W - 1):]


def _rel_bias(table, rel):
    idx = jnp.clip(rel, -REL_MAX, REL_MAX) + REL_MAX
    return jnp.transpose(table[idx], (2, 0, 1))


def _attend(q, k, v, bias, valid):
    s = jnp.einsum('...qhd,...khd->...hqk', q, k).astype(jnp.float32) * (HEAD_DIM ** -0.5)
    s = s + bias.astype(jnp.float32)
    if valid is not None:
        s = jnp.where(valid[..., None, None, :], s, -jnp.inf)
    p = jax.nn.softmax(s, axis=-1).astype(v.dtype)
    return jnp.einsum('...hqk,...khd->...qhd', p, v)


def _band_attention_prompt(q, k, v, table):
    b, s = q.shape[0], q.shape[1]
    nc = s // CHUNK
    pad = ((0, 0), (BAND_PAST, 0), (0, 0), (0, 0))
    kp, vp = jnp.pad(k, pad), jnp.pad(v, pad)
    idx = jnp.arange(nc)[:, None] * CHUNK + jnp.arange(BAND)[None, :]
    kb, vb = kp[:, idx], vp[:, idx]
    valid = idx >= BAND_PAST
    qc = q.reshape(b, nc, CHUNK, N_HEADS, HEAD_DIM)
    rel = jnp.arange(CHUNK)[:, None] + BAND_PAST - jnp.arange(BAND)[None, :]
    o = _attend(qc, kb, vb, _rel_bias(table, rel), valid)
    return o.reshape(b, s, D_MODEL)


def _band_attention_step(q, k_new, v_new, k_cache, v_cache, table):
    b, t = q.shape[0], q.shape[1]
    w = k_cache.shape[1]
    k = jnp.concatenate([k_cache.astype(k_new.dtype), k_new], axis=1)
    v = jnp.concatenate([v_cache.astype(v_new.dtype), v_new], axis=1)
    rel = w + jnp.arange(t)[:, None] - jnp.arange(w + t)[None, :]
    o = _attend(q, k, v, _rel_bias(table, rel), None)
    return o.reshape(b, t, D_MODEL)


def _trunk(x, conv_prev, k_cache, v_cache, ln_mix_g, ln_mix_b, ln_ffn_g, ln_ffn_b, w_up, w_down,
           w_in_a, conv_w_a, w_out_a, w_k, w_v, w_q_b, w_o_b, rel_bias_b):
    b, t = x.shape[0], x.shape[1]
    conv_states = []
    k = v = None
    for l in range(DEPTH):
        if l < N_A:
            h, st = _short_conv_mixer(x, conv_prev[l], w_in_a[l], conv_w_a[l], w_out_a[l])
            conv_states.append(st)
        else:
            if l == N_A:
                k = (x @ w_k).reshape(b, t, N_HEADS, HEAD_DIM)
                v = (x @ w_v).reshape(b, t, N_HEADS, HEAD_DIM)
            j = l - N_A
            q = (x @ w_q_b[j]).reshape(b, t, N_HEADS, HEAD_DIM)
            if k_cache is None:
                o = _band_attention_prompt(q, k, v, rel_bias_b[j])
            else:
                o = _band_attention_step(q, k, v, k_cache, v_cache, rel_bias_b[j])
            h = o @ w_o_b[j]
        x = _layer_norm(ALPHA * x + h, ln_mix_g[l], ln_mix_b[l])
        x = _layer_norm(ALPHA * x + _sq_relu_mlp(x, w_up[l], w_down[l]), ln_ffn_g[l], ln_ffn_b[l])
    return x, jnp.stack(conv_states, axis=0), k, v


def setup_inputs(seed: int = 0) -> dict:
    key = jax.random.key(seed)
    ks = jax.random.split(key, 20)
    f32 = jnp.float32
    d = D_MODEL
    w_cache = min(BAND_PAST, PAST_LEN)
    nrm = lambda k, shape, s: jax.random.normal(k, shape, f32) * s
    return {
        "x_prompt": nrm(ks[0], (BATCH, SEQ, d), 1.0),
        "x_sample": nrm(ks[1], (DEC_BATCH, DEC_SEQ, d), 1.0),
        "cache_conv": nrm(ks[2], (N_A, DEC_BATCH, CONV_W - 1, d), 1.0),
        "cache_k": nrm(ks[3], (DEC_BATCH, w_cache, N_HEADS, HEAD_DIM), 1.0),
        "cache_v": nrm(ks[4], (DEC_BATCH, w_cache, N_HEADS, HEAD_DIM), BETA),
        "ln_mix_g": 1.0 + nrm(ks[5], (DEPTH, d), 0.02),
        "ln_mix_b": nrm(ks[6], (DEPTH, d), 0.02),
        "ln_ffn_g": 1.0 + nrm(ks[7], (DEPTH, d), 0.02),
        "ln_ffn_b": nrm(ks[8], (DEPTH, d), 0.02),
        "w_up": nrm(ks[9], (DEPTH, d, D_FF), d ** -0.5),
        "w_down": nrm(ks[10], (DEPTH, D_FF, d), BETA * D_FF ** -0.5),
        "w_in_a": nrm(ks[11], (N_A, d, 3 * d), d ** -0.5),
        "conv_w_a": nrm(ks[12], (N_A, CONV_W, d), CONV_W ** -0.5),
        "w_out_a": nrm(ks[13], (N_A, d, d), BETA * d ** -0.5),
        "w_k": nrm(ks[14], (d, d), d ** -0.5),
        "w_v": nrm(ks[15], (d, d), BETA * d ** -0.5),
        "w_q_b": nrm(ks[16], (N_B, d, d), d ** -0.5),
        "w_o_b": nrm(ks[17], (N_B, d, d), BETA * d ** -0.5),
        "rel_bias_b": nrm(ks[18], (N_B, 2 * REL_MAX + 1, N_HEADS), 0.5),
    }


def reference(x_prompt, x_sample, cache_conv, cache_k, cache_v, ln_mix_g, ln_mix_b, ln_ffn_g,
              ln_ffn_b, w_up, w_down, w_in_a, conv_w_a, w_out_a, w_k, w_v, w_q_b, w_o_b, rel_bias_b):
    conv_zero = jnp.zeros((N_A, x_prompt.shape[0], CONV_W - 1, D_MODEL), x_prompt.dtype)
    y_prompt, conv_prompt, k_p, v_p = _trunk(
        x_prompt, conv_zero, None, None, ln_mix_g=ln_mix_g, ln_mix_b=ln_mix_b, ln_ffn_g=ln_ffn_g,
        ln_ffn_b=ln_ffn_b, w_up=w_up, w_down=w_down, w_in_a=w_in_a, conv_w_a=conv_w_a,
        w_out_a=w_out_a, w_k=w_k, w_v=w_v, w_q_b=w_q_b, w_o_b=w_o_b, rel_bias_b=rel_bias_b)
    y_sample, conv_sample, k_sample, v_sample = _trunk(
        x_sample, cache_conv, cache_k, cache_v, ln_mix_g=ln_mix_g, ln_mix_b=ln_mix_b,
        ln_ffn_g=ln_ffn_g, ln_ffn_b=ln_ffn_b, w_up=w_up, w_down=w_down, w_in_a=w_in_a,
        conv_w_a=conv_w_a, w_out_a=w_out_a, w_k=w_k, w_v=w_v, w_q_b=w_q_b, w_o_b=w_o_b,
        rel_bias_b=rel_bias_b)
    keep = min(BAND_PAST, x_prompt.shape[1])
    k_prompt = k_p[:, -keep:]
    v_prompt = v_p[:, -keep:]
    return (y_prompt, y_sample, conv_prompt, k_prompt, v_prompt, conv_sample, k_sample, v_sample)
```

```python
import numpy as np
from contextlib import ExitStack
import concourse.bass as bass
import concourse.mybir as mybir
from concourse.bass_utils import run_bass_kernel_spmd

F32 = mybir.dt.float32
BF16 = mybir.dt.bfloat16
AF = mybir.ActivationFunctionType
ALU = mybir.AluOpType

D = 1024
NCORES = 8
TPC = 2048
NMAIN = 4
HALO = 516
XROWS = HALO + TPC
ALPHA = 8.0 ** 0.25
EPS = 1e-5
NEG = -30000.0
NS = 3
SLOT = 8192
WA = 520


class Op:
    __slots__ = ("eng", "fn", "deps", "dma", "rdeps", "ev", "need", "tag")

    def __init__(self, eng, fn, deps, dma):
        self.eng, self.fn, self.deps, self.dma = eng, fn, deps, dma
        self.rdeps, self.ev, self.need = [], None, False


class Sched:
    ENGS = ("pe", "act", "dve", "pool", "sp")

    def __init__(self):
        self.ops = []
        self.lastw = {}
        self.readers = {}
        self.tag = ""

    def add(self, eng, fn, reads=(), writes=(), dma=None):
        i = len(self.ops)
        deps = set()
        for r in reads:
            w = self.lastw.get(r)
            if w is not None:
                deps.add(w)
        for r in writes:
            w = self.lastw.get(r)
            if w is not None:
                deps.add(w)
            for j in self.readers.get(r, {}).values():
                deps.add(j)
        chan = ("dma", dma) if dma is not None else eng
        for r in reads:
            self.readers.setdefault(r, {})[chan] = i
        for r in writes:
            self.lastw[r] = i
            self.readers[r] = {}
        deps.discard(i)
        op = Op(eng, fn, deps, dma)
        op.tag = self.tag
        self.ops.append(op)
        return i

    def chan(self, op):
        return ("dma", op.dma) if op.dma is not None else op.eng

    def finalize(self):
        ops = self.ops
        for op in ops:
            by = {}
            for d in op.deps:
                c = self.chan(ops[d])
                if c == "pe" and op.eng == "pe" and op.dma is None:
                    continue
                if by.get(c, -1) < d:
                    by[c] = d
            op.rdeps = sorted(by.values())
            for d in op.rdeps:
                ops[d].need = True


class PsumRR:
    def __init__(self, banks):
        self.banks = list(banks)
        self.i = 0

    def next(self):
        b = self.banks[self.i % len(self.banks)]
        self.i += 1
        return b


class Tile:
    pass


def mk_tile(kind, idx=0):
    t = Tile()
    t.kind = kind
    t.idx = idx
    if kind == "halo":
        t.W = 516
        t.cbs = [(0, 258, 0, 0), (258, 258, 0, 258)]
        t.groups = [(0, 4), (4, 128), (132, 128), (260, 128), (388, 128)]
        t.segs = [(0, 516, 516)]
        t.row0 = 0
        t.par = 0
        t.kcbs = [(4, 256), (260, 256)]
        t.kgroups = [1, 2, 3, 4]
    elif kind == "main":
        t.W = 512
        t.cbs = [(0, 512, 0, 0)]
        t.groups = [(0, 128), (128, 128), (256, 128), (384, 128)]
        t.segs = [(0, 512, 512)]
        t.row0 = HALO + idx * 512
        t.par = (idx + 1) % 2
        t.kcbs = [(0, 512)]
        t.kgroups = [0, 1, 2, 3]
    else:
        t.W = 64
        t.cbs = [(0, 32, 0, 0), (32, 32, 1, 0)]
        t.groups = [(0, 64)]
        t.segs = [(0, 32, 16), (32, 32, 16)]
        t.row0 = 0
        t.par = 0
        t.kcbs = [(0, 64)]
        t.kgroups = [0]
    return t


def groups_of(t, c0, n):
    return [gi for gi, (g0, sz) in enumerate(t.groups) if g0 < c0 + n and c0 < g0 + sz]


def xt_keys(t, gis):
    return [(t.kp + "xT", x) for x in gis] + [(t.kp + "xTd", x) for x in gis]


class Builder:
    def __init__(self, cfg):
        self.cfg = cfg
        self.nc = bass.Bass("TRN2", target_bir_lowering=False)
        self.S = Sched()
        self.P = PsumRR(range(6))
        self.PT = PsumRR((6, 7))
        self.P8 = PsumRR(range(8))
        self.cnt = 0
        self.wplan = None
        self.wrec = []
        self.wissued = 0
        self.wcur = 0
        self.dma_keys = {}
        self.xqmap = {}
        self.pdef = []

    def declare_dram(self):
        nc = self.nc

        def din(name, shape):
            return nc.dram_tensor(name, list(shape), F32, kind="ExternalInput").ap()

        def dout(name, shape):
            return nc.dram_tensor(name, list(shape), F32, kind="ExternalOutput").ap()

        d = {}
        d["xh"] = din("xh", [XROWS, D])
        d["xs"] = din("xs", [64, D])
        d["hv"] = din("hv", [128, 1])
        d["cconv"] = din("cconv", [128, 2, 2, 8, 2])
        d["ckT"] = din("ckT", [128, 8, 1024])
        d["cv"] = din("cv", [128, 8, 1024])
        d["lnp"] = din("lnp", [4, 4, D])
        d["lnT"] = din("lnT", [128, 4, 4, 8])
        d["convw"] = din("convw", [128, 2, 8, 3])
        d["bv"] = din("bv", [2, 128, 16, 256])
        d["bvs"] = din("bvs", [2, 64, 16, 32])
        d["chi"] = din("chi", [128, 2, 16])
        d["ident"] = din("ident", [128, 128])
        d["w_up"] = din("w_up", [4, D, 4 * D])
        d["w_down"] = din("w_down", [4, 4 * D, D])
        d["w_in_a"] = din("w_in_a", [2, D, 3 * D])
        d["w_out_a"] = din("w_out_a", [2, D, D])
        d["w_k"] = din("w_k", [D, D])
        d["w_v"] = din("w_v", [D, D])
        d["w_q_b"] = din("w_q_b", [2, D, D])
        d["w_o_b"] = din("w_o_b", [2, D, D])
        d["y"] = dout("y", [TPC, D])
        d["ys"] = dout("ys", [64, D])
        d["convp"] = dout("convp", [128, 2, 2, 8, 2])
        d["kT"] = dout("kT", [128, 8, 512])
        d["v"] = dout("v", [512, D])
        d["convs"] = dout("convs", [128, 2, 2, 8, 2])
        d["ksT"] = dout("ksT", [128, 8, 64])
        d["vs"] = dout("vs", [64, D])
        self.d = d

    def alloc(self, es):
        nc = self.nc

        def sb(name, shape, dt=F32):
            return es.enter_context(nc.sbuf_tensor("sb_" + name, list(shape), dt))

        self.m_xres = sb("xres", [128, 5, D])
        self.m_xT = sb("xT", [128, 8, WA], BF16)
        self.m_uT = sb("uT", [128, 8, WA], BF16)
        self.big = sb("big", [128, 32 * WA], BF16)
        self.m_hT = self.big[:, :].rearrange("p (j w) -> p j w", w=WA)
        o = 0
        self.QT = sb("QT", [128, 8, WA], BF16)
        self.BV = self.big[:, o:o + 16 * 256 * 2].bitcast(F32).rearrange("p (h i) -> p h i", i=256)
        o += 16 * 256 * 2
        self.ET = self.big[:, o:o + 3 * 640].rearrange("p (q i) -> p q i", i=640)
        o += 3 * 640
        self.T1 = self.big[:, o:o + 3 * 256 * 2].bitcast(F32).rearrange("p (q i) -> p q i", i=256)
        o += 3 * 256 * 2
        self.On = self.big[:, o:o + 1024 * 2].bitcast(F32)
        self.Onb = self.big[:, o:o + 1024]
        o += 1024 * 2
        assert o <= 32 * WA
        self.BVs = sb("BVs", [64, 16, 32])
        self.rd2 = sb("rd2", [128, 2])
        self.lnT = sb("lnT", [128, 4, 4, 8])
        self.zt = sb("zt", [128, 2, 524])
        self.csb = sb("csb", [128, 2, 512])
        self.ysb = sb("ysb", [128, 2, 512])
        self.kst = self.csb
        self.vst = self.ysb
        self.sq = sb("sq", [128, 2, 512], BF16)
        self.KT = sb("KT", [128, 8, 1024], BF16)
        self.Vb = sb("Vb", [128, 8, 16, 66], BF16)
        self.KTs = sb("KTs", [128, 8, 64], BF16)
        self.Vs = sb("Vs", [64, 16, 66], BF16)
        self.lnp = sb("lnp", [128, 2, D])
        self.xnb = sb("xnb", [128, 2, D], BF16)
        self.identb = sb("identb", [128, 128], BF16)
        self.st5 = sb("st5", [128, 6, 12])
        self.mv5 = sb("mv5", [128, 6, 2])
        self.sd5 = sb("sd5", [128, 6, 1])
        self.rs5 = sb("rs5", [128, 6, 1])
        self.nm5 = sb("nm5", [128, 6, 1])
        self.s_xres = sb("s_xres", [128, 1, D])
        self.s_xT = sb("s_xT", [128, 8, 64], BF16)
        self.s_uT = sb("s_uT", [128, 8, 64], BF16)
        self.s_hT = sb("s_hT", [128, 32, 64], BF16)
        self.zpre_p = sb("zpre_p", [128, 2, 2, 8, 2])
        self.zpre_s = sb("zpre_s", [128, 2, 2, 8, 2])
        self.convw = sb("convw", [128, 2, 8, 3])
        self.chi = sb("chi", [128, 2, 16])
        self.ident = sb("ident", [128, 128])
        self.hvt = sb("hvt", [128, 1])
        self.st = sb("st", [128, 2, 12])
        self.mv = sb("mv", [128, 2, 2])
        self.sd = sb("sd", [128, 2, 1])
        self.rs = sb("rs", [128, 2, 1])
        self.rden = sb("rden", [128, 16])
        self.fz = sb("fz", [128, 2])
        self.epst = sb("epst", [128, 1])
        self.ones16 = sb("ones16", [128, 16])
        self.wring = sb("wring", [128, NS, SLOT], BF16)
        self.ps = es.enter_context(nc.psum_tensor("ps", [128, 8, 512], F32))

    def nxt(self):
        self.cnt += 1
        return self.cnt % 2

    def dkey(self, key):
        self.dma_keys[key] = True
        return key

    def mm(self, out, lhsT, rhs, start, stop, reads, writes):
        self.S.add("pe", lambda e, o=out, l=lhsT, r=rhs, s=start, t=stop:
                   e.matmul(o, l, r, start=s, stop=t), reads, writes)

    def fence(self):
        self.S.add("dve", lambda e, a=self.fz[:, 0:2]: e.memset(a, 0.0), reads=[], writes=["ALIAS", "fz"])

    def wsrc(self, key):
        d = self.d
        kind = key[0]
        if kind == "win":
            _, l, jp = key
            v = d["w_in_a"][l].rearrange("(kc p) (w f) -> p kc w f", p=128, w=3)
            return [(v[:, :, wi, jp * 256:(jp + 1) * 256], (8, 256), wi * 8 * 256) for wi in range(3)], (8 * 3 * 256)
        if kind == "wup":
            _, l, jp = key
            v = d["w_up"][l].rearrange("(kc p) f -> p kc f", p=128)
            return [(v[:, :, jp * 512:(jp + 1) * 512], (8, 512), 0)], 8 * 512
        if kind == "wdown":
            _, l, half, kh = key
            v = d["w_down"][l].rearrange("(kc p) f -> p kc f", p=128)
            return [(v[:, kh * 16:(kh + 1) * 16, half * 512:(half + 1) * 512], (16, 512), 0)], 16 * 512
        if kind in ("wout", "wo", "wv"):
            _, l, half, kh = key
            src = {"wout": d["w_out_a"], "wo": d["w_o_b"]}.get(kind)
            m = d["w_v"] if kind == "wv" else src[l]
            v = m.rearrange("(kc p) f -> p kc f", p=128)
            return [(v[:, :, half * 512:(half + 1) * 512], (8, 512), 0)], 8 * 512
        if kind in ("wq", "wk"):
            _, l = key
            m = d["w_k"] if kind == "wk" else d["w_q_b"][l]
            v = m.rearrange("(kc p) f -> p kc f", p=128)
            return [(v[:, :, 0:512], (8, 512), 0), (v[:, :, 512:1024], (8, 512), 8 * 512)], 8 * 1024
        raise KeyError(key)

    def wissue(self, i):
        key = self.wplan[i]
        s = i % NS
        parts, tot = self.wsrc(key)
        if key in self.wscr:
            scr = self.wscr[key]
            self.S.add("sp", lambda e, o=self.wring[:, s, 0:tot], a=scr: [e.dma_start(out=o, in_=a)],
                       reads=[("scr", key)], writes=[("w", s)], dma=self.dkey(("w", s)))
            return
        outs = []
        for src, (a, b), off in parts:
            dst = self.wring[:, s, off:off + a * b].rearrange("p (a b) -> p a b", b=b)
            outs.append((dst, src))

        def fn(e, outs=outs):
            return [e.dma_start(out=o, in_=i_) for o, i_ in outs]
        self.S.add("pool", fn, reads=[], writes=[("w", s)], dma=self.dkey(("wc", s)))
        if self.cfg.get("scratch", True) and self.wuses.get(key, 0) > 1:
            scr = self.nc.dram_tensor("scr%d" % len(self.wscr), [128, tot], BF16, kind="Internal").ap()
            self.wscr[key] = scr
            self.S.add("sp", lambda e, o=scr, a=self.wring[:, s, 0:tot]: [e.dma_start(out=o, in_=a)],
                       reads=[("w", s)], writes=[("scr", key)], dma=self.dkey(("wb", s)))

    def flush_pool(self):
        ops, self.pdef = self.pdef, []
        for fn, rd, wr in ops:
            self.S.add("pool", fn, reads=rd, writes=wr)

    def wget(self, key, nheld=1):
        i = self.wcur
        self.wcur += 1
        if self.wplan is None:
            self.wrec.append(key)
            self.flush_pool()
            return 0
        assert self.wplan[i] == key, (self.wplan[i], key)
        while self.wissued < min(len(self.wplan), i + NS - (nheld - 1)):
            self.wissue(self.wissued)
            self.wissued += 1
        self.flush_pool()
        return i % NS

    def wview(self, s, kind):
        r = self.wring[:, s, :]
        if kind == "win":
            return r[:, 0:8 * 3 * 256].rearrange("p (w k f) -> p k w f", w=3, k=8)
        if kind == "k512":
            return r[:, 0:8 * 512].rearrange("p (k f) -> p k f", f=512)
        if kind == "k16":
            return r[:, 0:16 * 512].rearrange("p (k f) -> p k f", f=512)
        if kind == "full":
            return r[:, 0:8 * 1024].rearrange("p (h k f) -> p k h f", h=2, k=8)
        raise KeyError(kind)

    def to_feat(self, t, gi, src, dst, dkeyname, src_reads, extra_reads=(), gb=None):
        c0, sz = t.groups[gi]
        S = self.S
        for c4 in range(2):
            bank = self.PT.next()
            for i in range(4):
                c = c4 * 4 + i
                S.add("pe", lambda e, o=self.ps[:, bank, i * 128:i * 128 + sz],
                      a=src[:, c * 128:(c + 1) * 128], idn=self.ident[:sz, :sz]:
                      e.transpose(o, a, idn),
                      reads=list(src_reads) + ["ident"], writes=[("ps", bank)])
            if gb is not None:
                l_, li_ = gb
                for i in range(4):
                    c = c4 * 4 + i
                    S.add("act", lambda e, o=dst[:, c, c0:c0 + sz], a=self.ps[:, bank, i * 128:i * 128 + sz],
                          g_=self.lnT[:, l_, li_ * 2, c:c + 1], b_=self.lnT[:, l_, li_ * 2 + 1, c:c + 1]:
                          e.activation(out=o, in_=a, func=AF.Identity, bias=b_, scale=g_),
                          reads=[("ps", bank), "lnT"] + list(extra_reads), writes=[(t.kp + dkeyname, gi)])
                continue
            pv = self.ps[:, bank, :].rearrange("p (a b) -> p a b", b=128)[:, :, 0:sz]
            S.add("act", lambda e, o=dst[:, c4 * 4:(c4 + 1) * 4, c0:c0 + sz], a=pv:
                  e.activation(out=o, in_=a, func=AF.Copy),
                  reads=[("ps", bank)] + list(extra_reads), writes=[(t.kp + dkeyname, gi)])

    def load_x(self, t):
        self.S.tag = "%s%d.loadx" % (t.kind, t.idx)
        S = self.S
        src = self.d["xs"] if t.kind == "sample" else self.d["xh"]
        for gi, (c0, sz) in enumerate(t.groups):
            S.add("sp", lambda e, o=t.xres[:sz, gi, :], a=src[t.row0 + c0:t.row0 + c0 + sz, :]:
                  [e.dma_start(out=o, in_=a)],
                  reads=[], writes=[(t.kp + "xres", gi)], dma=self.dkey((t.kp + "x", gi)))
            self.to_feat(t, gi, t.xres[:sz, gi, :], t.xT, "xT", [(t.kp + "xres", gi)])

    def load_lnp(self, l, which):
        self.flush_pool()
        def fn(e, l=l, which=which):
            return [e.dma_start(out=self.lnp[:, v, :], in_=self.d["lnp"][l, which * 2 + v].partition_broadcast(128))
                    for v in range(2)]
        self.S.add("sp", fn, reads=[], writes=["lnp"], dma=self.dkey("lnp"))

    def ln(self, t, gi, lnidx):
        S = self.S
        c0, sz = t.groups[gi]
        q = self.nxt()
        xr = t.xres[:sz, gi, :]
        st, mv, sd, rs = self.st[:sz, q, :], self.mv[:sz, q, :], self.sd[:sz, q, :], self.rs[:sz, q, :]
        X = (t.kp + "xres", gi)
        S.add("dve", lambda e: e.bn_stats(st[:, 0:6], xr[:, 0:512]), reads=[X], writes=[("st", q)])
        S.add("dve", lambda e: e.bn_stats(st[:, 6:12], xr[:, 512:1024]), reads=[X, ("st", q)], writes=[("st", q)])
        S.add("dve", lambda e: e.bn_aggr(mv, st), reads=[("st", q)], writes=[("mv", q)])
        S.add("act", lambda e: e.activation(out=sd, in_=mv[:, 1:2], func=AF.Sqrt, bias=self.epst[:sz, :], scale=1.0),
              reads=[("mv", q), "epst"], writes=[("sd", q)])
        S.add("dve", lambda e: e.reciprocal(rs, sd), reads=[("sd", q)], writes=[("rs", q)])
        S.add("dve", lambda e: e.tensor_scalar(out=xr, in0=xr, scalar1=mv[:, 0:1], scalar2=rs,
                                               op0=ALU.subtract, op1=ALU.mult),
              reads=[X, ("mv", q), ("rs", q)], writes=[X])

    def ln_stats(self, t, gi, lnidx):
        S = self.S
        c0, sz = t.groups[gi]
        q = gi + t.lnoff
        xr = t.xres[:sz, gi, :]
        st, mv, sd, rs = self.st5[:sz, q, :], self.mv5[:sz, q, :], self.sd5[:sz, q, :], self.rs5[:sz, q, :]
        self.xq = getattr(self, "xq", 0) + 1
        xi = self.xq % 2
        self.xqmap[(t.kp, gi)] = xi
        xb = self.xnb[:sz, xi, :]
        X = (t.kp + "xres", gi)
        S.add("dve", lambda e: e.bn_stats(st[:, 0:6], xr[:, 0:512]), reads=[X], writes=[("st5", q)])
        S.add("dve", lambda e: e.bn_stats(st[:, 6:12], xr[:, 512:1024]), reads=[X, ("st5", q)], writes=[("st5", q)])
        S.add("dve", lambda e: e.bn_aggr(mv, st), reads=[("st5", q)], writes=[("mv5", q)])
        S.add("act", lambda e: e.activation(out=sd, in_=mv[:, 1:2], func=AF.Sqrt, bias=self.epst[:sz, :], scale=1.0),
              reads=[("mv5", q), "epst"], writes=[("sd5", q)])
        S.add("dve", lambda e: e.reciprocal(rs, sd), reads=[("sd5", q)], writes=[("rs5", q)])
        nm = self.nm5[:sz, q, :]
        S.add("dve", lambda e: e.tensor_scalar(out=nm, in0=mv[:, 0:1], scalar1=-1.0, scalar2=rs, op0=ALU.mult, op1=ALU.mult),
              reads=[("mv5", q), ("rs5", q)], writes=[("nm5", q)])
        S.add("act", lambda e: e.activation(out=xb, in_=xr, func=AF.Identity, bias=nm, scale=rs),
              reads=[X, ("nm5", q), ("rs5", q)], writes=[("xnb", xi)])
        self.pdef.append((lambda e: e.tensor_scalar(out=xr, in0=xr, scalar1=rs, scalar2=nm, op0=ALU.mult, op1=ALU.add),
                          [X, ("nm5", q), ("rs5", q)], [X]))
        self.pdef.append((lambda e: e.tensor_tensor(out=xr, in0=xr, in1=self.lnp[:sz, 0, :], op=ALU.mult),
                          [X, "lnp"], [X]))
        self.pdef.append((lambda e: e.tensor_tensor(out=xr, in0=xr, in1=self.lnp[:sz, 1, :], op=ALU.add),
                          [X, "lnp"], [X]))

    def ln_out(self, t, gi, lnidx):
        S = self.S
        c0, sz = t.groups[gi]
        l_ = self.cur_l
        xi = self.xqmap[(t.kp, gi)]
        xb = self.xnb[:sz, xi, :]
        for c4 in range(2):
            bank = self.PT.next()
            psb = self.ps[:, bank, :].bitcast(BF16)
            for i in range(4):
                c = c4 * 4 + i
                S.add("pe", lambda e, o=psb[:, i * 128:i * 128 + sz], a=xb[:, c * 128:(c + 1) * 128],
                      idn=self.identb[:sz, :sz]: e.transpose(o, a, idn),
                      reads=[("xnb", xi), "identb"], writes=[("ps", bank)])
            for i in range(4):
                c = c4 * 4 + i
                o = t.xT[:, c, c0:c0 + sz]
                a = psb[:, i * 128:i * 128 + sz]
                g_ = self.lnT[:, l_, lnidx * 2, c:c + 1]
                b_ = self.lnT[:, l_, lnidx * 2 + 1, c:c + 1]
                if c4 == 0:
                    S.add("act", lambda e, o=o, a=a, g_=g_, b_=b_: e.activation(out=o, in_=a, func=AF.Identity, bias=b_, scale=g_),
                          reads=[("ps", bank), "lnT"], writes=[(t.kp + "xT", gi)])
                else:
                    S.add("dve", lambda e, o=o, a=a, g_=g_, b_=b_: e.tensor_scalar(out=o, in0=a, scalar1=g_, scalar2=b_,
                                                                                 op0=ALU.mult, op1=ALU.add),
                          reads=[("ps", bank), "lnT"], writes=[(t.kp + "xTd", gi)])

    def ln_gb(self, t, gi, lnidx):
        S = self.S
        c0, sz = t.groups[gi]
        xr = t.xres[:sz, gi, :]
        X = (t.kp + "xres", gi)
        eng = self.cfg.get("gb_eng", "pool")
        S.add(eng, lambda e: e.tensor_tensor(out=xr, in0=xr, in1=self.lnp[:sz, 0, :], op=ALU.mult),
              reads=[X, "lnp"], writes=[X])
        S.add(eng, lambda e: e.tensor_tensor(out=xr, in0=xr, in1=self.lnp[:sz, 1, :], op=ALU.add),
              reads=[X, "lnp"], writes=[X])

    def proj_tok_resid(self, tiles, bufname, nk, wkeyf, lnidx, alias=False):
        S = self.S
        self.flush_pool()
        S.tag = S.tag.rsplit(".", 1)[0] + (".down" if nk == 32 else ".oproj")
        kper = min(nk, 16)
        nkh = nk // kper
        ar = ["ALIAS"] if alias else []
        vk = "k16" if kper == 16 else "k512"
        tg = [(t, gi) for t in tiles for gi in range(len(t.groups))]

        def resid(t, gi, half, b):
            c0, sz = t.groups[gi]
            xr = t.xres[:sz, gi, half * 512:(half + 1) * 512]
            S.add("dve", lambda e, xr=xr, p=self.ps[:sz, b, :]:
                  e.scalar_tensor_tensor(out=xr, in0=xr, scalar=ALPHA, in1=p, op0=ALU.mult, op1=ALU.add),
                  reads=[(t.kp + "xres", gi), ("ps", b)], writes=[(t.kp + "xres", gi)])

        def mms(t, gi, b, s, wv, kh):
            c0, sz = t.groups[gi]
            actT = getattr(t, bufname)
            ard = [(t.kp + bufname, x) for x in range(len(t.cbs))] + ar
            for kk in range(kper):
                k = kh * kper + kk
                self.mm(self.ps[:sz, b, :], actT[:, k, c0:c0 + sz], wv[:, kk, :], k == 0, k == nk - 1,
                        reads=[("w", s)] + ard, writes=[("ps", b)])
        pending = None
        if nkh == 1:
            ss = [self.wget(wkeyf(0, 0)), self.wget(wkeyf(1, 0), nheld=2)]
            wvs = [self.wview(s, vk) for s in ss]
            for (t, gi) in tg:
                bs = []
                for half in range(2):
                    b = self.P.next()
                    bs.append(b)
                    mms(t, gi, b, ss[half], wvs[half], 0)
                for half in range(2):
                    resid(t, gi, half, bs[half])
                self.ln_stats(t, gi, lnidx)
                if pending is not None:
                    self.ln_out(pending[0], pending[1], lnidx)
                pending = (t, gi)
            self.ln_out(pending[0], pending[1], lnidx)
            return
        banks = {}
        assert len(tg) <= 6
        for kh in range(nkh):
            s = self.wget(wkeyf(0, kh))
            wv = self.wview(s, vk)
            for i, (t, gi) in enumerate(tg):
                if kh == 0:
                    banks[i] = self.P.next()
                mms(t, gi, banks[i], s, wv, kh)
        for i, (t, gi) in enumerate(tg):
            resid(t, gi, 0, banks[i])
        ss = [self.wget(wkeyf(1, kh), nheld=kh + 1) for kh in range(nkh)]
        wvs = [self.wview(s, vk) for s in ss]
        for (t, gi) in tg:
            b = self.P.next()
            for kh in range(nkh):
                mms(t, gi, b, ss[kh], wvs[kh], kh)
            resid(t, gi, 1, b)
            self.ln_stats(t, gi, lnidx)
            if pending is not None:
                self.ln_out(pending[0], pending[1], lnidx)
            pending = (t, gi)
        self.ln_out(pending[0], pending[1], lnidx)

    def mixer_A(self, tiles, l):
        self.S.tag = "%s%d.L%d.mixer" % (tiles[0].kind, tiles[0].idx, l)
        S = self.S
        for jp in range(4):
            s = self.wget(("win", l, jp))
            wv = self.wview(s, "win")
            for jj in range(2):
              for t in tiles:
                zpre = self.zpre_s if t.kind == "sample" else self.zpre_p
                zk = "zpre_s" if t.kind == "sample" else "zpre_p"
                L = t.segs[0][1]
                j = jp * 2 + jj
                self.zcnt = getattr(self, 'zcnt', 0) + 1
                zq = self.zcnt % 2
                Z = ("zt", zq)
                for si, (sc0, L_, Lr) in enumerate(t.segs):
                    zoff = si * (L + 2)
                    S.add("dve", lambda e, o=self.zt[:, zq, zoff:zoff + 2], a=zpre[:, l, si, j, :]:
                          e.tensor_copy(out=o, in_=a), reads=[(zk, l)], writes=[Z])
                for cbi, (c0, n, si, off) in enumerate(t.cbs):
                    bk = [self.P8.next() for _ in range(3)]
                    xr = xt_keys(t, groups_of(t, c0, n))
                    for wi in range(3):
                        for k in range(8):
                            self.mm(self.ps[:, bk[wi], 0:n], wv[:, k, wi, jj * 128:(jj + 1) * 128],
                                    t.xT[:, k, c0:c0 + n], k == 0, k == 7,
                                    reads=[("w", s)] + xr, writes=[("ps", bk[wi])])
                    q = self.nxt()
                    zo = si * (L + 2) + 2 + off
                    pb_, pc_, ph_ = (self.ps[:, bk[i], 0:n] for i in range(3))
                    cs, ys = self.csb[:, q, 0:n], self.ysb[:, q, 0:n]
                    zt = self.zt
                    cw = self.convw
                    S.add("act", lambda e, cs=cs, pc_=pc_: e.activation(out=cs, in_=pc_, func=AF.Copy),
                          reads=[("ps", bk[1])], writes=[("csb", q)])
                    S.add("dve", lambda e, o=zt[:, zq, zo:zo + n], cs=cs, ph_=ph_:
                          e.tensor_tensor(out=o, in0=cs, in1=ph_, op=ALU.mult),
                          reads=[("csb", q), ("ps", bk[2])], writes=[Z])
                    S.add("act", lambda e, ys=ys, a=zt[:, zq, zo:zo + n], w=cw[:, l, j, 2:3]:
                          e.activation(out=ys, in_=a, func=AF.Identity, scale=w),
                          reads=[Z, "convw"], writes=[("ysb", q)])
                    S.add("dve", lambda e, ys=ys, a=zt[:, zq, zo - 1:zo - 1 + n], w=cw[:, l, j, 1:2]:
                          e.scalar_tensor_tensor(out=ys, in0=a, scalar=w, in1=ys, op0=ALU.mult, op1=ALU.add),
                          reads=[Z, ("ysb", q), "convw"], writes=[("ysb", q)])
                    S.add("dve", lambda e, ys=ys, a=zt[:, zq, zo - 2:zo - 2 + n], w=cw[:, l, j, 0:1]:
                          e.scalar_tensor_tensor(out=ys, in0=a, scalar=w, in1=ys, op0=ALU.mult, op1=ALU.add),
                          reads=[Z, ("ysb", q), "convw"], writes=[("ysb", q)])
                    S.add("dve", lambda e, o=t.uT[:, j, c0:c0 + n], ys=ys, pb_=pb_:
                          e.tensor_tensor(out=o, in0=ys, in1=pb_, op=ALU.mult),
                          reads=[("ysb", q), ("ps", bk[0])], writes=[(t.kp + "uT", cbi)])
                for si, (sc0, L_, Lr) in enumerate(t.segs):
                    zoff = si * (L + 2)
                    src = self.zt[:, zq, zoff + Lr:zoff + Lr + 2]
                    dst = zpre[:, l, si, j, :]
                    if t.kind == "halo":
                        S.add("dve", lambda e, o=dst, a=src: e.tensor_scalar(
                            out=o, in0=a, scalar1=self.hvt[:, 0:1], scalar2=None, op0=ALU.mult),
                            reads=[Z, "hvt"], writes=[(zk, l)])
                    else:
                        S.add("dve", lambda e, o=dst, a=src: e.tensor_copy(out=o, in_=a),
                              reads=[Z], writes=[(zk, l)])
        self.proj_tok_resid(tiles, "uT", 8, lambda half, kh: ("wout", l, half, kh), 0)

    def mlp(self, tiles, l):
        self.S.tag = "%s%d.L%d.mlp_up" % (tiles[0].kind, tiles[0].idx, l)
        S = self.S
        self.fence()
        for jp in range(8):
            s = self.wget(("wup", l, jp))
            wv = self.wview(s, "k512")
            for jj in range(4):
                j = jp * 4 + jj
                for t in tiles:
                    for cbi, (c0, n, si, off) in enumerate(t.cbs):
                        b = self.P8.next()
                        xr = xt_keys(t, groups_of(t, c0, n))
                        for k in range(8):
                            self.mm(self.ps[:, b, 0:n], wv[:, k, jj * 128:(jj + 1) * 128], t.xT[:, k, c0:c0 + n],
                                    k == 0, k == 7, reads=[("w", s)] + xr, writes=[("ps", b)])
                        q = self.nxt()
                        sq = self.sq[:, q, 0:n]
                        p = self.ps[:, b, 0:n]
                        S.add("act", lambda e, sq=sq, p=p: e.activation(out=sq, in_=p, func=AF.Square),
                              reads=[("ps", b)], writes=[("sq", q)])
                        S.add("dve", lambda e, o=t.hT[:, j, c0:c0 + n], sq=sq, p=p:
                              e.scalar_tensor_tensor(out=o, in0=p, scalar=0.0, in1=sq, op0=ALU.is_gt, op1=ALU.mult),
                              reads=[("ps", b), ("sq", q), "ALIAS"], writes=[(t.kp + "hT", cbi)])
        self.proj_tok_resid(tiles, "hT", 32, lambda half, kh: ("wdown", l, half, kh), 1, alias=True)

    def proj_feat(self, t, wkey, dst, dcol_of, cbs, scale, dkeyname, stage=None, alias=False):
        self.proj_feat_multi(wkey, [(t, dst, dcol_of, cbs, dkeyname, stage)], scale, alias)

    def proj_feat_multi(self, wkey, specs, scale, alias=False):
        S = self.S
        s = self.wget(wkey)
        wv = self.wview(s, "full")
        ar = ["ALIAS"] if alias else []
        for c in range(8):
            for (t, dst, dcol_of, cbs, dkeyname, stage) in specs:
                for (c0, n) in cbs:
                    b = self.P.next()
                    xr = xt_keys(t, groups_of(t, c0, n))
                    for k in range(8):
                        self.mm(self.ps[:, b, 0:n], wv[:, k, c // 4, (c % 4) * 128:(c % 4 + 1) * 128],
                                t.xT[:, k, c0:c0 + n], k == 0, k == 7, reads=[("w", s)] + xr, writes=[("ps", b)])
                    dc = dcol_of(c0)
                    p = self.ps[:, b, 0:n]
                    S.add("act", lambda e, o=dst[:, c, dc:dc + n], p=p: e.activation(out=o, in_=p, func=AF.Copy, scale=scale),
                          reads=[("ps", b)] + ar, writes=[(dkeyname, c)])
                    if stage is not None and self.cfg.get("kstage", True):
                        q = 0
                        S.add("dve", lambda e, o=self.kst[:, q, 0:n], p=p: e.tensor_copy(out=o, in_=p),
                              reads=[("ps", b), (dkeyname, c)], writes=[("csb", q)])
                        oc = dc - stage[1]
                        S.add("sp", lambda e, o=stage[0][:, c, oc:oc + n], a=self.kst[:, q, 0:n]: [e.dma_start(out=o, in_=a)],
                              reads=[("csb", q)], writes=[], dma=self.dkey(("csb", q)))

    def proj_kv(self, tiles, out_kv):
        self.S.tag = "%s%d.kv" % (tiles[0].kind, tiles[0].idx)
        S = self.S
        d = self.d
        if not self.cfg.get("kvout", True):
            out_kv = False
        specs = []
        for t in tiles:
            if t.kind == "sample":
                specs.append((t, self.KTs, lambda c0: c0, t.kcbs, "KTs", (d["ksT"], 0)))
            else:
                base = t.par * 512 - t.kcbs[0][0]
                specs.append((t, self.KT, lambda c0, base=base: base + c0, t.kcbs, "KTc",
                              (d["kT"], t.par * 512) if out_kv else None))
        self.proj_feat_multi(("wk", 0), specs, 1.0)
        for half in range(2 if self.cfg.get("vproj", True) else 0):
            s = self.wget(("wv", 0, half, 0))
            wv = self.wview(s, "k512")
            for t, n_, gi in [(t, n_, gi) for t in tiles for n_, gi in enumerate(t.kgroups)]:
                c0, sz = t.groups[gi]
                b = self.P.next()
                for k in range(8):
                    self.mm(self.ps[:sz, b, :], t.xT[:, k, c0:c0 + sz], wv[:, k, :], k == 0, k == 7,
                            reads=[("w", s), (t.kp + "xT", gi), (t.kp + "xTd", gi)], writes=[("ps", b)])
                pv = self.ps[:sz, b, :].rearrange("p (h e) -> p h e", e=64)
                if t.kind == "sample":
                    dst = self.Vs[:sz, half * 8:(half + 1) * 8, 0:64]
                    vk = ("Vs",)
                else:
                    slot = t.par * 4 + n_
                    dst = self.Vb[:sz, slot, half * 8:(half + 1) * 8, 0:64]
                    vk = ("V", slot)
                if t.kind == "halo":
                    S.add("act", lambda e, o=dst, a=pv, sz=sz: e.activation(out=o, in_=a, func=AF.Identity, scale=self.hvt[:sz, 0:1]),
                          reads=[("ps", b), "hvt"], writes=[vk])
                else:
                    S.add("act", lambda e, o=dst, a=pv: e.activation(out=o, in_=a, func=AF.Copy),
                          reads=[("ps", b)], writes=[vk])
                if half == 0:
                    if t.kind == "sample":
                        S.add("dve", lambda e, o=self.Vs[:sz, :, 64:65]: e.memset(o, 1.0), reads=[], writes=[("Vs1",)])
                    elif t.kind == "halo":
                        S.add("dve", lambda e, o=self.Vb[:sz, slot, :, 64:65], sz=sz:
                              e.tensor_scalar(out=o, in0=self.ones16[:sz, :].unsqueeze(2), scalar1=self.hvt[:sz, 0:1],
                                              scalar2=None, op0=ALU.mult),
                              reads=["hvt", "ones16"], writes=[("V1", slot)])
                    else:
                        S.add("dve", lambda e, o=self.Vb[:sz, slot, :, 64:65]: e.memset(o, 1.0), reads=[], writes=[("V1", slot)])
                if (out_kv or t.kind == "sample") and self.cfg.get("vstage", True):
                    q = 0
                    S.add("dve", lambda e, o=self.vst[:sz, q, :], p=self.ps[:sz, b, :]: e.tensor_copy(out=o, in_=p),
                          reads=[("ps", b), vk], writes=[("ysb", q)])
                    od = d["vs"] if t.kind == "sample" else d["v"]
                    r0 = c0
                    S.add("sp", lambda e, o=od[r0:r0 + sz, half * 512:(half + 1) * 512], a=self.vst[:sz, q, :]:
                          [e.dma_start(out=o, in_=a)], reads=[("ysb", q)], writes=[], dma=self.dkey(("ysb", q)))

    def slot_of(self, t, kg):
        return t.par * 4 + kg if kg >= 0 else (1 - t.par) * 4 + kg + 4

    def att_finish(self, t, gi, sz):
        S = self.S
        for bi, (h0, nh) in enumerate(((0, 7), (7, 7), (14, 2))):
            ov = self.ps[:sz, 4 + bi, 0:nh * 65].rearrange("p (h e) -> p h e", e=65)
            S.add("dve", lambda e, o=self.rden[:sz, h0:h0 + nh], a=ov[:, :, 64]: e.reciprocal(o, a),
                  reads=[("ps", 4 + bi)], writes=[("rden", bi)])
            S.add("dve", lambda e, o=self.On[:sz, h0 * 64:(h0 + nh) * 64].rearrange("p (h e) -> p h e", e=64),
                  a=ov[:, :, 0:64], r=self.rden[:sz, h0:h0 + nh].unsqueeze(2).broadcast_to([sz, nh, 64]):
                  e.tensor_tensor(out=o, in0=a, in1=r, op=ALU.mult),
                  reads=[("ps", 4 + bi), ("rden", bi), "ALIAS"], writes=[("On", c) for c in range(8)])
        c0 = t.groups[gi][0]
        for c4 in range(2):
            bank = 7
            for i in range(4):
                c = c4 * 4 + i
                S.add("pe", lambda e, o=self.ps[:, bank, i * 128:i * 128 + sz],
                      a=self.On[:sz, c * 128:(c + 1) * 128], idn=self.ident[:sz, :sz]: e.transpose(o, a, idn),
                      reads=[("On", c) for c in range(8)] + ["ident", "ALIAS"], writes=[("ps", bank)])
            pv = self.ps[:, bank, :].rearrange("p (a b) -> p a b", b=128)[:, :, 0:sz]
            S.add("act", lambda e, o=t.uT[:, c4 * 4:(c4 + 1) * 4, c0:c0 + sz], a=pv: e.activation(out=o, in_=a, func=AF.Copy),
                  reads=[("ps", bank)], writes=[(t.kp + "uT", 0), (t.kp + "uT", 1)])

    def attention(self, t, l):
        self.S.tag = "%s%d.L%d.att" % (t.kind, t.idx, l)
        S = self.S
        lb = l - 2
        self.fence()
        S.add("sp", lambda e: [e.dma_start(out=self.BV[:, 0:8, :], in_=self.d["bv"][lb, :, 0:8, :]),
                               e.dma_start(out=self.BV[:, 8:16, :], in_=self.d["bv"][lb, :, 8:16, :])],
              reads=["ALIAS"], writes=["BV"], dma=self.dkey("BV"))
        self.proj_feat(t, ("wq", lb), self.QT, lambda c0: c0, [(c0, n) for (c0, n, _, _) in t.cbs], 0.125, "QT", alias=True)
        steps = [(m, h) for m in range(4) for h in range(16)]
        LA = 2

        def qk(i):
            m, h = steps[i]
            q = i % 3
            c, pb = h // 2, (h % 2) * 64
            A, B = 2 * q, 2 * q + 1
            for dd in range(5):
                slot = self.slot_of(t, m - dd)
                o = self.ps[:, A, dd * 128:(dd + 1) * 128] if dd < 2 else self.ps[:, B, (dd - 2) * 128:(dd - 1) * 128]
                bk = A if dd < 2 else B
                self.mm(o, self.KT[pb:pb + 64, c, slot * 128:(slot + 1) * 128],
                        self.QT[pb:pb + 64, c, m * 128:(m + 1) * 128], True, True,
                        reads=[("KTc", c), ("QT", c), "ALIAS"], writes=[("ps", bk)])
            S.add("dve", lambda e, o=self.T1[:, q, :], a=self.ps[:, A, 0:256], bvv=self.BV[:, h, :]:
                  e.tensor_tensor(out=o, in0=a, in1=bvv, op=ALU.add),
                  reads=[("ps", A), "BV", "ALIAS"], writes=[("T1", q)])
            S.add("act", lambda e, o=self.ET[:, q, 0:256], a=self.T1[:, q, :]: e.activation(out=o, in_=a, func=AF.Exp),
                  reads=[("T1", q), "ALIAS"], writes=[("ET", q)])
            S.add("act", lambda e, o=self.ET[:, q, 256:640], a=self.ps[:, B, 0:384], bb=self.chi[:, lb, h:h + 1]:
                  e.activation(out=o, in_=a, func=AF.Exp, bias=bb, scale=1.0),
                  reads=[("ps", B), "chi", "ALIAS"], writes=[("ET2", q)])

        def pv(i):
            m, h = steps[i]
            q = i % 3
            r = i % 2
            ob = 6 + r
            for dd in range(4):
                slot = self.slot_of(t, m - dd)
                self.mm(self.ps[:, ob, 0:65], self.ET[:, q, dd * 128:(dd + 1) * 128],
                        self.Vb[:, slot, h, 0:65], dd == 0, False,
                        reads=[("ET", q), ("ET2", q), ("V", slot), ("V1", slot), "ALIAS"], writes=[("ps", ob)])
            slot = self.slot_of(t, m - 4)
            rd = [("ET", q), ("ET2", q), ("V", slot), ("V1", slot), "ALIAS"]
            self.mm(self.ps[0:64, ob, 0:65], self.ET[0:64, q, 512:576], self.Vb[0:64, slot, h, 0:65], False, False,
                    reads=rd, writes=[("ps", ob)])
            self.mm(self.ps[:, ob, 0:65], self.ET[64:128, q, 512:640], self.Vb[64:128, slot, h, 0:65], False, True,
                    reads=rd, writes=[("ps", ob)])
            S.add("dve", lambda e, o=self.rd2[:, r:r + 1], a=self.ps[:, ob, 64:65]: e.reciprocal(o, a),
                  reads=[("ps", ob)], writes=[("rd2", r)])
            S.add("dve", lambda e, o=self.Onb[:, h * 64:(h + 1) * 64], a=self.ps[:, ob, 0:64], rr=self.rd2[:, r:r + 1]:
                  e.tensor_scalar(out=o, in0=a, scalar1=rr, scalar2=None, op0=ALU.mult),
                  reads=[("ps", ob), ("rd2", r), "ALIAS"], writes=[("On", h // 2)])

        def fin(m):
            for c4 in range(2):
                bank = self.PT.next()
                psb = self.ps[:, bank, :].bitcast(BF16)
                for i in range(4):
                    c = c4 * 4 + i
                    S.add("pe", lambda e, o=psb[:, i * 128:(i + 1) * 128],
                          a=self.Onb[:, c * 128:(c + 1) * 128], idn=self.identb[:, :]: e.transpose(o, a, idn),
                          reads=[("On", c), "identb", "ALIAS"], writes=[("ps", bank)])
                pvw = psb[:, 0:512].rearrange("p (a b) -> p a b", b=128)
                S.add("act", lambda e, o=t.uT[:, c4 * 4:(c4 + 1) * 4, m * 128:(m + 1) * 128], a=pvw:
                      e.activation(out=o, in_=a, func=AF.Copy),
                      reads=[("ps", bank)], writes=[(t.kp + "uT", 0), (t.kp + "uT", 1)])
        for i in range(LA):
            qk(i)
        for i in range(len(steps)):
            if i + LA < len(steps):
                qk(i + LA)
            pv(i)
            if steps[i][1] == 15:
                fin(steps[i][0])
        self.proj_tok_resid([t], "uT", 8, lambda half, kh: ("wo", lb, half, kh), 0)

    def attention_sample(self, t, l):
        self.S.tag = "%s%d.L%d.att" % (t.kind, t.idx, l)
        S = self.S
        lb = l - 2
        self.fence()
        S.add("sp", lambda e: [e.dma_start(out=self.BV[:, 0:8, :], in_=self.d["bv"][lb, :, 0:8, :]),
                               e.dma_start(out=self.BV[:, 8:16, :], in_=self.d["bv"][lb, :, 8:16, :]),
                               e.dma_start(out=self.BVs[0:64, :, :], in_=self.d["bvs"][lb])],
              reads=["ALIAS"], writes=["BV"], dma=self.dkey("BV"))
        self.proj_feat(t, ("wq", lb), self.QT, lambda c0: c0, [(0, 64)], 0.125, "QT", alias=True)
        steps = [(s, h) for s in range(2) for h in range(16)]

        def qk(s, h, q):
            c, pb = h // 2, (h % 2) * 64
            A, B = 2 * q, 2 * q + 1
            p0 = s * 32
            rq = self.QT[pb:pb + 64, c, p0:p0 + 32]
            rd = [("KTc", c), ("KTs", c), ("QT", c), "ALIAS"]
            self.mm(self.ps[p0:p0 + 16, A, 0:32], self.KTs[pb:pb + 64, c, p0:p0 + 16], rq, True, True, rd, [("ps", A)])
            sl3 = s * 4 + 3
            self.mm(self.ps[:, A, 32:64], self.KT[pb:pb + 64, c, sl3 * 128:(sl3 + 1) * 128], rq, True, True, rd, [("ps", A)])
            for kg in range(3):
                sl = s * 4 + kg
                self.mm(self.ps[:, B, kg * 32:(kg + 1) * 32], self.KT[pb:pb + 64, c, sl * 128:(sl + 1) * 128], rq,
                        True, True, rd, [("ps", B)])
            S.add("dve", lambda e, o=self.T1[p0:p0 + 16, q, 0:32], a=self.ps[p0:p0 + 16, A, 0:32], bvv=self.BVs[p0:p0 + 16, h, :]:
                  e.tensor_tensor(out=o, in0=a, in1=bvv, op=ALU.add),
                  reads=[("ps", A), "BV", "ALIAS"], writes=[("T1", q)])
            S.add("dve", lambda e, o=self.T1[:, q, 32:64], a=self.ps[:, A, 32:64], bvv=self.BV[:, h, 128:160]:
                  e.tensor_tensor(out=o, in0=a, in1=bvv, op=ALU.add),
                  reads=[("ps", A), "BV", "ALIAS", ("T1", q)], writes=[("T1", q)])
            S.add("act", lambda e, o=self.ET[p0:p0 + 16, q, 0:32], a=self.T1[p0:p0 + 16, q, 0:32]: e.activation(out=o, in_=a, func=AF.Exp),
                  reads=[("T1", q), "ALIAS"], writes=[("ET", q)])
            S.add("act", lambda e, o=self.ET[:, q, 32:64], a=self.T1[:, q, 32:64]: e.activation(out=o, in_=a, func=AF.Exp),
                  reads=[("T1", q), "ALIAS", ("ET", q)], writes=[("ET", q)])
            S.add("act", lambda e, o=self.ET[:, q, 64:160], a=self.ps[:, B, 0:96], bb=self.chi[:, lb, h:h + 1]:
                  e.activation(out=o, in_=a, func=AF.Exp, bias=bb, scale=1.0),
                  reads=[("ps", B), "chi", "ALIAS"], writes=[("ET2", q)])

        def pv(s, h, q):
            ob, oc = 4 + h // 7, (h % 7) * 65
            p0 = s * 32
            o = self.ps[p0:p0 + 32, ob, oc:oc + 65]
            rd = [("ET", q), ("ET2", q), ("Vs",), ("Vs1",), "ALIAS"] + [("V", s * 4 + k) for k in range(4)]
            self.mm(o, self.ET[p0:p0 + 16, q, 0:32], self.Vs[p0:p0 + 16, h, 0:65], True, False, rd, [("ps", ob)])
            self.mm(o, self.ET[:, q, 32:64], self.Vb[:, s * 4 + 3, h, 0:65], False, False, rd, [("ps", ob)])
            for kg in range(3):
                self.mm(o, self.ET[:, q, 64 + kg * 32:64 + (kg + 1) * 32], self.Vb[:, s * 4 + kg, h, 0:65], False, kg == 2,
                        rd, [("ps", ob)])
        qk(0, 0, 0)
        for i, (s, h) in enumerate(steps):
            if i + 1 < len(steps):
                qk(steps[i + 1][0], steps[i + 1][1], (i + 1) % 2)
            pv(s, h, i % 2)
        self.att_finish(t, 0, 64)
        self.proj_tok_resid([t], "uT", 8, lambda half, kh: ("wo", lb, half, kh), 0)

    def prologue(self):
        S = self.S
        d = self.d
        S.add("dve", lambda e: e.memset(self.epst[:, :], EPS), reads=[], writes=["epst"])
        S.add("dve", lambda e: e.memset(self.ones16[:, :], 1.0), reads=[], writes=["ones16"])
        S.add("dve", lambda e: e.memset(self.zpre_p[:, :, :, :, :].rearrange("p a b c d -> p (a b c d)"), 0.0),
              reads=[], writes=[("zpre_p", 0), ("zpre_p", 1)])
        for nm, tl in (("convw", self.convw), ("chi", self.chi), ("ident", self.ident), ("hv", self.hvt), ("lnT", self.lnT)):
            S.add("sp", lambda e, o=tl, a=d[nm]: [e.dma_start(out=o[tuple(slice(None) for _ in o.shape)], in_=a)],
                  reads=[], writes=[nm if nm != "hv" else "hvt"], dma=self.dkey(("c", nm)))

    def prologue2(self):
        self.S.add("dve", lambda e: e.tensor_copy(out=self.identb[:, :], in_=self.ident[:, :]), reads=["ident"], writes=["identb"])

    def bind(self, t):
        if t.kind == "sample":
            t.xres, t.xT, t.uT, t.hT, t.kp, t.lnoff = self.s_xres, self.s_xT, self.s_uT, self.s_hT, "s_", 5
        else:
            t.xres, t.xT, t.uT, t.hT, t.kp, t.lnoff = self.m_xres, self.m_xT, self.m_uT, self.m_hT, "", 0
        return t

    def run_A(self, tiles):
        for t in tiles:
            self.load_x(t)
        for l in range(2):
            self.cur_l = l
            self.load_lnp(l, 0)
            self.mixer_A(tiles, l)
            self.load_lnp(l, 1)
            self.mlp(tiles, l)
        self.cur_l = 2
        self.proj_kv(tiles, False)

    def run_tile(self, t, out_kv=False):
        S = self.S
        cfg = self.cfg
        self.load_x(t)
        nl = cfg.get("nlayers", 4)
        for l in range(nl):
            self.cur_l = l
            self.load_lnp(l, 0)
            if l < 2:
                if cfg.get("mixer", True):
                    self.mixer_A([t], l)
            else:
                if l == 2:
                    self.proj_kv([t], out_kv)
                if not cfg.get("att", True):
                    pass
                else:
                    self.attention(t, l)
            if cfg.get("mlp", True):
                self.load_lnp(l, 1)
                self.mlp([t], l)
        self.store_y(t)

    def store_y(self, t):
        self.flush_pool()
        S = self.S
        od = self.d["ys"] if t.kind == "sample" else self.d["y"]
        r0 = 0 if t.kind == "sample" else t.idx * 512
        for gi, (c0, sz) in enumerate(t.groups):
            S.add("sp", lambda e, o=od[r0 + c0:r0 + c0 + sz, :], a=t.xres[:sz, gi, :]: [e.dma_start(out=o, in_=a)],
                  reads=[(t.kp + "xres", gi)], writes=[], dma=self.dkey((t.kp + "y", gi)))

    def emit_all(self):
        S = self.S
        d = self.d
        cfg = self.cfg
        self.prologue()
        self.prologue2()
        nm = cfg.get("nmain", NMAIN)
        halo = self.bind(mk_tile("halo"))
        samp = self.bind(mk_tile("sample"))
        do_s = cfg.get("sample", True)
        if do_s:
            S.add("sp", lambda e: [e.dma_start(out=self.zpre_s[:, :, :, :, :], in_=d["cconv"])],
                  reads=[], writes=[("zpre_s", 0), ("zpre_s", 1)], dma=self.dkey("cconv"))
        self.run_A([halo, samp] if do_s else [halo])
        if do_s:
            S.add("sp", lambda e: [e.dma_start(out=d["convs"], in_=self.zpre_s[:, :, :, :, :])],
                  reads=[("zpre_s", 0), ("zpre_s", 1)], writes=[], dma=self.dkey("convs"))
        for i in range(nm):
            t = self.bind(mk_tile("main", i))
            self.run_tile(t, out_kv=(i == nm - 1))
        S.add("sp", lambda e: [e.dma_start(out=d["convp"], in_=self.zpre_p[:, :, :, :, :])],
              reads=[("zpre_p", 0), ("zpre_p", 1)], writes=[], dma=self.dkey("convp"))
        if do_s:
            t = samp
            S.tag = "sample0.cache"
            S.add("pool", lambda e: [e.dma_start(out=self.KT[:, c, :], in_=d["ckT"][:, c, :]) for c in range(8)],
                  reads=[], writes=[("KTc", c) for c in range(8)], dma=self.dkey("ckT"))
            S.add("pool", lambda e: [e.dma_start(out=self.Vb[:, sl, :, 0:64],
                                                 in_=d["cv"][:, sl, :].rearrange("p (h e) -> p h e", e=64))
                                     for sl in range(8)],
                  reads=[], writes=[("V", sl) for sl in range(8)], dma=self.dkey("cv"))
            S.add("dve", lambda e: e.memset(self.Vb[:, :, :, 64:65].rearrange("p a b c -> p (a b) c"), 1.0),
                  reads=[], writes=[("V1", sl) for sl in range(8)])
            for l in (2, 3):
                self.cur_l = l
                self.load_lnp(l, 0)
                self.attention_sample(t, l)
                self.load_lnp(l, 1)
                self.mlp([t], l)
            self.store_y(t)

    def build(self):
        nc = self.nc
        self.declare_dram()
        es = ExitStack()
        self.es = es
        with es:
            self.alloc(es)
            self.emit_all()
            plan = self.wrec
            self.S = Sched()
            self.P = PsumRR(range(6))
            self.PT = PsumRR((6, 7))
            self.P8 = PsumRR(range(8))
            self.cnt = 0
            self.wplan = plan
            self.wscr = {}
            self.wuses = {}
            for k in plan:
                self.wuses[k] = self.wuses.get(k, 0) + 1
            self.wcur = 0
            self.wissued = 0
            self.dma_keys = {}
            self.emit_all()
            S = self.S
            S.finalize()
            ops = S.ops
            self._esem = {e: es.enter_context(nc.semaphore("sem_" + e)) for e in ("pe", "act", "dve", "pool")}
            self._dsem = {k: es.enter_context(nc.semaphore("dsem%d" % i)) for i, k in enumerate(self.dma_keys)}
            per_eng = {e: [] for e in Sched.ENGS}
            for i, op in enumerate(ops):
                per_eng[op.eng].append(i)
            self._ops = ops
            with nc.Block() as block:
                self._emit_block(block, per_eng)
        return nc

    def _emit_block(self, block, per_eng):
        ops = self._ops
        esem, dsem = self._esem, self._dsem
        class Fake:
            def dma_start(self, **kw):
                return 1
        fake = Fake()
        cnt = {e: 0 for e in esem}
        dcnt = {k: 0 for k in dsem}
        for op in ops:
            if op.dma is not None:
                n = len(op.fn(fake))
                dcnt[op.dma] += 16 * n
                op.ev = (dsem[op.dma], dcnt[op.dma])
            elif op.need:
                cnt[op.eng] += 1
                op.ev = (esem[op.eng], cnt[op.eng])
        final_d = dict(dcnt)

        def run(engname, e):
            waited = {}
            for i in per_eng[engname]:
                op = ops[i]
                for dd in op.rdeps:
                    sem, val = ops[dd].ev
                    key = id(sem)
                    if waited.get(key, 0) < val:
                        e.wait_ge(sem, val)
                        waited[key] = val
                if op.dma is not None:
                    for ins in op.fn(e):
                        ins.then_inc(op.ev[0], 16)
                else:
                    ins = op.fn(e)
                    if op.need:
                        ins.then_inc(op.ev[0], 1)
            if engname == "sp":
                for k, v in final_d.items():
                    if v > 0 and waited.get(id(dsem[k]), 0) < v:
                        e.wait_ge(dsem[k], v)

        @block.tensor
        def _(e):
            run("pe", e)

        @block.scalar
        def _(e):
            run("act", e)

        @block.vector
        def _(e):
            run("dve", e)

        @block.gpsimd
        def _(e):
            run("pool", e)

        @block.sync
        def _(e):
            run("sp", e)


def _bias_tables(rel_bias_b):
    nb = rel_bias_b.shape[0]
    jj = np.arange(128)[:, None]
    ii = np.arange(256)[None, :]
    idx = np.clip(ii - jj, -128, 128) + 128
    bv = np.empty((nb, 128, 16, 256), np.float32)
    for l in range(nb):
        tb = rel_bias_b[l]
        g = tb[idx]
        bv[l] = np.transpose(g, (0, 2, 1))
    bv[:, 64:128, :, 0:64] = NEG
    j2 = (np.arange(64) % 32)[:, None]
    i2 = np.arange(32)[None, :]
    idx2 = np.clip(i2 - j2, -128, 128) + 128
    bvs = np.empty((nb, 64, 16, 32), np.float32)
    for l in range(nb):
        bvs[l] = np.transpose(rel_bias_b[l][idx2], (0, 2, 1))
    chi = np.broadcast_to(np.transpose(rel_bias_b[:, 256, :], (0, 1))[None], (128, nb, 16)).astype(np.float32)
    return bv, bvs, np.ascontiguousarray(chi)


_NC_CACHE = {}


def _get_nc(cfg_key=()):
    if cfg_key not in _NC_CACHE:
        b = Builder(dict(cfg_key))
        _NC_CACHE[cfg_key] = b.build()
    return _NC_CACHE[cfg_key]


def kernel(x_prompt, x_sample, cache_conv, cache_k, cache_v, ln_mix_g, ln_mix_b, ln_ffn_g,
           ln_ffn_b, w_up, w_down, w_in_a, conv_w_a, w_out_a, w_k, w_v, w_q_b, w_o_b, rel_bias_b, _cfg=()):
    f = lambda a: np.ascontiguousarray(np.asarray(a, dtype=np.float32))
    x_prompt, x_sample, cache_conv, cache_k, cache_v = map(f, (x_prompt, x_sample, cache_conv, cache_k, cache_v))
    rel_bias_b = f(rel_bias_b)
    xp = x_prompt[0]
    lnp = f(np.stack([ln_mix_g, ln_mix_b, ln_ffn_g, ln_ffn_b], axis=1))
    convw = f(np.transpose(f(conv_w_a).reshape(2, 3, 8, 128), (3, 0, 2, 1)))
    lnT = f(np.transpose(lnp.reshape(4, 4, 8, 128), (3, 0, 1, 2)))
    bv, bvs, chi = _bias_tables(rel_bias_b)
    ident = np.eye(128, dtype=np.float32)
    shared = dict(lnp=lnp, lnT=lnT, convw=convw, bv=bv, bvs=bvs, chi=chi, ident=ident,
                  w_up=f(w_up), w_down=f(w_down), w_in_a=f(w_in_a), w_out_a=f(w_out_a),
                  w_k=f(w_k), w_v=f(w_v), w_q_b=f(w_q_b), w_o_b=f(w_o_b))
    in_maps = []
    for c in range(NCORES):
        s = c * TPC
        xh = np.zeros((XROWS, D), np.float32)
        lo = s - HALO
        a = max(lo, 0)
        xh[a - lo:] = xp[a:s + TPC]
        xs = np.zeros((64, D), np.float32)
        xs[0:16] = x_sample[2 * c]
        xs[32:48] = x_sample[2 * c + 1]
        hv = np.full((128, 1), 0.0 if c == 0 else 1.0, np.float32)
        cc = cache_conv[:, 2 * c:2 * c + 2].reshape(2, 2, 2, 8, 128)
        cconv = f(np.transpose(cc, (4, 0, 1, 3, 2)))
        ck = cache_k[2 * c:2 * c + 2].reshape(2, 512, 8, 128)
        ckT = f(np.transpose(ck, (3, 2, 0, 1)).reshape(128, 8, 1024))
        cvv = cache_v[2 * c:2 * c + 2].reshape(2, 4, 128, 1024)
        cv = f(np.transpose(cvv, (2, 0, 1, 3)).reshape(128, 8, 1024))
        m = dict(shared)
        m.update(xh=xh, xs=xs, hv=hv, cconv=cconv, ckT=ckT, cv=cv)
        in_maps.append(m)
    nc = _get_nc(_cfg)
    ncr = dict(_cfg).get("ncores", NCORES)
    res = run_bass_kernel_spmd(nc, in_maps[:ncr], core_ids=list(range(ncr)))
    R = list(res.results)
    while len(R) < NCORES:
        R.append(R[0])
    y_prompt = np.concatenate([R[c]["y"] for c in range(NCORES)], axis=0)[None]
    y_sample = np.empty((16, 16, D), np.float32)
    conv_sample = np.empty((2, 16, 2, D), np.float32)
    k_sample = np.empty((16, 16, 16, 64), np.float32)
    v_sample = np.empty((16, 16, 16, 64), np.float32)
    for c in range(NCORES):
        r = R[c]
        for sg in range(2):
            b = 2 * c + sg
            y_sample[b] = r["ys"][sg * 32:sg * 32 + 16]
            v_sample[b] = r["vs"][sg * 32:sg * 32 + 16].reshape(16, 16, 64)
            kk = r["ksT"][:, :, sg * 32:sg * 32 + 16]
            k_sample[b] = np.transpose(kk, (2, 1, 0)).reshape(16, 16, 64)
            cs = r["convs"][:, :, sg]
            conv_sample[:, b] = np.transpose(cs, (1, 3, 2, 0)).reshape(2, 2, D)
    last = R[NCORES - 1]
    cp = last["convp"][:, :, 0]
    conv_prompt = np.ascontiguousarray(np.transpose(cp, (1, 3, 2, 0)).reshape(2, 1, 2, D))
    k_prompt = np.ascontiguousarray(np.transpose(last["kT"], (2, 1, 0)).reshape(1, 512, 16, 64))
    v_prompt = np.ascontiguousarray(last["v"].reshape(1, 512, 16, 64))
    return (y_prompt, y_sample, conv_prompt, k_prompt, v_prompt, conv_sample, k_sample, v_sample)
```

```python
import numpy as np
from contextlib import ExitStack
import concourse.bass as bass
import concourse.mybir as mybir
from concourse.bass_utils import run_bass_kernel_spmd

F32 = mybir.dt.float32
BF16 = mybir.dt.bfloat16
AF = mybir.ActivationFunctionType
ALU = mybir.AluOpType

D = 1024
NCORES = 8
TPC = 2048
NMAIN = 4
HALO = 516
XROWS = HALO + TPC
ALPHA = 8.0 ** 0.25
EPS = 1e-5
NEG = -30000.0
NS = 3
SLOT = 8192
WA = 520


class Op:
    __slots__ = ("eng", "fn", "deps", "dma", "rdeps", "ev", "need", "tag")

    def __init__(self, eng, fn, deps, dma):
        self.eng, self.fn, self.deps, self.dma = eng, fn, deps, dma
        self.rdeps, self.ev, self.need = [], None, False


class Sched:
    ENGS = ("pe", "act", "dve", "pool", "sp")

    def __init__(self):
        self.ops = []
        self.lastw = {}
        self.readers = {}
        self.tag = ""

    def add(self, eng, fn, reads=(), writes=(), dma=None):
        i = len(self.ops)
        deps = set()
        for r in reads:
            w = self.lastw.get(r)
            if w is not None:
                deps.add(w)
        for r in writes:
            w = self.lastw.get(r)
            if w is not None:
                deps.add(w)
            for j in self.readers.get(r, {}).values():
                deps.add(j)
        chan = ("dma", dma) if dma is not None else eng
        for r in reads:
            self.readers.setdefault(r, {})[chan] = i
        for r in writes:
            self.lastw[r] = i
            self.readers[r] = {}
        deps.discard(i)
        op = Op(eng, fn, deps, dma)
        op.tag = self.tag
        self.ops.append(op)
        return i

    def chan(self, op):
        return ("dma", op.dma) if op.dma is not None else op.eng

    def finalize(self):
        ops = self.ops
        for op in ops:
            by = {}
            for d in op.deps:
                c = self.chan(ops[d])
                if c == "pe" and op.eng == "pe" and op.dma is None:
                    continue
                if by.get(c, -1) < d:
                    by[c] = d
            op.rdeps = sorted(by.values())
            for d in op.rdeps:
                ops[d].need = True


class PsumRR:
    def __init__(self, banks):
        self.banks = list(banks)
        self.i = 0

    def next(self):
        b = self.banks[self.i % len(self.banks)]
        self.i += 1
        return b


class Tile:
    pass


def mk_tile(kind, idx=0):
    t = Tile()
    t.kind = kind
    t.idx = idx
    if kind == "halo":
        t.W = 516
        t.cbs = [(0, 258, 0, 0), (258, 258, 0, 258)]
        t.groups = [(0, 4), (4, 128), (132, 128), (260, 128), (388, 128)]
        t.segs = [(0, 516, 516)]
        t.row0 = 0
        t.par = 0
        t.kcbs = [(4, 256), (260, 256)]
        t.kgroups = [1, 2, 3, 4]
    elif kind == "main":
        t.W = 512
        t.cbs = [(0, 512, 0, 0)]
        t.groups = [(0, 128), (128, 128), (256, 128), (384, 128)]
        t.segs = [(0, 512, 512)]
        t.row0 = HALO + idx * 512
        t.par = (idx + 1) % 2
        t.kcbs = [(0, 512)]
        t.kgroups = [0, 1, 2, 3]
    else:
        t.W = 64
        t.cbs = [(0, 32, 0, 0), (32, 32, 1, 0)]
        t.groups = [(0, 64)]
        t.segs = [(0, 32, 16), (32, 32, 16)]
        t.row0 = 0
        t.par = 0
        t.kcbs = [(0, 64)]
        t.kgroups = [0]
    return t


def groups_of(t, c0, n):
    return [gi for gi, (g0, sz) in enumerate(t.groups) if g0 < c0 + n and c0 < g0 + sz]


def xt_keys(t, gis):
    return [(t.kp + "xT", x) for x in gis] + [(t.kp + "xTd", x) for x in gis]


class Builder:
    def __init__(self, cfg):
        self.cfg = cfg
        self.nc = bass.Bass("TRN2", target_bir_lowering=False)
        self.S = Sched()
        self.P = PsumRR(range(6))
        self.PT = PsumRR((6, 7))
        self.P8 = PsumRR(range(8))
        self.cnt = 0
        self.wplan = None
        self.wrec = []
        self.wissued = 0
        self.wcur = 0
        self.dma_keys = {}
        self.xqmap = {}
        self.pdef = []

    def declare_dram(self):
        nc = self.nc

        def din(name, shape):
            return nc.dram_tensor(name, list(shape), F32, kind="ExternalInput").ap()

        def dout(name, shape):
            return nc.dram_tensor(name, list(shape), F32, kind="ExternalOutput").ap()

        d = {}
        d["xh"] = din("xh", [XROWS, D])
        d["xs"] = din("xs", [64, D])
        d["hv"] = din("hv", [128, 1])
        d["cconv"] = din("cconv", [128, 2, 2, 8, 2])
        d["ckT"] = din("ckT", [128, 8, 1024])
        d["cv"] = din("cv", [128, 8, 1024])
        d["lnp"] = din("lnp", [4, 4, D])
        d["lnT"] = din("lnT", [128, 4, 4, 8])
        d["convw"] = din("convw", [128, 2, 8, 3])
        d["bv"] = din("bv", [2, 128, 16, 256])
        d["bvs"] = din("bvs", [2, 64, 16, 32])
        d["chi"] = din("chi", [128, 2, 16])
        d["ident"] = din("ident", [128, 128])
        d["w_up"] = din("w_up", [4, D, 4 * D])
        d["w_down"] = din("w_down", [4, 4 * D, D])
        d["w_in_a"] = din("w_in_a", [2, D, 3 * D])
        d["w_out_a"] = din("w_out_a", [2, D, D])
        d["w_k"] = din("w_k", [D, D])
        d["w_v"] = din("w_v", [D, D])
        d["w_q_b"] = din("w_q_b", [2, D, D])
        d["w_o_b"] = din("w_o_b", [2, D, D])
        d["y"] = dout("y", [TPC, D])
        d["ys"] = dout("ys", [64, D])
        d["convp"] = dout("convp", [128, 2, 2, 8, 2])
        d["kT"] = dout("kT", [128, 8, 512])
        d["v"] = dout("v", [512, D])
        d["convs"] = dout("convs", [128, 2, 2, 8, 2])
        d["ksT"] = dout("ksT", [128, 8, 64])
        d["vs"] = dout("vs", [64, D])
        self.d = d

    def alloc(self, es):
        nc = self.nc

        def sb(name, shape, dt=F32):
            return es.enter_context(nc.sbuf_tensor("sb_" + name, list(shape), dt))

        self.m_xres = sb("xres", [128, 5, D])
        self.m_xT = sb("xT", [128, 8, WA], BF16)
        self.m_uT = sb("uT", [128, 8, WA], BF16)
        self.big = sb("big", [128, 32 * WA], BF16)
        self.m_hT = self.big[:, :].rearrange("p (j w) -> p j w", w=WA)
        o = 0
        self.QT = sb("QT", [128, 8, WA], BF16)
        self.BV = self.big[:, o:o + 16 * 256 * 2].bitcast(F32).rearrange("p (h i) -> p h i", i=256)
        o += 16 * 256 * 2
        self.ET = self.big[:, o:o + 3 * 640].rearrange("p (q i) -> p q i", i=640)
        o += 3 * 640
        self.T1 = self.big[:, o:o + 3 * 256 * 2].bitcast(F32).rearrange("p (q i) -> p q i", i=256)
        o += 3 * 256 * 2
        self.On = self.big[:, o:o + 1024 * 2].bitcast(F32)
        self.Onb = self.big[:, o:o + 1024]
        o += 1024 * 2
        assert o <= 32 * WA
        self.BVs = sb("BVs", [64, 16, 32])
        self.rd2 = sb("rd2", [128, 2])
        self.lnT = sb("lnT", [128, 4, 4, 8])
        self.zt = sb("zt", [128, 2, 524])
        self.csb = sb("csb", [128, 2, 512])
        self.ysb = sb("ysb", [128, 2, 512])
        self.kst = self.csb
        self.vst = self.ysb
        self.sq = sb("sq", [128, 2, 512], BF16)
        self.KT = sb("KT", [128, 8, 1024], BF16)
        self.Vb = sb("Vb", [128, 8, 16, 66], BF16)
        self.KTs = sb("KTs", [128, 8, 64], BF16)
        self.Vs = sb("Vs", [64, 16, 66], BF16)
        self.lnp = sb("lnp", [128, 2, D])
        self.xnb = sb("xnb", [128, 2, D], BF16)
        self.identb = sb("identb", [128, 128], BF16)
        self.st5 = sb("st5", [128, 6, 12])
        self.mv5 = sb("mv5", [128, 6, 2])
        self.sd5 = sb("sd5", [128, 6, 1])
        self.rs5 = sb("rs5", [128, 6, 1])
        self.nm5 = sb("nm5", [128, 6, 1])
        self.s_xres = sb("s_xres", [128, 1, D])
        self.s_xT = sb("s_xT", [128, 8, 64], BF16)
        self.s_uT = sb("s_uT", [128, 8, 64], BF16)
        self.s_hT = sb("s_hT", [128, 32, 64], BF16)
        self.zpre_p = sb("zpre_p", [128, 2, 2, 8, 2])
        self.zpre_s = sb("zpre_s", [128, 2, 2, 8, 2])
        self.convw = sb("convw", [128, 2, 8, 3])
        self.chi = sb("chi", [128, 2, 16])
        self.ident = sb("ident", [128, 128])
        self.hvt = sb("hvt", [128, 1])
        self.st = sb("st", [128, 2, 12])
        self.mv = sb("mv", [128, 2, 2])
        self.sd = sb("sd", [128, 2, 1])
        self.rs = sb("rs", [128, 2, 1])
        self.rden = sb("rden", [128, 16])
        self.fz = sb("fz", [128, 2])
        self.epst = sb("epst", [128, 1])
        self.ones16 = sb("ones16", [128, 16])
        self.wring = sb("wring", [128, NS, SLOT], BF16)
        self.ps = es.enter_context(nc.psum_tensor("ps", [128, 8, 512], F32))

    def nxt(self):
        self.cnt += 1
        return self.cnt % 2

    def dkey(self, key):
        self.dma_keys[key] = True
        return key

    def mm(self, out, lhsT, rhs, start, stop, reads, writes):
        self.S.add("pe", lambda e, o=out, l=lhsT, r=rhs, s=start, t=stop:
                   e.matmul(o, l, r, start=s, stop=t), reads, writes)

    def fence(self):
        self.S.add("dve", lambda e, a=self.fz[:, 0:2]: e.memset(a, 0.0), reads=[], writes=["ALIAS", "fz"])

    def wsrc(self, key):
        d = self.d
        kind = key[0]
        if kind == "win":
            _, l, jp = key
            v = d["w_in_a"][l].rearrange("(kc p) (w f) -> p kc w f", p=128, w=3)
            return [(v[:, :, wi, jp * 256:(jp + 1) * 256], (8, 256), wi * 8 * 256) for wi in range(3)], (8 * 3 * 256)
        if kind == "wup":
            _, l, jp = key
            v = d["w_up"][l].rearrange("(kc p) f -> p kc f", p=128)
            return [(v[:, :, jp * 512:(jp + 1) * 512], (8, 512), 0)], 8 * 512
        if kind == "wdown":
            _, l, half, kh = key
            v = d["w_down"][l].rearrange("(kc p) f -> p kc f", p=128)
            return [(v[:, kh * 16:(kh + 1) * 16, half * 512:(half + 1) * 512], (16, 512), 0)], 16 * 512
        if kind in ("wout", "wo", "wv"):
            _, l, half, kh = key
            src = {"wout": d["w_out_a"], "wo": d["w_o_b"]}.get(kind)
            m = d["w_v"] if kind == "wv" else src[l]
            v = m.rearrange("(kc p) f -> p kc f", p=128)
            return [(v[:, :, half * 512:(half + 1) * 512], (8, 512), 0)], 8 * 512
        if kind in ("wq", "wk"):
            _, l = key
            m = d["w_k"] if kind == "wk" else d["w_q_b"][l]
            v = m.rearrange("(kc p) f -> p kc f", p=128)
            return [(v[:, :, 0:512], (8, 512), 0), (v[:, :, 512:1024], (8, 512), 8 * 512)], 8 * 1024
        raise KeyError(key)

    def wissue(self, i):
        key = self.wplan[i]
        s = i % NS
        parts, tot = self.wsrc(key)
        if key in self.wscr:
            scr = self.wscr[key]
            self.S.add("sp", lambda e, o=self.wring[:, s, 0:tot], a=scr: [e.dma_start(out=o, in_=a)],
                       reads=[("scr", key)], writes=[("w", s)], dma=self.dkey(("w", s)))
            return
        outs = []
        for src, (a, b), off in parts:
            dst = self.wring[:, s, off:off + a * b].rearrange("p (a b) -> p a b", b=b)
            outs.append((dst, src))

        def fn(e, outs=outs):
            return [e.dma_start(out=o, in_=i_) for o, i_ in outs]
        self.S.add("pool", fn, reads=[], writes=[("w", s)], dma=self.dkey(("wc", s)))
        if self.cfg.get("scratch", True) and self.wuses.get(key, 0) > 1:
            scr = self.nc.dram_tensor("scr%d" % len(self.wscr), [128, tot], BF16, kind="Internal").ap()
            self.wscr[key] = scr
            self.S.add("sp", lambda e, o=scr, a=self.wring[:, s, 0:tot]: [e.dma_start(out=o, in_=a)],
                       reads=[("w", s)], writes=[("scr", key)], dma=self.dkey(("wb", s)))

    def flush_pool(self):
        ops, self.pdef = self.pdef, []
        for fn, rd, wr in ops:
            self.S.add("pool", fn, reads=rd, writes=wr)

    def wget(self, key, nheld=1):
        i = self.wcur
        self.wcur += 1
        if self.wplan is None:
            self.wrec.append(key)
            self.flush_pool()
            return 0
        assert self.wplan[i] == key, (self.wplan[i], key)
        while self.wissued < min(len(self.wplan), i + NS - (nheld - 1)):
            self.wissue(self.wissued)
            self.wissued += 1
        self.flush_pool()
        return i % NS

    def wview(self, s, kind):
        r = self.wring[:, s, :]
        if kind == "win":
            return r[:, 0:8 * 3 * 256].rearrange("p (w k f) -> p k w f", w=3, k=8)
        if kind == "k512":
            return r[:, 0:8 * 512].rearrange("p (k f) -> p k f", f=512)
        if kind == "k16":
            return r[:, 0:16 * 512].rearrange("p (k f) -> p k f", f=512)
        if kind == "full":
            return r[:, 0:8 * 1024].rearrange("p (h k f) -> p k h f", h=2, k=8)
        raise KeyError(kind)

    def to_feat(self, t, gi, src, dst, dkeyname, src_reads, extra_reads=(), gb=None):
        c0, sz = t.groups[gi]
        S = self.S
        for c4 in range(2):
            bank = self.PT.next()
            for i in range(4):
                c = c4 * 4 + i
                S.add("pe", lambda e, o=self.ps[:, bank, i * 128:i * 128 + sz],
                      a=src[:, c * 128:(c + 1) * 128], idn=self.ident[:sz, :sz]:
                      e.transpose(o, a, idn),
                      reads=list(src_reads) + ["ident"], writes=[("ps", bank)])
            if gb is not None:
                l_, li_ = gb
                for i in range(4):
                    c = c4 * 4 + i
                    S.add("act", lambda e, o=dst[:, c, c0:c0 + sz], a=self.ps[:, bank, i * 128:i * 128 + sz],
                          g_=self.lnT[:, l_, li_ * 2, c:c + 1], b_=self.lnT[:, l_, li_ * 2 + 1, c:c + 1]:
                          e.activation(out=o, in_=a, func=AF.Identity, bias=b_, scale=g_),
                          reads=[("ps", bank), "lnT"] + list(extra_reads), writes=[(t.kp + dkeyname, gi)])
                continue
            pv = self.ps[:, bank, :].rearrange("p (a b) -> p a b", b=128)[:, :, 0:sz]
            S.add("act", lambda e, o=dst[:, c4 * 4:(c4 + 1) * 4, c0:c0 + sz], a=pv:
                  e.activation(out=o, in_=a, func=AF.Copy),
                  reads=[("ps", bank)] + list(extra_reads), writes=[(t.kp + dkeyname, gi)])

    def load_x(self, t):
        self.S.tag = "%s%d.loadx" % (t.kind, t.idx)
        S = self.S
        src = self.d["xs"] if t.kind == "sample" else self.d["xh"]
        for gi, (c0, sz) in enumerate(t.groups):
            S.add("sp", lambda e, o=t.xres[:sz, gi, :], a=src[t.row0 + c0:t.row0 + c0 + sz, :]:
                  [e.dma_start(out=o, in_=a)],
                  reads=[], writes=[(t.kp + "xres", gi)], dma=self.dkey((t.kp + "x", gi)))
            self.to_feat(t, gi, t.xres[:sz, gi, :], t.xT, "xT", [(t.kp + "xres", gi)])

    def load_lnp(self, l, which):
        self.flush_pool()
        def fn(e, l=l, which=which):
            return [e.dma_start(out=self.lnp[:, v, :], in_=self.d["lnp"][l, which * 2 + v].partition_broadcast(128))
                    for v in range(2)]
        self.S.add("sp", fn, reads=[], writes=["lnp"], dma=self.dkey("lnp"))

    def ln(self, t, gi, lnidx):
        S = self.S
        c0, sz = t.groups[gi]
        q = self.nxt()
        xr = t.xres[:sz, gi, :]
        st, mv, sd, rs = self.st[:sz, q, :], self.mv[:sz, q, :], self.sd[:sz, q, :], self.rs[:sz, q, :]
        X = (t.kp + "xres", gi)
        S.add("dve", lambda e: e.bn_stats(st[:, 0:6], xr[:, 0:512]), reads=[X], writes=[("st", q)])
        S.add("dve", lambda e: e.bn_stats(st[:, 6:12], xr[:, 512:1024]), reads=[X, ("st", q)], writes=[("st", q)])
        S.add("dve", lambda e: e.bn_aggr(mv, st), reads=[("st", q)], writes=[("mv", q)])
        S.add("act", lambda e: e.activation(out=sd, in_=mv[:, 1:2], func=AF.Sqrt, bias=self.epst[:sz, :], scale=1.0),
              reads=[("mv", q), "epst"], writes=[("sd", q)])
        S.add("dve", lambda e: e.reciprocal(rs, sd), reads=[("sd", q)], writes=[("rs", q)])
        S.add("dve", lambda e: e.tensor_scalar(out=xr, in0=xr, scalar1=mv[:, 0:1], scalar2=rs,
                                               op0=ALU.subtract, op1=ALU.mult),
              reads=[X, ("mv", q), ("rs", q)], writes=[X])

    def ln_stats(self, t, gi, lnidx):
        S = self.S
        c0, sz = t.groups[gi]
        q = gi + t.lnoff
        xr = t.xres[:sz, gi, :]
        st, mv, sd, rs = self.st5[:sz, q, :], self.mv5[:sz, q, :], self.sd5[:sz, q, :], self.rs5[:sz, q, :]
        self.xq = getattr(self, "xq", 0) + 1
        xi = self.xq % 2
        self.xqmap[(t.kp, gi)] = xi
        xb = self.xnb[:sz, xi, :]
        X = (t.kp + "xres", gi)
        S.add("dve", lambda e: e.bn_stats(st[:, 0:6], xr[:, 0:512]), reads=[X], writes=[("st5", q)])
        S.add("dve", lambda e: e.bn_stats(st[:, 6:12], xr[:, 512:1024]), reads=[X, ("st5", q)], writes=[("st5", q)])
        S.add("dve", lambda e: e.bn_aggr(mv, st), reads=[("st5", q)], writes=[("mv5", q)])
        S.add("act", lambda e: e.activation(out=sd, in_=mv[:, 1:2], func=AF.Sqrt, bias=self.epst[:sz, :], scale=1.0),
              reads=[("mv5", q), "epst"], writes=[("sd5", q)])
        S.add("dve", lambda e: e.reciprocal(rs, sd), reads=[("sd5", q)], writes=[("rs5", q)])
        nm = self.nm5[:sz, q, :]
        S.add("dve", lambda e: e.tensor_scalar(out=nm, in0=mv[:, 0:1], scalar1=-1.0, scalar2=rs, op0=ALU.mult, op1=ALU.mult),
              reads=[("mv5", q), ("rs5", q)], writes=[("nm5", q)])
        S.add("act", lambda e: e.activation(out=xb, in_=xr, func=AF.Identity, bias=nm, scale=rs),
              reads=[X, ("nm5", q), ("rs5", q)], writes=[("xnb", xi)])
        self.pdef.append((lambda e: e.tensor_scalar(out=xr, in0=xr, scalar1=rs, scalar2=nm, op0=ALU.mult, op1=ALU.add),
                          [X, ("nm5", q), ("rs5", q)], [X]))
        self.pdef.append((lambda e: e.tensor_tensor(out=xr, in0=xr, in1=self.lnp[:sz, 0, :], op=ALU.mult),
                          [X, "lnp"], [X]))
        self.pdef.append((lambda e: e.tensor_tensor(out=xr, in0=xr, in1=self.lnp[:sz, 1, :], op=ALU.add),
                          [X, "lnp"], [X]))

    def ln_out(self, t, gi, lnidx):
        S = self.S
        c0, sz = t.groups[gi]
        l_ = self.cur_l
        xi = self.xqmap[(t.kp, gi)]
        xb = self.xnb[:sz, xi, :]
        for c4 in range(2):
            bank = self.PT.next()
            psb = self.ps[:, bank, :].bitcast(BF16)
            for i in range(4):
                c = c4 * 4 + i
                S.add("pe", lambda e, o=psb[:, i * 128:i * 128 + sz], a=xb[:, c * 128:(c + 1) * 128],
                      idn=self.identb[:sz, :sz]: e.transpose(o, a, idn),
                      reads=[("xnb", xi), "identb"], writes=[("ps", bank)])
            for i in range(4):
                c = c4 * 4 + i
                o = t.xT[:, c, c0:c0 + sz]
                a = psb[:, i * 128:i * 128 + sz]
                g_ = self.lnT[:, l_, lnidx * 2, c:c + 1]
                b_ = self.lnT[:, l_, lnidx * 2 + 1, c:c + 1]
                if c4 == 0:
                    S.add("act", lambda e, o=o, a=a, g_=g_, b_=b_: e.activation(out=o, in_=a, func=AF.Identity, bias=b_, scale=g_),
                          reads=[("ps", bank), "lnT"], writes=[(t.kp + "xT", gi)])
                else:
                    S.add("dve", lambda e, o=o, a=a, g_=g_, b_=b_: e.tensor_scalar(out=o, in0=a, scalar1=g_, scalar2=b_,
                                                                                 op0=ALU.mult, op1=ALU.add),
                          reads=[("ps", bank), "lnT"], writes=[(t.kp + "xTd", gi)])

    def ln_gb(self, t, gi, lnidx):
        S = self.S
        c0, sz = t.groups[gi]
        xr = t.xres[:sz, gi, :]
        X = (t.kp + "xres", gi)
        eng = self.cfg.get("gb_eng", "pool")
        S.add(eng, lambda e: e.tensor_tensor(out=xr, in0=xr, in1=self.lnp[:sz, 0, :], op=ALU.mult),
              reads=[X, "lnp"], writes=[X])
        S.add(eng, lambda e: e.tensor_tensor(out=xr, in0=xr, in1=self.lnp[:sz, 1, :], op=ALU.add),
              reads=[X, "lnp"], writes=[X])

    def proj_tok_resid(self, tiles, bufname, nk, wkeyf, lnidx, alias=False):
        S = self.S
        self.flush_pool()
        S.tag = S.tag.rsplit(".", 1)[0] + (".down" if nk == 32 else ".oproj")
        kper = min(nk, 16)
        nkh = nk // kper
        ar = ["ALIAS"] if alias else []
        vk = "k16" if kper == 16 else "k512"
        tg = [(t, gi) for t in tiles for gi in range(len(t.groups))]

        def resid(t, gi, half, b):
            c0, sz = t.groups[gi]
            xr = t.xres[:sz, gi, half * 512:(half + 1) * 512]
            S.add("dve", lambda e, xr=xr, p=self.ps[:sz, b, :]:
                  e.scalar_tensor_tensor(out=xr, in0=xr, scalar=ALPHA, in1=p, op0=ALU.mult, op1=ALU.add),
                  reads=[(t.kp + "xres", gi), ("ps", b)], writes=[(t.kp + "xres", gi)])

        def mms(t, gi, b, s, wv, kh):
            c0, sz = t.groups[gi]
            actT = getattr(t, bufname)
            ard = [(t.kp + bufname, x) for x in range(len(t.cbs))] + ar
            for kk in range(kper):
                k = kh * kper + kk
                self.mm(self.ps[:sz, b, :], actT[:, k, c0:c0 + sz], wv[:, kk, :], k == 0, k == nk - 1,
                        reads=[("w", s)] + ard, writes=[("ps", b)])
        pending = None
        if nkh == 1:
            ss = [self.wget(wkeyf(0, 0)), self.wget(wkeyf(1, 0), nheld=2)]
            wvs = [self.wview(s, vk) for s in ss]
            for (t, gi) in tg:
                bs = []
                for half in range(2):
                    b = self.P.next()
                    bs.append(b)
                    mms(t, gi, b, ss[half], wvs[half], 0)
                for half in range(2):
                    resid(t, gi, half, bs[half])
                self.ln_stats(t, gi, lnidx)
                if pending is not None:
                    self.ln_out(pending[0], pending[1], lnidx)
                pending = (t, gi)
            self.ln_out(pending[0], pending[1], lnidx)
            return
        banks = {}
        assert len(tg) <= 6
        for kh in range(nkh):
            s = self.wget(wkeyf(0, kh))
            wv = self.wview(s, vk)
            for i, (t, gi) in enumerate(tg):
                if kh == 0:
                    banks[i] = self.P.next()
                mms(t, gi, banks[i], s, wv, kh)
        for i, (t, gi) in enumerate(tg):
            resid(t, gi, 0, banks[i])
        ss = [self.wget(wkeyf(1, kh), nheld=kh + 1) for kh in range(nkh)]
        wvs = [self.wview(s, vk) for s in ss]
        for (t, gi) in tg:
            b = self.P.next()
            for kh in range(nkh):
                mms(t, gi, b, ss[kh], wvs[kh], kh)
            resid(t, gi, 1, b)
            self.ln_stats(t, gi, lnidx)
            if pending is not None:
                self.ln_out(pending[0], pending[1], lnidx)
            pending = (t, gi)
        self.ln_out(pending[0], pending[1], lnidx)

    def mixer_A(self, tiles, l):
        self.S.tag = "%s%d.L%d.mixer" % (tiles[0].kind, tiles[0].idx, l)
        S = self.S
        for jp in range(4):
            s = self.wget(("win", l, jp))
            wv = self.wview(s, "win")
            for jj in range(2):
              for t in tiles:
                zpre = self.zpre_s if t.kind == "sample" else self.zpre_p
                zk = "zpre_s" if t.kind == "sample" else "zpre_p"
                L = t.segs[0][1]
                j = jp * 2 + jj
                self.zcnt = getattr(self, 'zcnt', 0) + 1
                zq = self.zcnt % 2
                Z = ("zt", zq)
                for si, (sc0, L_, Lr) in enumerate(t.segs):
                    zoff = si * (L + 2)
                    S.add("dve", lambda e, o=self.zt[:, zq, zoff:zoff + 2], a=zpre[:, l, si, j, :]:
                          e.tensor_copy(out=o, in_=a), reads=[(zk, l)], writes=[Z])
                for cbi, (c0, n, si, off) in enumerate(t.cbs):
                    bk = [self.P8.next() for _ in range(3)]
                    xr = xt_keys(t, groups_of(t, c0, n))
                    for wi in range(3):
                        for k in range(8):
                            self.mm(self.ps[:, bk[wi], 0:n], wv[:, k, wi, jj * 128:(jj + 1) * 128],
                                    t.xT[:, k, c0:c0 + n], k == 0, k == 7,
                                    reads=[("w", s)] + xr, writes=[("ps", bk[wi])])
                    q = self.nxt()
                    zo = si * (L + 2) + 2 + off
                    pb_, pc_, ph_ = (self.ps[:, bk[i], 0:n] for i in range(3))
                    cs, ys = self.csb[:, q, 0:n], self.ysb[:, q, 0:n]
                    zt = self.zt
                    cw = self.convw
                    S.add("act", lambda e, cs=cs, pc_=pc_: e.activation(out=cs, in_=pc_, func=AF.Copy),
                          reads=[("ps", bk[1])], writes=[("csb", q)])
                    S.add("dve", lambda e, o=zt[:, zq, zo:zo + n], cs=cs, ph_=ph_:
                          e.tensor_tensor(out=o, in0=cs, in1=ph_, op=ALU.mult),
                          reads=[("csb", q), ("ps", bk[2])], writes=[Z])
                    S.add("act", lambda e, ys=ys, a=zt[:, zq, zo:zo + n], w=cw[:, l, j, 2:3]:
                          e.activation(out=ys, in_=a, func=AF.Identity, scale=w),
                          reads=[Z, "convw"], writes=[("ysb", q)])
                    S.add("dve", lambda e, ys=ys, a=zt[:, zq, zo - 1:zo - 1 + n], w=cw[:, l, j, 1:2]:
                          e.scalar_tensor_tensor(out=ys, in0=a, scalar=w, in1=ys, op0=ALU.mult, op1=ALU.add),
                          reads=[Z, ("ysb", q), "convw"], writes=[("ysb", q)])
                    S.add("dve", lambda e, ys=ys, a=zt[:, zq, zo - 2:zo - 2 + n], w=cw[:, l, j, 0:1]:
                          e.scalar_tensor_tensor(out=ys, in0=a, scalar=w, in1=ys, op0=ALU.mult, op1=ALU.add),
                          reads=[Z, ("ysb", q), "convw"], writes=[("ysb", q)])
                    S.add("dve", lambda e, o=t.uT[:, j, c0:c0 + n], ys=ys, pb_=pb_:
                          e.tensor_tensor(out=o, in0=ys, in1=pb_, op=ALU.mult),
                          reads=[("ysb", q), ("ps", bk[0])], writes=[(t.kp + "uT", cbi)])
                for si, (sc0, L_, Lr) in enumerate(t.segs):
                    zoff = si * (L + 2)
                    src = self.zt[:, zq, zoff + Lr:zoff + Lr + 2]
                    dst = zpre[:, l, si, j, :]
                    if t.kind == "halo":
                        S.add("dve", lambda e, o=dst, a=src: e.tensor_scalar(
                            out=o, in0=a, scalar1=self.hvt[:, 0:1], scalar2=None, op0=ALU.mult),
                            reads=[Z, "hvt"], writes=[(zk, l)])
                    else:
                        S.add("dve", lambda e, o=dst, a=src: e.tensor_copy(out=o, in_=a),
                              reads=[Z], writes=[(zk, l)])
        self.proj_tok_resid(tiles, "uT", 8, lambda half, kh: ("wout", l, half, kh), 0)

    def mlp(self, tiles, l):
        self.S.tag = "%s%d.L%d.mlp_up" % (tiles[0].kind, tiles[0].idx, l)
        S = self.S
        self.fence()
        for jp in range(8):
            s = self.wget(("wup", l, jp))
            wv = self.wview(s, "k512")
            for jj in range(4):
                j = jp * 4 + jj
                for t in tiles:
                    for cbi, (c0, n, si, off) in enumerate(t.cbs):
                        b = self.P8.next()
                        xr = xt_keys(t, groups_of(t, c0, n))
                        for k in range(8):
                            self.mm(self.ps[:, b, 0:n], wv[:, k, jj * 128:(jj + 1) * 128], t.xT[:, k, c0:c0 + n],
                                    k == 0, k == 7, reads=[("w", s)] + xr, writes=[("ps", b)])
                        q = self.nxt()
                        sq = self.sq[:, q, 0:n]
                        p = self.ps[:, b, 0:n]
                        S.add("act", lambda e, sq=sq, p=p: e.activation(out=sq, in_=p, func=AF.Square),
                              reads=[("ps", b)], writes=[("sq", q)])
                        S.add("dve", lambda e, o=t.hT[:, j, c0:c0 + n], sq=sq, p=p:
                              e.scalar_tensor_tensor(out=o, in0=p, scalar=0.0, in1=sq, op0=ALU.is_gt, op1=ALU.mult),
                              reads=[("ps", b), ("sq", q), "ALIAS"], writes=[(t.kp + "hT", cbi)])
        self.proj_tok_resid(tiles, "hT", 32, lambda half, kh: ("wdown", l, half, kh), 1, alias=True)

    def proj_feat(self, t, wkey, dst, dcol_of, cbs, scale, dkeyname, stage=None, alias=False):
        self.proj_feat_multi(wkey, [(t, dst, dcol_of, cbs, dkeyname, stage)], scale, alias)

    def proj_feat_multi(self, wkey, specs, scale, alias=False):
        S = self.S
        s = self.wget(wkey)
        wv = self.wview(s, "full")
        ar = ["ALIAS"] if alias else []
        for c in range(8):
            for (t, dst, dcol_of, cbs, dkeyname, stage) in specs:
                for (c0, n) in cbs:
                    b = self.P8.next()
                    xr = xt_keys(t, groups_of(t, c0, n))
                    for k in range(8):
                        self.mm(self.ps[:, b, 0:n], wv[:, k, c // 4, (c % 4) * 128:(c % 4 + 1) * 128],
                                t.xT[:, k, c0:c0 + n], k == 0, k == 7, reads=[("w", s)] + xr, writes=[("ps", b)])
                    dc = dcol_of(c0)
                    p = self.ps[:, b, 0:n]
                    S.add("act", lambda e, o=dst[:, c, dc:dc + n], p=p: e.activation(out=o, in_=p, func=AF.Copy, scale=scale),
                          reads=[("ps", b)] + ar, writes=[(dkeyname, c)])
                    if stage is not None and self.cfg.get("kstage", True):
                        q = 0
                        S.add("dve", lambda e, o=self.kst[:, q, 0:n], p=p: e.tensor_copy(out=o, in_=p),
                              reads=[("ps", b), (dkeyname, c)], writes=[("csb", q)])
                        oc = dc - stage[1]
                        S.add("sp", lambda e, o=stage[0][:, c, oc:oc + n], a=self.kst[:, q, 0:n]: [e.dma_start(out=o, in_=a)],
                              reads=[("csb", q)], writes=[], dma=self.dkey(("csb", q)))

    def proj_kv(self, tiles, out_kv):
        self.S.tag = "%s%d.kv" % (tiles[0].kind, tiles[0].idx)
        S = self.S
        d = self.d
        if not self.cfg.get("kvout", True):
            out_kv = False
        specs = []
        for t in tiles:
            if t.kind == "sample":
                specs.append((t, self.KTs, lambda c0: c0, t.kcbs, "KTs", (d["ksT"], 0)))
            else:
                base = t.par * 512 - t.kcbs[0][0]
                specs.append((t, self.KT, lambda c0, base=base: base + c0, t.kcbs, "KTc",
                              (d["kT"], t.par * 512) if out_kv else None))
        self.proj_feat_multi(("wk", 0), specs, 1.0)
        for half in range(2 if self.cfg.get("vproj", True) else 0):
            s = self.wget(("wv", 0, half, 0))
            wv = self.wview(s, "k512")
            for t, n_, gi in [(t, n_, gi) for t in tiles for n_, gi in enumerate(t.kgroups)]:
                c0, sz = t.groups[gi]
                b = self.P.next()
                for k in range(8):
                    self.mm(self.ps[:sz, b, :], t.xT[:, k, c0:c0 + sz], wv[:, k, :], k == 0, k == 7,
                            reads=[("w", s), (t.kp + "xT", gi), (t.kp + "xTd", gi)], writes=[("ps", b)])
                pv = self.ps[:sz, b, :].rearrange("p (h e) -> p h e", e=64)
                if t.kind == "sample":
                    dst = self.Vs[:sz, half * 8:(half + 1) * 8, 0:64]
                    vk = ("Vs",)
                else:
                    slot = t.par * 4 + n_
                    dst = self.Vb[:sz, slot, half * 8:(half + 1) * 8, 0:64]
                    vk = ("V", slot)
                if t.kind == "halo":
                    S.add("act", lambda e, o=dst, a=pv, sz=sz: e.activation(out=o, in_=a, func=AF.Identity, scale=self.hvt[:sz, 0:1]),
                          reads=[("ps", b), "hvt"], writes=[vk])
                else:
                    S.add("act", lambda e, o=dst, a=pv: e.activation(out=o, in_=a, func=AF.Copy),
                          reads=[("ps", b)], writes=[vk])
                if half == 0:
                    if t.kind == "sample":
                        S.add("dve", lambda e, o=self.Vs[:sz, :, 64:65]: e.memset(o, 1.0), reads=[], writes=[("Vs1",)])
                    elif t.kind == "halo":
                        S.add("dve", lambda e, o=self.Vb[:sz, slot, :, 64:65], sz=sz:
                              e.tensor_scalar(out=o, in0=self.ones16[:sz, :].unsqueeze(2), scalar1=self.hvt[:sz, 0:1],
                                              scalar2=None, op0=ALU.mult),
                              reads=["hvt", "ones16"], writes=[("V1", slot)])
                    else:
                        S.add("dve", lambda e, o=self.Vb[:sz, slot, :, 64:65]: e.memset(o, 1.0), reads=[], writes=[("V1", slot)])
                if (out_kv or t.kind == "sample") and self.cfg.get("vstage", True):
                    q = 0
                    S.add("dve", lambda e, o=self.vst[:sz, q, :], p=self.ps[:sz, b, :]: e.tensor_copy(out=o, in_=p),
                          reads=[("ps", b), vk], writes=[("ysb", q)])
                    od = d["vs"] if t.kind == "sample" else d["v"]
                    r0 = c0
                    S.add("sp", lambda e, o=od[r0:r0 + sz, half * 512:(half + 1) * 512], a=self.vst[:sz, q, :]:
                          [e.dma_start(out=o, in_=a)], reads=[("ysb", q)], writes=[], dma=self.dkey(("ysb", q)))

    def slot_of(self, t, kg):
        return t.par * 4 + kg if kg >= 0 else (1 - t.par) * 4 + kg + 4

    def att_finish(self, t, gi, sz):
        S = self.S
        for bi, (h0, nh) in enumerate(((0, 7), (7, 7), (14, 2))):
            ov = self.ps[:sz, 4 + bi, 0:nh * 65].rearrange("p (h e) -> p h e", e=65)
            S.add("dve", lambda e, o=self.rden[:sz, h0:h0 + nh], a=ov[:, :, 64]: e.reciprocal(o, a),
                  reads=[("ps", 4 + bi)], writes=[("rden", bi)])
            S.add("dve", lambda e, o=self.On[:sz, h0 * 64:(h0 + nh) * 64].rearrange("p (h e) -> p h e", e=64),
                  a=ov[:, :, 0:64], r=self.rden[:sz, h0:h0 + nh].unsqueeze(2).broadcast_to([sz, nh, 64]):
                  e.tensor_tensor(out=o, in0=a, in1=r, op=ALU.mult),
                  reads=[("ps", 4 + bi), ("rden", bi), "ALIAS"], writes=[("On", c) for c in range(8)])
        c0 = t.groups[gi][0]
        for c4 in range(2):
            bank = 7
            for i in range(4):
                c = c4 * 4 + i
                S.add("pe", lambda e, o=self.ps[:, bank, i * 128:i * 128 + sz],
                      a=self.On[:sz, c * 128:(c + 1) * 128], idn=self.ident[:sz, :sz]: e.transpose(o, a, idn),
                      reads=[("On", c) for c in range(8)] + ["ident", "ALIAS"], writes=[("ps", bank)])
            pv = self.ps[:, bank, :].rearrange("p (a b) -> p a b", b=128)[:, :, 0:sz]
            S.add("act", lambda e, o=t.uT[:, c4 * 4:(c4 + 1) * 4, c0:c0 + sz], a=pv: e.activation(out=o, in_=a, func=AF.Copy),
                  reads=[("ps", bank)], writes=[(t.kp + "uT", 0), (t.kp + "uT", 1)])

    def attention(self, t, l):
        self.S.tag = "%s%d.L%d.att" % (t.kind, t.idx, l)
        S = self.S
        lb = l - 2
        self.fence()
        S.add("sp", lambda e: [e.dma_start(out=self.BV[:, 0:8, :], in_=self.d["bv"][lb, :, 0:8, :]),
                               e.dma_start(out=self.BV[:, 8:16, :], in_=self.d["bv"][lb, :, 8:16, :])],
              reads=["ALIAS"], writes=["BV"], dma=self.dkey("BV"))
        self.proj_feat(t, ("wq", lb), self.QT, lambda c0: c0, [(c0, n) for (c0, n, _, _) in t.cbs], 0.125, "QT", alias=True)
        steps = [(m, h) for m in range(4) for h in range(16)]
        LA = 2

        def qk(i):
            m, h = steps[i]
            q = i % 3
            c, pb = h // 2, (h % 2) * 64
            A, B = 2 * q, 2 * q + 1
            for dd in range(5):
                slot = self.slot_of(t, m - dd)
                o = self.ps[:, A, dd * 128:(dd + 1) * 128] if dd < 2 else self.ps[:, B, (dd - 2) * 128:(dd - 1) * 128]
                bk = A if dd < 2 else B
                self.mm(o, self.KT[pb:pb + 64, c, slot * 128:(slot + 1) * 128],
                        self.QT[pb:pb + 64, c, m * 128:(m + 1) * 128], True, True,
                        reads=[("KTc", c), ("QT", c), "ALIAS"], writes=[("ps", bk)])
            S.add("dve", lambda e, o=self.T1[:, q, :], a=self.ps[:, A, 0:256], bvv=self.BV[:, h, :]:
                  e.tensor_tensor(out=o, in0=a, in1=bvv, op=ALU.add),
                  reads=[("ps", A), "BV", "ALIAS"], writes=[("T1", q)])
            S.add("act", lambda e, o=self.ET[:, q, 0:256], a=self.T1[:, q, :]: e.activation(out=o, in_=a, func=AF.Exp),
                  reads=[("T1", q), "ALIAS"], writes=[("ET", q)])
            S.add("act", lambda e, o=self.ET[:, q, 256:640], a=self.ps[:, B, 0:384], bb=self.chi[:, lb, h:h + 1]:
                  e.activation(out=o, in_=a, func=AF.Exp, bias=bb, scale=1.0),
                  reads=[("ps", B), "chi", "ALIAS"], writes=[("ET2", q)])

        def pv(i):
            m, h = steps[i]
            q = i % 3
            r = i % 2
            ob = 6 + r
            for dd in range(4):
                slot = self.slot_of(t, m - dd)
                self.mm(self.ps[:, ob, 0:65], self.ET[:, q, dd * 128:(dd + 1) * 128],
                        self.Vb[:, slot, h, 0:65], dd == 0, False,
                        reads=[("ET", q), ("ET2", q), ("V", slot), ("V1", slot), "ALIAS"], writes=[("ps", ob)])
            slot = self.slot_of(t, m - 4)
            rd = [("ET", q), ("ET2", q), ("V", slot), ("V1", slot), "ALIAS"]
            self.mm(self.ps[0:64, ob, 0:65], self.ET[0:64, q, 512:576], self.Vb[0:64, slot, h, 0:65], False, False,
                    reads=rd, writes=[("ps", ob)])
            self.mm(self.ps[:, ob, 0:65], self.ET[64:128, q, 512:640], self.Vb[64:128, slot, h, 0:65], False, True,
                    reads=rd, writes=[("ps", ob)])
            S.add("dve", lambda e, o=self.rd2[:, r:r + 1], a=self.ps[:, ob, 64:65]: e.reciprocal(o, a),
                  reads=[("ps", ob)], writes=[("rd2", r)])
            S.add("dve", lambda e, o=self.Onb[:, h * 64:(h + 1) * 64], a=self.ps[:, ob, 0:64], rr=self.rd2[:, r:r + 1]:
                  e.tensor_scalar(out=o, in0=a, scalar1=rr, scalar2=None, op0=ALU.mult),
                  reads=[("ps", ob), ("rd2", r), "ALIAS"], writes=[("On", h // 2)])

        def fin(m):
            for c4 in range(2):
                bank = self.PT.next()
                psb = self.ps[:, bank, :].bitcast(BF16)
                for i in range(4):
                    c = c4 * 4 + i
                    S.add("pe", lambda e, o=psb[:, i * 128:(i + 1) * 128],
                          a=self.Onb[:, c * 128:(c + 1) * 128], idn=self.identb[:, :]: e.transpose(o, a, idn),
                          reads=[("On", c), "identb", "ALIAS"], writes=[("ps", bank)])
                pvw = psb[:, 0:512].rearrange("p (a b) -> p a b", b=128)
                S.add("act", lambda e, o=t.uT[:, c4 * 4:(c4 + 1) * 4, m * 128:(m + 1) * 128], a=pvw:
                      e.activation(out=o, in_=a, func=AF.Copy),
                      reads=[("ps", bank)], writes=[(t.kp + "uT", 0), (t.kp + "uT", 1)])
        for i in range(LA):
            qk(i)
        for i in range(len(steps)):
            if i + LA < len(steps):
                qk(i + LA)
            pv(i)
            if steps[i][1] == 15:
                fin(steps[i][0])
        self.proj_tok_resid([t], "uT", 8, lambda half, kh: ("wo", lb, half, kh), 0)

    def attention_sample(self, t, l):
        self.S.tag = "%s%d.L%d.att" % (t.kind, t.idx, l)
        S = self.S
        lb = l - 2
        self.fence()
        S.add("sp", lambda e: [e.dma_start(out=self.BV[:, 0:8, :], in_=self.d["bv"][lb, :, 0:8, :]),
                               e.dma_start(out=self.BV[:, 8:16, :], in_=self.d["bv"][lb, :, 8:16, :]),
                               e.dma_start(out=self.BVs[0:64, :, :], in_=self.d["bvs"][lb])],
              reads=["ALIAS"], writes=["BV"], dma=self.dkey("BV"))
        self.proj_feat(t, ("wq", lb), self.QT, lambda c0: c0, [(0, 64)], 0.125, "QT", alias=True)
        steps = [(s, h) for s in range(2) for h in range(16)]

        def qk(s, h, q):
            c, pb = h // 2, (h % 2) * 64
            A, B = 2 * q, 2 * q + 1
            p0 = s * 32
            rq = self.QT[pb:pb + 64, c, p0:p0 + 32]
            rd = [("KTc", c), ("KTs", c), ("QT", c), "ALIAS"]
            self.mm(self.ps[p0:p0 + 16, A, 0:32], self.KTs[pb:pb + 64, c, p0:p0 + 16], rq, True, True, rd, [("ps", A)])
            sl3 = s * 4 + 3
            self.mm(self.ps[:, A, 32:64], self.KT[pb:pb + 64, c, sl3 * 128:(sl3 + 1) * 128], rq, True, True, rd, [("ps", A)])
            for kg in range(3):
                sl = s * 4 + kg
                self.mm(self.ps[:, B, kg * 32:(kg + 1) * 32], self.KT[pb:pb + 64, c, sl * 128:(sl + 1) * 128], rq,
                        True, True, rd, [("ps", B)])
            S.add("dve", lambda e, o=self.T1[p0:p0 + 16, q, 0:32], a=self.ps[p0:p0 + 16, A, 0:32], bvv=self.BVs[p0:p0 + 16, h, :]:
                  e.tensor_tensor(out=o, in0=a, in1=bvv, op=ALU.add),
                  reads=[("ps", A), "BV", "ALIAS"], writes=[("T1", q)])
            S.add("dve", lambda e, o=self.T1[:, q, 32:64], a=self.ps[:, A, 32:64], bvv=self.BV[:, h, 128:160]:
                  e.tensor_tensor(out=o, in0=a, in1=bvv, op=ALU.add),
                  reads=[("ps", A), "BV", "ALIAS", ("T1", q)], writes=[("T1", q)])
            S.add("act", lambda e, o=self.ET[p0:p0 + 16, q, 0:32], a=self.T1[p0:p0 + 16, q, 0:32]: e.activation(out=o, in_=a, func=AF.Exp),
                  reads=[("T1", q), "ALIAS"], writes=[("ET", q)])
            S.add("act", lambda e, o=self.ET[:, q, 32:64], a=self.T1[:, q, 32:64]: e.activation(out=o, in_=a, func=AF.Exp),
                  reads=[("T1", q), "ALIAS", ("ET", q)], writes=[("ET", q)])
            S.add("act", lambda e, o=self.ET[:, q, 64:160], a=self.ps[:, B, 0:96], bb=self.chi[:, lb, h:h + 1]:
                  e.activation(out=o, in_=a, func=AF.Exp, bias=bb, scale=1.0),
                  reads=[("ps", B), "chi", "ALIAS"], writes=[("ET2", q)])

        def pv(s, h, q):
            ob, oc = 4 + h // 7, (h % 7) * 65
            p0 = s * 32
            o = self.ps[p0:p0 + 32, ob, oc:oc + 65]
            rd = [("ET", q), ("ET2", q), ("Vs",), ("Vs1",), "ALIAS"] + [("V", s * 4 + k) for k in range(4)]
            self.mm(o, self.ET[p0:p0 + 16, q, 0:32], self.Vs[p0:p0 + 16, h, 0:65], True, False, rd, [("ps", ob)])
            self.mm(o, self.ET[:, q, 32:64], self.Vb[:, s * 4 + 3, h, 0:65], False, False, rd, [("ps", ob)])
            for kg in range(3):
                self.mm(o, self.ET[:, q, 64 + kg * 32:64 + (kg + 1) * 32], self.Vb[:, s * 4 + kg, h, 0:65], False, kg == 2,
                        rd, [("ps", ob)])
        qk(0, 0, 0)
        for i, (s, h) in enumerate(steps):
            if i + 1 < len(steps):
                qk(steps[i + 1][0], steps[i + 1][1], (i + 1) % 2)
            pv(s, h, i % 2)
        self.att_finish(t, 0, 64)
        self.proj_tok_resid([t], "uT", 8, lambda half, kh: ("wo", lb, half, kh), 0)

    def prologue(self):
        S = self.S
        d = self.d
        S.add("dve", lambda e: e.memset(self.epst[:, :], EPS), reads=[], writes=["epst"])
        S.add("dve", lambda e: e.memset(self.ones16[:, :], 1.0), reads=[], writes=["ones16"])
        S.add("dve", lambda e: e.memset(self.zpre_p[:, :, :, :, :].rearrange("p a b c d -> p (a b c d)"), 0.0),
              reads=[], writes=[("zpre_p", 0), ("zpre_p", 1)])
        for nm, tl in (("convw", self.convw), ("chi", self.chi), ("ident", self.ident), ("hv", self.hvt), ("lnT", self.lnT)):
            S.add("sp", lambda e, o=tl, a=d[nm]: [e.dma_start(out=o[tuple(slice(None) for _ in o.shape)], in_=a)],
                  reads=[], writes=[nm if nm != "hv" else "hvt"], dma=self.dkey(("c", nm)))

    def prologue2(self):
        self.S.add("dve", lambda e: e.tensor_copy(out=self.identb[:, :], in_=self.ident[:, :]), reads=["ident"], writes=["identb"])

    def bind(self, t):
        if t.kind == "sample":
            t.xres, t.xT, t.uT, t.hT, t.kp, t.lnoff = self.s_xres, self.s_xT, self.s_uT, self.s_hT, "s_", 5
        else:
            t.xres, t.xT, t.uT, t.hT, t.kp, t.lnoff = self.m_xres, self.m_xT, self.m_uT, self.m_hT, "", 0
        return t

    def run_A(self, tiles):
        for t in tiles:
            self.load_x(t)
        for l in range(2):
            self.cur_l = l
            self.load_lnp(l, 0)
            self.mixer_A(tiles, l)
            self.load_lnp(l, 1)
            self.mlp(tiles, l)
        self.cur_l = 2
        self.proj_kv(tiles, False)

    def run_tile(self, t, out_kv=False):
        S = self.S
        cfg = self.cfg
        self.load_x(t)
        nl = cfg.get("nlayers", 4)
        for l in range(nl):
            self.cur_l = l
            self.load_lnp(l, 0)
            if l < 2:
                if cfg.get("mixer", True):
                    self.mixer_A([t], l)
            else:
                if l == 2:
                    self.proj_kv([t], out_kv)
                if not cfg.get("att", True):
                    pass
                else:
                    self.attention(t, l)
            if cfg.get("mlp", True):
                self.load_lnp(l, 1)
                self.mlp([t], l)
        self.store_y(t)

    def store_y(self, t):
        self.flush_pool()
        S = self.S
        od = self.d["ys"] if t.kind == "sample" else self.d["y"]
        r0 = 0 if t.kind == "sample" else t.idx * 512
        for gi, (c0, sz) in enumerate(t.groups):
            S.add("sp", lambda e, o=od[r0 + c0:r0 + c0 + sz, :], a=t.xres[:sz, gi, :]: [e.dma_start(out=o, in_=a)],
                  reads=[(t.kp + "xres", gi)], writes=[], dma=self.dkey((t.kp + "y", gi)))

    def emit_all(self):
        S = self.S
        d = self.d
        cfg = self.cfg
        self.prologue()
        self.prologue2()
        nm = cfg.get("nmain", NMAIN)
        halo = self.bind(mk_tile("halo"))
        samp = self.bind(mk_tile("sample"))
        do_s = cfg.get("sample", True)
        if do_s:
            S.add("sp", lambda e: [e.dma_start(out=self.zpre_s[:, :, :, :, :], in_=d["cconv"])],
                  reads=[], writes=[("zpre_s", 0), ("zpre_s", 1)], dma=self.dkey("cconv"))
        self.run_A([halo, samp] if do_s else [halo])
        if do_s:
            S.add("sp", lambda e: [e.dma_start(out=d["convs"], in_=self.zpre_s[:, :, :, :, :])],
                  reads=[("zpre_s", 0), ("zpre_s", 1)], writes=[], dma=self.dkey("convs"))
        for i in range(nm):
            t = self.bind(mk_tile("main", i))
            self.run_tile(t, out_kv=(i == nm - 1))
        S.add("sp", lambda e: [e.dma_start(out=d["convp"], in_=self.zpre_p[:, :, :, :, :])],
              reads=[("zpre_p", 0), ("zpre_p", 1)], writes=[], dma=self.dkey("convp"))
        if do_s:
            t = samp
            S.tag = "sample0.cache"
            S.add("pool", lambda e: [e.dma_start(out=self.KT[:, c, :], in_=d["ckT"][:, c, :]) for c in range(8)],
                  reads=[], writes=[("KTc", c) for c in range(8)], dma=self.dkey("ckT"))
            S.add("pool", lambda e: [e.dma_start(out=self.Vb[:, sl, :, 0:64],
                                                 in_=d["cv"][:, sl, :].rearrange("p (h e) -> p h e", e=64))
                                     for sl in range(8)],
                  reads=[], writes=[("V", sl) for sl in range(8)], dma=self.dkey("cv"))
            S.add("dve", lambda e: e.memset(self.Vb[:, :, :, 64:65].rearrange("p a b c -> p (a b) c"), 1.0),
                  reads=[], writes=[("V1", sl) for sl in range(8)])
            for l in (2, 3):
                self.cur_l = l
                self.load_lnp(l, 0)
                self.attention_sample(t, l)
                self.load_lnp(l, 1)
                self.mlp([t], l)
            self.store_y(t)

    def build(self):
        nc = self.nc
        self.declare_dram()
        es = ExitStack()
        self.es = es
        with es:
            self.alloc(es)
            self.emit_all()
            plan = self.wrec
            self.S = Sched()
            self.P = PsumRR(range(6))
            self.PT = PsumRR((6, 7))
            self.P8 = PsumRR(range(8))
            self.cnt = 0
            self.wplan = plan
            self.wscr = {}
            self.wuses = {}
            for k in plan:
                self.wuses[k] = self.wuses.get(k, 0) + 1
            self.wcur = 0
            self.wissued = 0
            self.dma_keys = {}
            self.emit_all()
            S = self.S
            S.finalize()
            ops = S.ops
            self._esem = {e: es.enter_context(nc.semaphore("sem_" + e)) for e in ("pe", "act", "dve", "pool")}
            self._dsem = {k: es.enter_context(nc.semaphore("dsem%d" % i)) for i, k in enumerate(self.dma_keys)}
            per_eng = {e: [] for e in Sched.ENGS}
            for i, op in enumerate(ops):
                per_eng[op.eng].append(i)
            self._ops = ops
            with nc.Block() as block:
                self._emit_block(block, per_eng)
        return nc

    def _emit_block(self, block, per_eng):
        ops = self._ops
        esem, dsem = self._esem, self._dsem
        class Fake:
            def dma_start(self, **kw):
                return 1
        fake = Fake()
        cnt = {e: 0 for e in esem}
        dcnt = {k: 0 for k in dsem}
        for op in ops:
            if op.dma is not None:
                n = len(op.fn(fake))
                dcnt[op.dma] += 16 * n
                op.ev = (dsem[op.dma], dcnt[op.dma])
            elif op.need:
                cnt[op.eng] += 1
                op.ev = (esem[op.eng], cnt[op.eng])
        final_d = dict(dcnt)

        def run(engname, e):
            waited = {}
            for i in per_eng[engname]:
                op = ops[i]
                for dd in op.rdeps:
                    sem, val = ops[dd].ev
                    key = id(sem)
                    if waited.get(key, 0) < val:
                        e.wait_ge(sem, val)
                        waited[key] = val
                if op.dma is not None:
                    for ins in op.fn(e):
                        ins.then_inc(op.ev[0], 16)
                else:
                    ins = op.fn(e)
                    if op.need:
                        ins.then_inc(op.ev[0], 1)
            if engname == "sp":
                for k, v in final_d.items():
                    if v > 0 and waited.get(id(dsem[k]), 0) < v:
                        e.wait_ge(dsem[k], v)

        @block.tensor
        def _(e):
            run("pe", e)

        @block.scalar
        def _(e):
            run("act", e)

        @block.vector
        def _(e):
            run("dve", e)

        @block.gpsimd
        def _(e):
            run("pool", e)

        @block.sync
        def _(e):
            run("sp", e)


def _bias_tables(rel_bias_b):
    nb = rel_bias_b.shape[0]
    jj = np.arange(128)[:, None]
    ii = np.arange(256)[None, :]
    idx = np.clip(ii - jj, -128, 128) + 128
    bv = np.empty((nb, 128, 16, 256), np.float32)
    for l in range(nb):
        tb = rel_bias_b[l]
        g = tb[idx]
        bv[l] = np.transpose(g, (0, 2, 1))
    bv[:, 64:128, :, 0:64] = NEG
    j2 = (np.arange(64) % 32)[:, None]
    i2 = np.arange(32)[None, :]
    idx2 = np.clip(i2 - j2, -128, 128) + 128
    bvs = np.empty((nb, 64, 16, 32), np.float32)
    for l in range(nb):
        bvs[l] = np.transpose(rel_bias_b[l][idx2], (0, 2, 1))
    chi = np.broadcast_to(np.transpose(rel_bias_b[:, 256, :], (0, 1))[None], (128, nb, 16)).astype(np.float32)
    return bv, bvs, np.ascontiguousarray(chi)


_NC_CACHE = {}


def _get_nc(cfg_key=()):
    if cfg_key not in _NC_CACHE:
        b = Builder(dict(cfg_key))
        _NC_CACHE[cfg_key] = b.build()
    return _NC_CACHE[cfg_key]


def kernel(x_prompt, x_sample, cache_conv, cache_k, cache_v, ln_mix_g, ln_mix_b, ln_ffn_g,
           ln_ffn_b, w_up, w_down, w_in_a, conv_w_a, w_out_a, w_k, w_v, w_q_b, w_o_b, rel_bias_b, _cfg=()):
    f = lambda a: np.ascontiguousarray(np.asarray(a, dtype=np.float32))
    x_prompt, x_sample, cache_conv, cache_k, cache_v = map(f, (x_prompt, x_sample, cache_conv, cache_k, cache_v))
    rel_bias_b = f(rel_bias_b)
    xp = x_prompt[0]
    lnp = f(np.stack([ln_mix_g, ln_mix_b, ln_ffn_g, ln_ffn_b], axis=1))
    convw = f(np.transpose(f(conv_w_a).reshape(2, 3, 8, 128), (3, 0, 2, 1)))
    lnT = f(np.transpose(lnp.reshape(4, 4, 8, 128), (3, 0, 1, 2)))
    bv, bvs, chi = _bias_tables(rel_bias_b)
    ident = np.eye(128, dtype=np.float32)
    shared = dict(lnp=lnp, lnT=lnT, convw=convw, bv=bv, bvs=bvs, chi=chi, ident=ident,
                  w_up=f(w_up), w_down=f(w_down), w_in_a=f(w_in_a), w_out_a=f(w_out_a),
                  w_k=f(w_k), w_v=f(w_v), w_q_b=f(w_q_b), w_o_b=f(w_o_b))
    in_maps = []
    for c in range(NCORES):
        s = c * TPC
        xh = np.zeros((XROWS, D), np.float32)
        lo = s - HALO
        a = max(lo, 0)
        xh[a - lo:] = xp[a:s + TPC]
        xs = np.zeros((64, D), np.float32)
        xs[0:16] = x_sample[2 * c]
        xs[32:48] = x_sample[2 * c + 1]
        hv = np.full((128, 1), 0.0 if c == 0 else 1.0, np.float32)
        cc = cache_conv[:, 2 * c:2 * c + 2].reshape(2, 2, 2, 8, 128)
        cconv = f(np.transpose(cc, (4, 0, 1, 3, 2)))
        ck = cache_k[2 * c:2 * c + 2].reshape(2, 512, 8, 128)
        ckT = f(np.transpose(ck, (3, 2, 0, 1)).reshape(128, 8, 1024))
        cvv = cache_v[2 * c:2 * c + 2].reshape(2, 4, 128, 1024)
        cv = f(np.transpose(cvv, (2, 0, 1, 3)).reshape(128, 8, 1024))
        m = dict(shared)
        m.update(xh=xh, xs=xs, hv=hv, cconv=cconv, ckT=ckT, cv=cv)
        in_maps.append(m)
    nc = _get_nc(_cfg)
    ncr = dict(_cfg).get("ncores", NCORES)
    res = run_bass_kernel_spmd(nc, in_maps[:ncr], core_ids=list(range(ncr)))
    R = list(res.results)
    while len(R) < NCORES:
        R.append(R[0])
    y_prompt = np.concatenate([R[c]["y"] for c in range(NCORES)], axis=0)[None]
    y_sample = np.empty((16, 16, D), np.float32)
    conv_sample = np.empty((2, 16, 2, D), np.float32)
    k_sample = np.empty((16, 16, 16, 64), np.float32)
    v_sample = np.empty((16, 16, 16, 64), np.float32)
    for c in range(NCORES):
        r = R[c]
        for sg in range(2):
            b = 2 * c + sg
            y_sample[b] = r["ys"][sg * 32:sg * 32 + 16]
            v_sample[b] = r["vs"][sg * 32:sg * 32 + 16].reshape(16, 16, 64)
            kk = r["ksT"][:, :, sg * 32:sg * 32 + 16]
            k_sample[b] = np.transpose(kk, (2, 1, 0)).reshape(16, 16, 64)
            cs = r["convs"][:, :, sg]
            conv_sample[:, b] = np.transpose(cs, (1, 3, 2, 0)).reshape(2, 2, D)
    last = R[NCORES - 1]
    cp = last["convp"][:, :, 0]
    conv_prompt = np.ascontiguousarray(np.transpose(cp, (1, 3, 2, 0)).reshape(2, 1, 2, D))
    k_prompt = np.ascontiguousarray(np.transpose(last["kT"], (2, 1, 0)).reshape(1, 512, 16, 64))
    v_prompt = np.ascontiguousarray(last["v"].reshape(1, 512, 16, 64))
    return (y_prompt, y_sample, conv_prompt, k_prompt, v_prompt, conv_sample, k_sample, v_sample)
```

```python
import numpy as np
from contextlib import ExitStack
import concourse.bass as bass
import concourse.mybir as mybir
from concourse.bass_utils import run_bass_kernel_spmd

F32 = mybir.dt.float32
BF16 = mybir.dt.bfloat16
AF = mybir.ActivationFunctionType
ALU = mybir.AluOpType

D = 1024
NCORES = 8
TPC = 2048
NMAIN = 4
HALO = 516
XROWS = HALO + TPC
ALPHA = 8.0 ** 0.25
EPS = 1e-5
NEG = -30000.0
NS = 3
SLOT = 8192
WA = 520


class Op:
    __slots__ = ("eng", "fn", "deps", "dma", "rdeps", "ev", "need", "tag")

    def __init__(self, eng, fn, deps, dma):
        self.eng, self.fn, self.deps, self.dma = eng, fn, deps, dma
        self.rdeps, self.ev, self.need = [], None, False


class Sched:
    ENGS = ("pe", "act", "dve", "pool", "sp")

    def __init__(self):
        self.ops = []
        self.lastw = {}
        self.readers = {}
        self.tag = ""

    def add(self, eng, fn, reads=(), writes=(), dma=None):
        i = len(self.ops)
        deps = set()
        for r in reads:
            w = self.lastw.get(r)
            if w is not None:
                deps.add(w)
        for r in writes:
            w = self.lastw.get(r)
            if w is not None:
                deps.add(w)
            for j in self.readers.get(r, {}).values():
                deps.add(j)
        chan = ("dma", dma) if dma is not None else eng
        for r in reads:
            self.readers.setdefault(r, {})[chan] = i
        for r in writes:
            self.lastw[r] = i
            self.readers[r] = {}
        deps.discard(i)
        op = Op(eng, fn, deps, dma)
        op.tag = self.tag
        self.ops.append(op)
        return i

    def chan(self, op):
        return ("dma", op.dma) if op.dma is not None else op.eng

    def finalize(self):
        ops = self.ops
        for op in ops:
            by = {}
            for d in op.deps:
                c = self.chan(ops[d])
                if c == "pe" and op.eng == "pe" and op.dma is None:
                    continue
                if by.get(c, -1) < d:
                    by[c] = d
            op.rdeps = sorted(by.values())
            for d in op.rdeps:
                ops[d].need = True


class PsumRR:
    def __init__(self, banks):
        self.banks = list(banks)
        self.i = 0

    def next(self):
        b = self.banks[self.i % len(self.banks)]
        self.i += 1
        return b


class Tile:
    pass


def mk_tile(kind, idx=0):
    t = Tile()
    t.kind = kind
    t.idx = idx
    if kind == "halo":
        t.W = 516
        t.cbs = [(0, 258, 0, 0), (258, 258, 0, 258)]
        t.groups = [(0, 4), (4, 128), (132, 128), (260, 128), (388, 128)]
        t.segs = [(0, 516, 516)]
        t.row0 = 0
        t.par = 0
        t.kcbs = [(4, 256), (260, 256)]
        t.kgroups = [1, 2, 3, 4]
    elif kind == "main":
        t.W = 512
        t.cbs = [(0, 512, 0, 0)]
        t.groups = [(0, 128), (128, 128), (256, 128), (384, 128)]
        t.segs = [(0, 512, 512)]
        t.row0 = HALO + idx * 512
        t.par = (idx + 1) % 2
        t.kcbs = [(0, 512)]
        t.kgroups = [0, 1, 2, 3]
    else:
        t.W = 64
        t.cbs = [(0, 32, 0, 0), (32, 32, 1, 0)]
        t.groups = [(0, 64)]
        t.segs = [(0, 32, 16), (32, 32, 16)]
        t.row0 = 0
        t.par = 0
        t.kcbs = [(0, 64)]
        t.kgroups = [0]
    return t


def groups_of(t, c0, n):
    return [gi for gi, (g0, sz) in enumerate(t.groups) if g0 < c0 + n and c0 < g0 + sz]


def xt_keys(t, gis):
    return [(t.kp + "xT", x) for x in gis] + [(t.kp + "xTd", x) for x in gis]


class Builder:
    def __init__(self, cfg):
        self.cfg = cfg
        self.nc = bass.Bass("TRN2", target_bir_lowering=False)
        self.S = Sched()
        self.P = PsumRR(range(6))
        self.PT = PsumRR((6, 7))
        self.P8 = PsumRR(range(8))
        self.cnt = 0
        self.wplan = None
        self.wrec = []
        self.wissued = 0
        self.wcur = 0
        self.dma_keys = {}
        self.xqmap = {}
        self.pdef = []

    def declare_dram(self):
        nc = self.nc

        def din(name, shape):
            return nc.dram_tensor(name, list(shape), F32, kind="ExternalInput").ap()

        def dout(name, shape):
            return nc.dram_tensor(name, list(shape), F32, kind="ExternalOutput").ap()

        d = {}
        d["xh"] = din("xh", [XROWS, D])
        d["xs"] = din("xs", [64, D])
        d["hv"] = din("hv", [128, 1])
        d["cconv"] = din("cconv", [128, 2, 2, 8, 2])
        d["ckT"] = din("ckT", [128, 8, 1024])
        d["cv"] = din("cv", [128, 8, 1024])
        d["lnp"] = din("lnp", [4, 4, D])
        d["lnT"] = din("lnT", [128, 4, 4, 8])
        d["convw"] = din("convw", [128, 2, 8, 3])
        d["bv"] = din("bv", [2, 128, 16, 256])
        d["bvs"] = din("bvs", [2, 64, 16, 32])
        d["chi"] = din("chi", [128, 2, 16])
        d["ident"] = din("ident", [128, 128])
        d["w_up"] = din("w_up", [4, D, 4 * D])
        d["w_down"] = din("w_down", [4, 4 * D, D])
        d["w_in_a"] = din("w_in_a", [2, D, 3 * D])
        d["w_out_a"] = din("w_out_a", [2, D, D])
        d["w_k"] = din("w_k", [D, D])
        d["w_v"] = din("w_v", [D, D])
        d["w_q_b"] = din("w_q_b", [2, D, D])
        d["w_o_b"] = din("w_o_b", [2, D, D])
        d["y"] = dout("y", [TPC, D])
        d["ys"] = dout("ys", [64, D])
        d["convp"] = dout("convp", [128, 2, 2, 8, 2])
        d["kT"] = dout("kT", [128, 8, 512])
        d["v"] = dout("v", [512, D])
        d["convs"] = dout("convs", [128, 2, 2, 8, 2])
        d["ksT"] = dout("ksT", [128, 8, 64])
        d["vs"] = dout("vs", [64, D])
        self.d = d

    def alloc(self, es):
        nc = self.nc

        def sb(name, shape, dt=F32):
            return es.enter_context(nc.sbuf_tensor("sb_" + name, list(shape), dt))

        self.m_xres = sb("xres", [128, 5, D])
        self.m_xT = sb("xT", [128, 8, WA], BF16)
        self.m_uT = sb("uT", [128, 8, WA], BF16)
        self.big = sb("big", [128, 32 * WA], BF16)
        self.m_hT = self.big[:, :].rearrange("p (j w) -> p j w", w=WA)
        o = 0
        self.QT = sb("QT", [128, 8, WA], BF16)
        self.BV = self.big[:, o:o + 16 * 256 * 2].bitcast(F32).rearrange("p (h i) -> p h i", i=256)
        o += 16 * 256 * 2
        self.ET = self.big[:, o:o + 3 * 640].rearrange("p (q i) -> p q i", i=640)
        o += 3 * 640
        self.T1 = self.big[:, o:o + 3 * 256 * 2].bitcast(F32).rearrange("p (q i) -> p q i", i=256)
        o += 3 * 256 * 2
        self.On = self.big[:, o:o + 1024 * 2].bitcast(F32)
        self.Onb = self.big[:, o:o + 1024]
        o += 1024 * 2
        assert o <= 32 * WA
        self.BVs = sb("BVs", [64, 16, 32])
        self.rd2 = sb("rd2", [128, 2])
        self.lnT = sb("lnT", [128, 4, 4, 8])
        self.zt = sb("zt", [128, 2, 524])
        self.csb = sb("csb", [128, 2, 512])
        self.ysb = sb("ysb", [128, 2, 512])
        self.kst = self.csb
        self.vst = self.ysb
        self.sq = sb("sq", [128, 2, 512], BF16)
        self.KT = sb("KT", [128, 8, 1024], BF16)
        self.Vb = sb("Vb", [128, 8, 16, 66], BF16)
        self.KTs = sb("KTs", [128, 8, 64], BF16)
        self.Vs = sb("Vs", [64, 16, 66], BF16)
        self.lnp = sb("lnp", [128, 2, D])
        self.xnb = sb("xnb", [128, 2, D], BF16)
        self.identb = sb("identb", [128, 128], BF16)
        self.st5 = sb("st5", [128, 6, 12])
        self.mv5 = sb("mv5", [128, 6, 2])
        self.sd5 = sb("sd5", [128, 6, 1])
        self.rs5 = sb("rs5", [128, 6, 1])
        self.nm5 = sb("nm5", [128, 6, 1])
        self.s_xres = sb("s_xres", [128, 1, D])
        self.s_xT = sb("s_xT", [128, 8, 64], BF16)
        self.s_uT = sb("s_uT", [128, 8, 64], BF16)
        self.s_hT = sb("s_hT", [128, 32, 64], BF16)
        self.zpre_p = sb("zpre_p", [128, 2, 2, 8, 2])
        self.zpre_s = sb("zpre_s", [128, 2, 2, 8, 2])
        self.convw = sb("convw", [128, 2, 8, 3])
        self.chi = sb("chi", [128, 2, 16])
        self.ident = sb("ident", [128, 128])
        self.hvt = sb("hvt", [128, 1])
        self.st = sb("st", [128, 2, 12])
        self.mv = sb("mv", [128, 2, 2])
        self.sd = sb("sd", [128, 2, 1])
        self.rs = sb("rs", [128, 2, 1])
        self.rden = sb("rden", [128, 16])
        self.fz = sb("fz", [128, 2])
        self.epst = sb("epst", [128, 1])
        self.ones16 = sb("ones16", [128, 16])
        self.wring = sb("wring", [128, NS, SLOT], BF16)
        self.ps = es.enter_context(nc.psum_tensor("ps", [128, 8, 512], F32))

    def nxt(self):
        self.cnt += 1
        return self.cnt % 2

    def dkey(self, key):
        self.dma_keys[key] = True
        return key

    def mm(self, out, lhsT, rhs, start, stop, reads, writes):
        self.S.add("pe", lambda e, o=out, l=lhsT, r=rhs, s=start, t=stop:
                   e.matmul(o, l, r, start=s, stop=t), reads, writes)

    def fence(self):
        self.S.add("dve", lambda e, a=self.fz[:, 0:2]: e.memset(a, 0.0), reads=[], writes=["ALIAS", "fz"])

    def wsrc(self, key):
        d = self.d
        kind = key[0]
        if kind == "win":
            _, l, jp = key
            v = d["w_in_a"][l].rearrange("(kc p) (w f) -> p kc w f", p=128, w=3)
            return [(v[:, :, wi, jp * 256:(jp + 1) * 256], (8, 256), wi * 8 * 256) for wi in range(3)], (8 * 3 * 256)
        if kind == "wup":
            _, l, jp = key
            v = d["w_up"][l].rearrange("(kc p) f -> p kc f", p=128)
            return [(v[:, :, jp * 512:(jp + 1) * 512], (8, 512), 0)], 8 * 512
        if kind == "wdown":
            _, l, half, kh = key
            v = d["w_down"][l].rearrange("(kc p) f -> p kc f", p=128)
            return [(v[:, kh * 16:(kh + 1) * 16, half * 512:(half + 1) * 512], (16, 512), 0)], 16 * 512
        if kind in ("wout", "wo", "wv"):
            _, l, half, kh = key
            src = {"wout": d["w_out_a"], "wo": d["w_o_b"]}.get(kind)
            m = d["w_v"] if kind == "wv" else src[l]
            v = m.rearrange("(kc p) f -> p kc f", p=128)
            return [(v[:, :, half * 512:(half + 1) * 512], (8, 512), 0)], 8 * 512
        if kind in ("wq", "wk"):
            _, l = key
            m = d["w_k"] if kind == "wk" else d["w_q_b"][l]
            v = m.rearrange("(kc p) f -> p kc f", p=128)
            return [(v[:, :, 0:512], (8, 512), 0), (v[:, :, 512:1024], (8, 512), 8 * 512)], 8 * 1024
        raise KeyError(key)

    def wissue(self, i):
        key = self.wplan[i]
        s = i % NS
        parts, tot = self.wsrc(key)
        if key in self.wscr:
            scr = self.wscr[key]
            self.S.add("sp", lambda e, o=self.wring[:, s, 0:tot], a=scr: [e.dma_start(out=o, in_=a)],
                       reads=[("scr", key)], writes=[("w", s)], dma=self.dkey(("w", s)))
            return
        outs = []
        for src, (a, b), off in parts:
            dst = self.wring[:, s, off:off + a * b].rearrange("p (a b) -> p a b", b=b)
            outs.append((dst, src))

        def fn(e, outs=outs):
            return [e.dma_start(out=o, in_=i_) for o, i_ in outs]
        self.S.add("pool", fn, reads=[], writes=[("w", s)], dma=self.dkey(("wc", s)))
        if self.cfg.get("scratch", True) and self.wuses.get(key, 0) > 1:
            scr = self.nc.dram_tensor("scr%d" % len(self.wscr), [128, tot], BF16, kind="Internal").ap()
            self.wscr[key] = scr
            self.S.add("sp", lambda e, o=scr, a=self.wring[:, s, 0:tot]: [e.dma_start(out=o, in_=a)],
                       reads=[("w", s)], writes=[("scr", key)], dma=self.dkey(("wb", s)))

    def flush_pool(self):
        ops, self.pdef = self.pdef, []
        for fn, rd, wr in ops:
            self.S.add("pool", fn, reads=rd, writes=wr)

    def wget(self, key, nheld=1):
        i = self.wcur
        self.wcur += 1
        if self.wplan is None:
            self.wrec.append(key)
            self.flush_pool()
            return 0
        assert self.wplan[i] == key, (self.wplan[i], key)
        while self.wissued < min(len(self.wplan), i + NS - (nheld - 1)):
            self.wissue(self.wissued)
            self.wissued += 1
        self.flush_pool()
        return i % NS

    def wview(self, s, kind):
        r = self.wring[:, s, :]
        if kind == "win":
            return r[:, 0:8 * 3 * 256].rearrange("p (w k f) -> p k w f", w=3, k=8)
        if kind == "k512":
            return r[:, 0:8 * 512].rearrange("p (k f) -> p k f", f=512)
        if kind == "k16":
            return r[:, 0:16 * 512].rearrange("p (k f) -> p k f", f=512)
        if kind == "full":
            return r[:, 0:8 * 1024].rearrange("p (h k f) -> p k h f", h=2, k=8)
        raise KeyError(kind)

    def to_feat(self, t, gi, src, dst, dkeyname, src_reads, extra_reads=(), gb=None):
        c0, sz = t.groups[gi]
        S = self.S
        for c4 in range(2):
            bank = self.PT.next()
            for i in range(4):
                c = c4 * 4 + i
                S.add("pe", lambda e, o=self.ps[:, bank, i * 128:i * 128 + sz],
                      a=src[:, c * 128:(c + 1) * 128], idn=self.ident[:sz, :sz]:
                      e.transpose(o, a, idn),
                      reads=list(src_reads) + ["ident"], writes=[("ps", bank)])
            if gb is not None:
                l_, li_ = gb
                for i in range(4):
                    c = c4 * 4 + i
                    S.add("act", lambda e, o=dst[:, c, c0:c0 + sz], a=self.ps[:, bank, i * 128:i * 128 + sz],
                          g_=self.lnT[:, l_, li_ * 2, c:c + 1], b_=self.lnT[:, l_, li_ * 2 + 1, c:c + 1]:
                          e.activation(out=o, in_=a, func=AF.Identity, bias=b_, scale=g_),
                          reads=[("ps", bank), "lnT"] + list(extra_reads), writes=[(t.kp + dkeyname, gi)])
                continue
            pv = self.ps[:, bank, :].rearrange("p (a b) -> p a b", b=128)[:, :, 0:sz]
            S.add("act", lambda e, o=dst[:, c4 * 4:(c4 + 1) * 4, c0:c0 + sz], a=pv:
                  e.activation(out=o, in_=a, func=AF.Copy),
                  reads=[("ps", bank)] + list(extra_reads), writes=[(t.kp + dkeyname, gi)])

    def load_x(self, t):
        self.S.tag = "%s%d.loadx" % (t.kind, t.idx)
        S = self.S
        src = self.d["xs"] if t.kind == "sample" else self.d["xh"]
        for gi, (c0, sz) in enumerate(t.groups):
            S.add("sp", lambda e, o=t.xres[:sz, gi, :], a=src[t.row0 + c0:t.row0 + c0 + sz, :]:
                  [e.dma_start(out=o, in_=a)],
                  reads=[], writes=[(t.kp + "xres", gi)], dma=self.dkey((t.kp + "x", gi)))
            self.to_feat(t, gi, t.xres[:sz, gi, :], t.xT, "xT", [(t.kp + "xres", gi)])

    def load_lnp(self, l, which):
        self.flush_pool()
        def fn(e, l=l, which=which):
            return [e.dma_start(out=self.lnp[:, v, :], in_=self.d["lnp"][l, which * 2 + v].partition_broadcast(128))
                    for v in range(2)]
        self.S.add("sp", fn, reads=[], writes=["lnp"], dma=self.dkey("lnp"))

    def ln(self, t, gi, lnidx):
        S = self.S
        c0, sz = t.groups[gi]
        q = self.nxt()
        xr = t.xres[:sz, gi, :]
        st, mv, sd, rs = self.st[:sz, q, :], self.mv[:sz, q, :], self.sd[:sz, q, :], self.rs[:sz, q, :]
        X = (t.kp + "xres", gi)
        S.add("dve", lambda e: e.bn_stats(st[:, 0:6], xr[:, 0:512]), reads=[X], writes=[("st", q)])
        S.add("dve", lambda e: e.bn_stats(st[:, 6:12], xr[:, 512:1024]), reads=[X, ("st", q)], writes=[("st", q)])
        S.add("dve", lambda e: e.bn_aggr(mv, st), reads=[("st", q)], writes=[("mv", q)])
        S.add("act", lambda e: e.activation(out=sd, in_=mv[:, 1:2], func=AF.Sqrt, bias=self.epst[:sz, :], scale=1.0),
              reads=[("mv", q), "epst"], writes=[("sd", q)])
        S.add("dve", lambda e: e.reciprocal(rs, sd), reads=[("sd", q)], writes=[("rs", q)])
        S.add("dve", lambda e: e.tensor_scalar(out=xr, in0=xr, scalar1=mv[:, 0:1], scalar2=rs,
                                               op0=ALU.subtract, op1=ALU.mult),
              reads=[X, ("mv", q), ("rs", q)], writes=[X])

    def ln_stats(self, t, gi, lnidx):
        S = self.S
        c0, sz = t.groups[gi]
        q = gi + t.lnoff
        xr = t.xres[:sz, gi, :]
        st, mv, sd, rs = self.st5[:sz, q, :], self.mv5[:sz, q, :], self.sd5[:sz, q, :], self.rs5[:sz, q, :]
        self.xq = getattr(self, "xq", 0) + 1
        xi = self.xq % 2
        self.xqmap[(t.kp, gi)] = xi
        xb = self.xnb[:sz, xi, :]
        X = (t.kp + "xres", gi)
        S.add("dve", lambda e: e.bn_stats(st[:, 0:6], xr[:, 0:512]), reads=[X], writes=[("st5", q)])
        S.add("dve", lambda e: e.bn_stats(st[:, 6:12], xr[:, 512:1024]), reads=[X, ("st5", q)], writes=[("st5", q)])
        S.add("dve", lambda e: e.bn_aggr(mv, st), reads=[("st5", q)], writes=[("mv5", q)])
        S.add("act", lambda e: e.activation(out=sd, in_=mv[:, 1:2], func=AF.Sqrt, bias=self.epst[:sz, :], scale=1.0),
              reads=[("mv5", q), "epst"], writes=[("sd5", q)])
        S.add("dve", lambda e: e.reciprocal(rs, sd), reads=[("sd5", q)], writes=[("rs5", q)])
        nm = self.nm5[:sz, q, :]
        S.add("dve", lambda e: e.tensor_scalar(out=nm, in0=mv[:, 0:1], scalar1=-1.0, scalar2=rs, op0=ALU.mult, op1=ALU.mult),
              reads=[("mv5", q), ("rs5", q)], writes=[("nm5", q)])
        S.add("act", lambda e: e.activation(out=xb, in_=xr, func=AF.Identity, bias=nm, scale=rs),
              reads=[X, ("nm5", q), ("rs5", q)], writes=[("xnb", xi)])
        self.pdef.append((lambda e: e.tensor_scalar(out=xr, in0=xr, scalar1=rs, scalar2=nm, op0=ALU.mult, op1=ALU.add),
                          [X, ("nm5", q), ("rs5", q)], [X]))
        self.pdef.append((lambda e: e.tensor_tensor(out=xr, in0=xr, in1=self.lnp[:sz, 0, :], op=ALU.mult),
                          [X, "lnp"], [X]))
        self.pdef.append((lambda e: e.tensor_tensor(out=xr, in0=xr, in1=self.lnp[:sz, 1, :], op=ALU.add),
                          [X, "lnp"], [X]))

    def ln_out(self, t, gi, lnidx):
        S = self.S
        c0, sz = t.groups[gi]
        l_ = self.cur_l
        xi = self.xqmap[(t.kp, gi)]
        xb = self.xnb[:sz, xi, :]
        for c4 in range(2):
            bank = self.PT.next()
            psb = self.ps[:, bank, :].bitcast(BF16)
            for i in range(4):
                c = c4 * 4 + i
                S.add("pe", lambda e, o=psb[:, i * 128:i * 128 + sz], a=xb[:, c * 128:(c + 1) * 128],
                      idn=self.identb[:sz, :sz]: e.transpose(o, a, idn),
                      reads=[("xnb", xi), "identb"], writes=[("ps", bank)])
            for i in range(4):
                c = c4 * 4 + i
                o = t.xT[:, c, c0:c0 + sz]
                a = psb[:, i * 128:i * 128 + sz]
                g_ = self.lnT[:, l_, lnidx * 2, c:c + 1]
                b_ = self.lnT[:, l_, lnidx * 2 + 1, c:c + 1]
                if c4 == 0:
                    S.add("act", lambda e, o=o, a=a, g_=g_, b_=b_: e.activation(out=o, in_=a, func=AF.Identity, bias=b_, scale=g_),
                          reads=[("ps", bank), "lnT"], writes=[(t.kp + "xT", gi)])
                else:
                    S.add("dve", lambda e, o=o, a=a, g_=g_, b_=b_: e.tensor_scalar(out=o, in0=a, scalar1=g_, scalar2=b_,
                                                                                 op0=ALU.mult, op1=ALU.add),
                          reads=[("ps", bank), "lnT"], writes=[(t.kp + "xTd", gi)])

    def ln_gb(self, t, gi, lnidx):
        S = self.S
        c0, sz = t.groups[gi]
        xr = t.xres[:sz, gi, :]
        X = (t.kp + "xres", gi)
        eng = self.cfg.get("gb_eng", "pool")
        S.add(eng, lambda e: e.tensor_tensor(out=xr, in0=xr, in1=self.lnp[:sz, 0, :], op=ALU.mult),
              reads=[X, "lnp"], writes=[X])
        S.add(eng, lambda e: e.tensor_tensor(out=xr, in0=xr, in1=self.lnp[:sz, 1, :], op=ALU.add),
              reads=[X, "lnp"], writes=[X])

    def proj_tok_resid(self, tiles, bufname, nk, wkeyf, lnidx, alias=False):
        S = self.S
        self.flush_pool()
        S.tag = S.tag.rsplit(".", 1)[0] + (".down" if nk == 32 else ".oproj")
        kper = min(nk, 16)
        nkh = nk // kper
        ar = ["ALIAS"] if alias else []
        vk = "k16" if kper == 16 else "k512"
        tg = [(t, gi) for t in tiles for gi in range(len(t.groups))]

        def resid(t, gi, half, b):
            c0, sz = t.groups[gi]
            xr = t.xres[:sz, gi, half * 512:(half + 1) * 512]
            S.add("dve", lambda e, xr=xr, p=self.ps[:sz, b, :]:
                  e.scalar_tensor_tensor(out=xr, in0=xr, scalar=ALPHA, in1=p, op0=ALU.mult, op1=ALU.add),
                  reads=[(t.kp + "xres", gi), ("ps", b)], writes=[(t.kp + "xres", gi)])

        def mms(t, gi, b, s, wv, kh):
            c0, sz = t.groups[gi]
            actT = getattr(t, bufname)
            ard = [(t.kp + bufname, x) for x in range(len(t.cbs))] + ar
            for kk in range(kper):
                k = kh * kper + kk
                self.mm(self.ps[:sz, b, :], actT[:, k, c0:c0 + sz], wv[:, kk, :], k == 0, k == nk - 1,
                        reads=[("w", s)] + ard, writes=[("ps", b)])
        pending = None
        if nkh == 1:
            ss = [self.wget(wkeyf(0, 0)), self.wget(wkeyf(1, 0), nheld=2)]
            wvs = [self.wview(s, vk) for s in ss]
            for (t, gi) in tg:
                bs = []
                for half in range(2):
                    b = self.P.next()
                    bs.append(b)
                    mms(t, gi, b, ss[half], wvs[half], 0)
                for half in range(2):
                    resid(t, gi, half, bs[half])
                self.ln_stats(t, gi, lnidx)
                if pending is not None:
                    self.ln_out(pending[0], pending[1], lnidx)
                pending = (t, gi)
            self.ln_out(pending[0], pending[1], lnidx)
            return
        banks = {}
        assert len(tg) <= 6
        for kh in range(nkh):
            s = self.wget(wkeyf(0, kh))
            wv = self.wview(s, vk)
            for i, (t, gi) in enumerate(tg):
                if kh == 0:
                    banks[i] = self.P.next()
                mms(t, gi, banks[i], s, wv, kh)
        for i, (t, gi) in enumerate(tg):
            resid(t, gi, 0, banks[i])
        ss = [self.wget(wkeyf(1, kh), nheld=kh + 1) for kh in range(nkh)]
        wvs = [self.wview(s, vk) for s in ss]
        for (t, gi) in tg:
            b = self.P.next()
            for kh in range(nkh):
                mms(t, gi, b, ss[kh], wvs[kh], kh)
            resid(t, gi, 1, b)
            self.ln_stats(t, gi, lnidx)
            if pending is not None:
                self.ln_out(pending[0], pending[1], lnidx)
            pending = (t, gi)
        self.ln_out(pending[0], pending[1], lnidx)

    def mixer_A(self, tiles, l):
        self.S.tag = "%s%d.L%d.mixer" % (tiles[0].kind, tiles[0].idx, l)
        S = self.S
        for jp in range(4):
            s = self.wget(("win", l, jp))
            wv = self.wview(s, "win")
            for jj in range(2):
              for t in tiles:
                zpre = self.zpre_s if t.kind == "sample" else self.zpre_p
                zk = "zpre_s" if t.kind == "sample" else "zpre_p"
                L = t.segs[0][1]
                j = jp * 2 + jj
                self.zcnt = getattr(self, 'zcnt', 0) + 1
                zq = self.zcnt % 2
                Z = ("zt", zq)
                for si, (sc0, L_, Lr) in enumerate(t.segs):
                    zoff = si * (L + 2)
                    S.add("dve", lambda e, o=self.zt[:, zq, zoff:zoff + 2], a=zpre[:, l, si, j, :]:
                          e.tensor_copy(out=o, in_=a), reads=[(zk, l)], writes=[Z])
                for cbi, (c0, n, si, off) in enumerate(t.cbs):
                    bk = [self.P8.next() for _ in range(3)]
                    xr = xt_keys(t, groups_of(t, c0, n))
                    for wi in range(3):
                        for k in range(8):
                            self.mm(self.ps[:, bk[wi], 0:n], wv[:, k, wi, jj * 128:(jj + 1) * 128],
                                    t.xT[:, k, c0:c0 + n], k == 0, k == 7,
                                    reads=[("w", s)] + xr, writes=[("ps", bk[wi])])
                    q = self.nxt()
                    zo = si * (L + 2) + 2 + off
                    pb_, pc_, ph_ = (self.ps[:, bk[i], 0:n] for i in range(3))
                    cs, ys = self.csb[:, q, 0:n], self.ysb[:, q, 0:n]
                    zt = self.zt
                    cw = self.convw
                    S.add("act", lambda e, cs=cs, pc_=pc_: e.activation(out=cs, in_=pc_, func=AF.Copy),
                          reads=[("ps", bk[1])], writes=[("csb", q)])
                    S.add("dve", lambda e, o=zt[:, zq, zo:zo + n], cs=cs, ph_=ph_:
                          e.tensor_tensor(out=o, in0=cs, in1=ph_, op=ALU.mult),
                          reads=[("csb", q), ("ps", bk[2])], writes=[Z])
                    S.add("act", lambda e, ys=ys, a=zt[:, zq, zo:zo + n], w=cw[:, l, j, 2:3]:
                          e.activation(out=ys, in_=a, func=AF.Identity, scale=w),
                          reads=[Z, "convw"], writes=[("ysb", q)])
                    S.add("dve", lambda e, ys=ys, a=zt[:, zq, zo - 1:zo - 1 + n], w=cw[:, l, j, 1:2]:
                          e.scalar_tensor_tensor(out=ys, in0=a, scalar=w, in1=ys, op0=ALU.mult, op1=ALU.add),
                          reads=[Z, ("ysb", q), "convw"], writes=[("ysb", q)])
                    S.add("dve", lambda e, ys=ys, a=zt[:, zq, zo - 2:zo - 2 + n], w=cw[:, l, j, 0:1]:
                          e.scalar_tensor_tensor(out=ys, in0=a, scalar=w, in1=ys, op0=ALU.mult, op1=ALU.add),
                          reads=[Z, ("ysb", q), "convw"], writes=[("ysb", q)])
                    S.add("dve", lambda e, o=t.uT[:, j, c0:c0 + n], ys=ys, pb_=pb_:
                          e.tensor_tensor(out=o, in0=ys, in1=pb_, op=ALU.mult),
                          reads=[("ysb", q), ("ps", bk[0])], writes=[(t.kp + "uT", cbi)])
                for si, (sc0, L_, Lr) in enumerate(t.segs):
                    zoff = si * (L + 2)
                    src = self.zt[:, zq, zoff + Lr:zoff + Lr + 2]
                    dst = zpre[:, l, si, j, :]
                    if t.kind == "halo":
                        S.add("dve", lambda e, o=dst, a=src: e.tensor_scalar(
                            out=o, in0=a, scalar1=self.hvt[:, 0:1], scalar2=None, op0=ALU.mult),
                            reads=[Z, "hvt"], writes=[(zk, l)])
                    else:
                        S.add("dve", lambda e, o=dst, a=src: e.tensor_copy(out=o, in_=a),
                              reads=[Z], writes=[(zk, l)])
        self.proj_tok_resid(tiles, "uT", 8, lambda half, kh: ("wout", l, half, kh), 0)

    def mlp(self, tiles, l):
        self.S.tag = "%s%d.L%d.mlp_up" % (tiles[0].kind, tiles[0].idx, l)
        S = self.S
        self.fence()
        for jp in range(8):
            s = self.wget(("wup", l, jp))
            wv = self.wview(s, "k512")
            for jj in range(4):
                j = jp * 4 + jj
                for t in tiles:
                    for cbi, (c0, n, si, off) in enumerate([(0, t.W, 0, 0)] if t.kind == "sample" else t.cbs):
                        b = self.P8.next()
                        xr = xt_keys(t, groups_of(t, c0, n))
                        for k in range(8):
                            self.mm(self.ps[:, b, 0:n], wv[:, k, jj * 128:(jj + 1) * 128], t.xT[:, k, c0:c0 + n],
                                    k == 0, k == 7, reads=[("w", s)] + xr, writes=[("ps", b)])
                        q = self.nxt()
                        sq = self.sq[:, q, 0:n]
                        p = self.ps[:, b, 0:n]
                        S.add("act", lambda e, sq=sq, p=p: e.activation(out=sq, in_=p, func=AF.Square),
                              reads=[("ps", b)], writes=[("sq", q)])
                        S.add("dve", lambda e, o=t.hT[:, j, c0:c0 + n], sq=sq, p=p:
                              e.scalar_tensor_tensor(out=o, in0=p, scalar=0.0, in1=sq, op0=ALU.is_gt, op1=ALU.mult),
                              reads=[("ps", b), ("sq", q), "ALIAS"], writes=[(t.kp + "hT", cbi)])
        self.proj_tok_resid(tiles, "hT", 32, lambda half, kh: ("wdown", l, half, kh), 1, alias=True)

    def proj_feat(self, t, wkey, dst, dcol_of, cbs, scale, dkeyname, stage=None, alias=False):
        self.proj_feat_multi(wkey, [(t, dst, dcol_of, cbs, dkeyname, stage)], scale, alias)

    def proj_feat_multi(self, wkey, specs, scale, alias=False):
        S = self.S
        s = self.wget(wkey)
        wv = self.wview(s, "full")
        ar = ["ALIAS"] if alias else []
        for c in range(8):
            for (t, dst, dcol_of, cbs, dkeyname, stage) in specs:
                for (c0, n) in cbs:
                    b = self.P.next()
                    xr = xt_keys(t, groups_of(t, c0, n))
                    for k in range(8):
                        self.mm(self.ps[:, b, 0:n], wv[:, k, c // 4, (c % 4) * 128:(c % 4 + 1) * 128],
                                t.xT[:, k, c0:c0 + n], k == 0, k == 7, reads=[("w", s)] + xr, writes=[("ps", b)])
                    dc = dcol_of(c0)
                    p = self.ps[:, b, 0:n]
                    S.add("act", lambda e, o=dst[:, c, dc:dc + n], p=p: e.activation(out=o, in_=p, func=AF.Copy, scale=scale),
                          reads=[("ps", b)] + ar, writes=[(dkeyname, c)])
                    if stage is not None and self.cfg.get("kstage", True):
                        q = 0
                        S.add("dve", lambda e, o=self.kst[:, q, 0:n], p=p: e.tensor_copy(out=o, in_=p),
                              reads=[("ps", b), (dkeyname, c)], writes=[("csb", q)])
                        oc = dc - stage[1]
                        S.add("sp", lambda e, o=stage[0][:, c, oc:oc + n], a=self.kst[:, q, 0:n]: [e.dma_start(out=o, in_=a)],
                              reads=[("csb", q)], writes=[], dma=self.dkey(("csb", q)))

    def proj_kv(self, tiles, out_kv):
        self.S.tag = "%s%d.kv" % (tiles[0].kind, tiles[0].idx)
        S = self.S
        d = self.d
        if not self.cfg.get("kvout", True):
            out_kv = False
        specs = []
        for t in tiles:
            if t.kind == "sample":
                specs.append((t, self.KTs, lambda c0: c0, t.kcbs, "KTs", (d["ksT"], 0)))
            else:
                base = t.par * 512 - t.kcbs[0][0]
                specs.append((t, self.KT, lambda c0, base=base: base + c0, t.kcbs, "KTc",
                              (d["kT"], t.par * 512) if out_kv else None))
        self.proj_feat_multi(("wk", 0), specs, 1.0)
        for half in range(2 if self.cfg.get("vproj", True) else 0):
            s = self.wget(("wv", 0, half, 0))
            wv = self.wview(s, "k512")
            for t, n_, gi in [(t, n_, gi) for t in tiles for n_, gi in enumerate(t.kgroups)]:
                c0, sz = t.groups[gi]
                b = self.P.next()
                for k in range(8):
                    self.mm(self.ps[:sz, b, :], t.xT[:, k, c0:c0 + sz], wv[:, k, :], k == 0, k == 7,
                            reads=[("w", s), (t.kp + "xT", gi), (t.kp + "xTd", gi)], writes=[("ps", b)])
                pv = self.ps[:sz, b, :].rearrange("p (h e) -> p h e", e=64)
                if t.kind == "sample":
                    dst = self.Vs[:sz, half * 8:(half + 1) * 8, 0:64]
                    vk = ("Vs",)
                else:
                    slot = t.par * 4 + n_
                    dst = self.Vb[:sz, slot, half * 8:(half + 1) * 8, 0:64]
                    vk = ("V", slot)
                if t.kind == "halo":
                    S.add("act", lambda e, o=dst, a=pv, sz=sz: e.activation(out=o, in_=a, func=AF.Identity, scale=self.hvt[:sz, 0:1]),
                          reads=[("ps", b), "hvt"], writes=[vk])
                else:
                    S.add("act", lambda e, o=dst, a=pv: e.activation(out=o, in_=a, func=AF.Copy),
                          reads=[("ps", b)], writes=[vk])
                if half == 0:
                    if t.kind == "sample":
                        S.add("dve", lambda e, o=self.Vs[:sz, :, 64:65]: e.memset(o, 1.0), reads=[], writes=[("Vs1",)])
                    elif t.kind == "halo":
                        S.add("dve", lambda e, o=self.Vb[:sz, slot, :, 64:65], sz=sz:
                              e.tensor_scalar(out=o, in0=self.ones16[:sz, :].unsqueeze(2), scalar1=self.hvt[:sz, 0:1],
                                              scalar2=None, op0=ALU.mult),
                              reads=["hvt", "ones16"], writes=[("V1", slot)])
                    else:
                        S.add("dve", lambda e, o=self.Vb[:sz, slot, :, 64:65]: e.memset(o, 1.0), reads=[], writes=[("V1", slot)])
                if (out_kv or t.kind == "sample") and self.cfg.get("vstage", True):
                    q = 0
                    S.add("dve", lambda e, o=self.vst[:sz, q, :], p=self.ps[:sz, b, :]: e.tensor_copy(out=o, in_=p),
                          reads=[("ps", b), vk], writes=[("ysb", q)])
                    od = d["vs"] if t.kind == "sample" else d["v"]
                    r0 = c0
                    S.add("sp", lambda e, o=od[r0:r0 + sz, half * 512:(half + 1) * 512], a=self.vst[:sz, q, :]:
                          [e.dma_start(out=o, in_=a)], reads=[("ysb", q)], writes=[], dma=self.dkey(("ysb", q)))

    def slot_of(self, t, kg):
        return t.par * 4 + kg if kg >= 0 else (1 - t.par) * 4 + kg + 4

    def att_finish(self, t, gi, sz):
        S = self.S
        for bi, (h0, nh) in enumerate(((0, 7), (7, 7), (14, 2))):
            ov = self.ps[:sz, 4 + bi, 0:nh * 65].rearrange("p (h e) -> p h e", e=65)
            S.add("dve", lambda e, o=self.rden[:sz, h0:h0 + nh], a=ov[:, :, 64]: e.reciprocal(o, a),
                  reads=[("ps", 4 + bi)], writes=[("rden", bi)])
            S.add("dve", lambda e, o=self.On[:sz, h0 * 64:(h0 + nh) * 64].rearrange("p (h e) -> p h e", e=64),
                  a=ov[:, :, 0:64], r=self.rden[:sz, h0:h0 + nh].unsqueeze(2).broadcast_to([sz, nh, 64]):
                  e.tensor_tensor(out=o, in0=a, in1=r, op=ALU.mult),
                  reads=[("ps", 4 + bi), ("rden", bi), "ALIAS"], writes=[("On", c) for c in range(8)])
        c0 = t.groups[gi][0]
        for c4 in range(2):
            bank = 7
            for i in range(4):
                c = c4 * 4 + i
                S.add("pe", lambda e, o=self.ps[:, bank, i * 128:i * 128 + sz],
                      a=self.On[:sz, c * 128:(c + 1) * 128], idn=self.ident[:sz, :sz]: e.transpose(o, a, idn),
                      reads=[("On", c) for c in range(8)] + ["ident", "ALIAS"], writes=[("ps", bank)])
            pv = self.ps[:, bank, :].rearrange("p (a b) -> p a b", b=128)[:, :, 0:sz]
            S.add("act", lambda e, o=t.uT[:, c4 * 4:(c4 + 1) * 4, c0:c0 + sz], a=pv: e.activation(out=o, in_=a, func=AF.Copy),
                  reads=[("ps", bank)], writes=[(t.kp + "uT", 0), (t.kp + "uT", 1)])

    def attention(self, t, l):
        self.S.tag = "%s%d.L%d.att" % (t.kind, t.idx, l)
        S = self.S
        lb = l - 2
        self.fence()
        S.add("sp", lambda e: [e.dma_start(out=self.BV[:, 0:8, :], in_=self.d["bv"][lb, :, 0:8, :]),
                               e.dma_start(out=self.BV[:, 8:16, :], in_=self.d["bv"][lb, :, 8:16, :])],
              reads=["ALIAS"], writes=["BV"], dma=self.dkey("BV"))
        self.proj_feat(t, ("wq", lb), self.QT, lambda c0: c0, [(c0, n) for (c0, n, _, _) in t.cbs], 0.125, "QT", alias=True)
        steps = [(m, h) for m in range(4) for h in range(16)]
        LA = 2

        def qk(i):
            m, h = steps[i]
            q = i % 3
            c, pb = h // 2, (h % 2) * 64
            A, B = 2 * q, 2 * q + 1
            for dd in range(5):
                slot = self.slot_of(t, m - dd)
                o = self.ps[:, A, dd * 128:(dd + 1) * 128] if dd < 2 else self.ps[:, B, (dd - 2) * 128:(dd - 1) * 128]
                bk = A if dd < 2 else B
                self.mm(o, self.KT[pb:pb + 64, c, slot * 128:(slot + 1) * 128],
                        self.QT[pb:pb + 64, c, m * 128:(m + 1) * 128], True, True,
                        reads=[("KTc", c), ("QT", c), "ALIAS"], writes=[("ps", bk)])
            S.add("dve", lambda e, o=self.T1[:, q, :], a=self.ps[:, A, 0:256], bvv=self.BV[:, h, :]:
                  e.tensor_tensor(out=o, in0=a, in1=bvv, op=ALU.add),
                  reads=[("ps", A), "BV", "ALIAS"], writes=[("T1", q)])
            S.add("act", lambda e, o=self.ET[:, q, 0:256], a=self.T1[:, q, :]: e.activation(out=o, in_=a, func=AF.Exp),
                  reads=[("T1", q), "ALIAS"], writes=[("ET", q)])
            S.add("act", lambda e, o=self.ET[:, q, 256:640], a=self.ps[:, B, 0:384], bb=self.chi[:, lb, h:h + 1]:
                  e.activation(out=o, in_=a, func=AF.Exp, bias=bb, scale=1.0),
                  reads=[("ps", B), "chi", "ALIAS"], writes=[("ET2", q)])

        def pv(i):
            m, h = steps[i]
            q = i % 3
            r = i % 2
            ob = 6 + r
            for dd in range(4):
                slot = self.slot_of(t, m - dd)
                self.mm(self.ps[:, ob, 0:65], self.ET[:, q, dd * 128:(dd + 1) * 128],
                        self.Vb[:, slot, h, 0:65], dd == 0, False,
                        reads=[("ET", q), ("ET2", q), ("V", slot), ("V1", slot), "ALIAS"], writes=[("ps", ob)])
            slot = self.slot_of(t, m - 4)
            rd = [("ET", q), ("ET2", q), ("V", slot), ("V1", slot), "ALIAS"]
            self.mm(self.ps[0:64, ob, 0:65], self.ET[0:64, q, 512:576], self.Vb[0:64, slot, h, 0:65], False, False,
                    reads=rd, writes=[("ps", ob)])
            self.mm(self.ps[:, ob, 0:65], self.ET[64:128, q, 512:640], self.Vb[64:128, slot, h, 0:65], False, True,
                    reads=rd, writes=[("ps", ob)])
            S.add("dve", lambda e, o=self.rd2[:, r:r + 1], a=self.ps[:, ob, 64:65]: e.reciprocal(o, a),
                  reads=[("ps", ob)], writes=[("rd2", r)])
            S.add("dve", lambda e, o=self.Onb[:, h * 64:(h + 1) * 64], a=self.ps[:, ob, 0:64], rr=self.rd2[:, r:r + 1]:
                  e.tensor_scalar(out=o, in0=a, scalar1=rr, scalar2=None, op0=ALU.mult),
                  reads=[("ps", ob), ("rd2", r), "ALIAS"], writes=[("On", h // 2)])

        def fin(m):
            for c4 in range(2):
                bank = self.PT.next()
                psb = self.ps[:, bank, :].bitcast(BF16)
                for i in range(4):
                    c = c4 * 4 + i
                    S.add("pe", lambda e, o=psb[:, i * 128:(i + 1) * 128],
                          a=self.Onb[:, c * 128:(c + 1) * 128], idn=self.identb[:, :]: e.transpose(o, a, idn),
                          reads=[("On", c), "identb", "ALIAS"], writes=[("ps", bank)])
                pvw = psb[:, 0:512].rearrange("p (a b) -> p a b", b=128)
                S.add("act", lambda e, o=t.uT[:, c4 * 4:(c4 + 1) * 4, m * 128:(m + 1) * 128], a=pvw:
                      e.activation(out=o, in_=a, func=AF.Copy),
                      reads=[("ps", bank)], writes=[(t.kp + "uT", 0), (t.kp + "uT", 1)])
        for i in range(LA):
            qk(i)
        for i in range(len(steps)):
            if i + LA < len(steps):
                qk(i + LA)
            pv(i)
            if steps[i][1] == 15:
                fin(steps[i][0])
        self.proj_tok_resid([t], "uT", 8, lambda half, kh: ("wo", lb, half, kh), 0)

    def attention_sample(self, t, l):
        self.S.tag = "%s%d.L%d.att" % (t.kind, t.idx, l)
        S = self.S
        lb = l - 2
        self.fence()
        S.add("sp", lambda e: [e.dma_start(out=self.BV[:, 0:8, :], in_=self.d["bv"][lb, :, 0:8, :]),
                               e.dma_start(out=self.BV[:, 8:16, :], in_=self.d["bv"][lb, :, 8:16, :]),
                               e.dma_start(out=self.BVs[0:64, :, :], in_=self.d["bvs"][lb])],
              reads=["ALIAS"], writes=["BV"], dma=self.dkey("BV"))
        self.proj_feat(t, ("wq", lb), self.QT, lambda c0: c0, [(0, 64)], 0.125, "QT", alias=True)
        steps = [(s, h) for s in range(2) for h in range(16)]

        def qk(s, h, q):
            c, pb = h // 2, (h % 2) * 64
            A, B = 2 * q, 2 * q + 1
            p0 = s * 32
            rq = self.QT[pb:pb + 64, c, p0:p0 + 32]
            rd = [("KTc", c), ("KTs", c), ("QT", c), "ALIAS"]
            self.mm(self.ps[p0:p0 + 16, A, 0:32], self.KTs[pb:pb + 64, c, p0:p0 + 16], rq, True, True, rd, [("ps", A)])
            sl3 = s * 4 + 3
            self.mm(self.ps[:, A, 32:64], self.KT[pb:pb + 64, c, sl3 * 128:(sl3 + 1) * 128], rq, True, True, rd, [("ps", A)])
            for kg in range(3):
                sl = s * 4 + kg
                self.mm(self.ps[:, B, kg * 32:(kg + 1) * 32], self.KT[pb:pb + 64, c, sl * 128:(sl + 1) * 128], rq,
                        True, True, rd, [("ps", B)])
            S.add("dve", lambda e, o=self.T1[p0:p0 + 16, q, 0:32], a=self.ps[p0:p0 + 16, A, 0:32], bvv=self.BVs[p0:p0 + 16, h, :]:
                  e.tensor_tensor(out=o, in0=a, in1=bvv, op=ALU.add),
                  reads=[("ps", A), "BV", "ALIAS"], writes=[("T1", q)])
            S.add("dve", lambda e, o=self.T1[:, q, 32:64], a=self.ps[:, A, 32:64], bvv=self.BV[:, h, 128:160]:
                  e.tensor_tensor(out=o, in0=a, in1=bvv, op=ALU.add),
                  reads=[("ps", A), "BV", "ALIAS", ("T1", q)], writes=[("T1", q)])
            S.add("act", lambda e, o=self.ET[p0:p0 + 16, q, 0:32], a=self.T1[p0:p0 + 16, q, 0:32]: e.activation(out=o, in_=a, func=AF.Exp),
                  reads=[("T1", q), "ALIAS"], writes=[("ET", q)])
            S.add("act", lambda e, o=self.ET[:, q, 32:64], a=self.T1[:, q, 32:64]: e.activation(out=o, in_=a, func=AF.Exp),
                  reads=[("T1", q), "ALIAS", ("ET", q)], writes=[("ET", q)])
            S.add("act", lambda e, o=self.ET[:, q, 64:160], a=self.ps[:, B, 0:96], bb=self.chi[:, lb, h:h + 1]:
                  e.activation(out=o, in_=a, func=AF.Exp, bias=bb, scale=1.0),
                  reads=[("ps", B), "chi", "ALIAS"], writes=[("ET2", q)])

        def pv(s, h, q):
            ob, oc = 4 + h // 7, (h % 7) * 65
            p0 = s * 32
            o = self.ps[p0:p0 + 32, ob, oc:oc + 65]
            rd = [("ET", q), ("ET2", q), ("Vs",), ("Vs1",), "ALIAS"] + [("V", s * 4 + k) for k in range(4)]
            self.mm(o, self.ET[p0:p0 + 16, q, 0:32], self.Vs[p0:p0 + 16, h, 0:65], True, False, rd, [("ps", ob)])
            self.mm(o, self.ET[:, q, 32:64], self.Vb[:, s * 4 + 3, h, 0:65], False, False, rd, [("ps", ob)])
            for kg in range(3):
                self.mm(o, self.ET[:, q, 64 + kg * 32:64 + (kg + 1) * 32], self.Vb[:, s * 4 + kg, h, 0:65], False, kg == 2,
                        rd, [("ps", ob)])
        qk(0, 0, 0)
        for i, (s, h) in enumerate(steps):
            if i + 1 < len(steps):
                qk(steps[i + 1][0], steps[i + 1][1], (i + 1) % 2)
            pv(s, h, i % 2)
        self.att_finish(t, 0, 64)
        self.proj_tok_resid([t], "uT", 8, lambda half, kh: ("wo", lb, half, kh), 0)

    def prologue(self):
        S = self.S
        d = self.d
        S.add("dve", lambda e: e.memset(self.epst[:, :], EPS), reads=[], writes=["epst"])
        S.add("dve", lambda e: e.memset(self.ones16[:, :], 1.0), reads=[], writes=["ones16"])
        S.add("dve", lambda e: e.memset(self.zpre_p[:, :, :, :, :].rearrange("p a b c d -> p (a b c d)"), 0.0),
              reads=[], writes=[("zpre_p", 0), ("zpre_p", 1)])
        for nm, tl in (("convw", self.convw), ("chi", self.chi), ("ident", self.ident), ("hv", self.hvt), ("lnT", self.lnT)):
            S.add("sp", lambda e, o=tl, a=d[nm]: [e.dma_start(out=o[tuple(slice(None) for _ in o.shape)], in_=a)],
                  reads=[], writes=[nm if nm != "hv" else "hvt"], dma=self.dkey(("c", nm)))

    def prologue2(self):
        self.S.add("dve", lambda e: e.tensor_copy(out=self.identb[:, :], in_=self.ident[:, :]), reads=["ident"], writes=["identb"])

    def bind(self, t):
        if t.kind == "sample":
            t.xres, t.xT, t.uT, t.hT, t.kp, t.lnoff = self.s_xres, self.s_xT, self.s_uT, self.s_hT, "s_", 5
        else:
            t.xres, t.xT, t.uT, t.hT, t.kp, t.lnoff = self.m_xres, self.m_xT, self.m_uT, self.m_hT, "", 0
        return t

    def run_A(self, tiles):
        for t in tiles:
            self.load_x(t)
        for l in range(2):
            self.cur_l = l
            self.load_lnp(l, 0)
            self.mixer_A(tiles, l)
            self.load_lnp(l, 1)
            self.mlp(tiles, l)
        self.cur_l = 2
        self.proj_kv(tiles, False)

    def run_tile(self, t, out_kv=False):
        S = self.S
        cfg = self.cfg
        self.load_x(t)
        nl = cfg.get("nlayers", 4)
        for l in range(nl):
            self.cur_l = l
            self.load_lnp(l, 0)
            if l < 2:
                if cfg.get("mixer", True):
                    self.mixer_A([t], l)
            else:
                if l == 2:
                    self.proj_kv([t], out_kv)
                if not cfg.get("att", True):
                    pass
                else:
                    self.attention(t, l)
            if cfg.get("mlp", True):
                self.load_lnp(l, 1)
                self.mlp([t], l)
        self.store_y(t)

    def store_y(self, t):
        self.flush_pool()
        S = self.S
        od = self.d["ys"] if t.kind == "sample" else self.d["y"]
        r0 = 0 if t.kind == "sample" else t.idx * 512
        for gi, (c0, sz) in enumerate(t.groups):
            S.add("sp", lambda e, o=od[r0 + c0:r0 + c0 + sz, :], a=t.xres[:sz, gi, :]: [e.dma_start(out=o, in_=a)],
                  reads=[(t.kp + "xres", gi)], writes=[], dma=self.dkey((t.kp + "y", gi)))

    def emit_all(self):
        S = self.S
        d = self.d
        cfg = self.cfg
        self.prologue()
        self.prologue2()
        nm = cfg.get("nmain", NMAIN)
        halo = self.bind(mk_tile("halo"))
        samp = self.bind(mk_tile("sample"))
        do_s = cfg.get("sample", True)
        if do_s:
            S.add("sp", lambda e: [e.dma_start(out=self.zpre_s[:, :, :, :, :], in_=d["cconv"])],
                  reads=[], writes=[("zpre_s", 0), ("zpre_s", 1)], dma=self.dkey("cconv"))
        self.run_A([halo, samp] if do_s else [halo])
        if do_s:
            S.add("sp", lambda e: [e.dma_start(out=d["convs"], in_=self.zpre_s[:, :, :, :, :])],
                  reads=[("zpre_s", 0), ("zpre_s", 1)], writes=[], dma=self.dkey("convs"))
        for i in range(nm):
            t = self.bind(mk_tile("main", i))
            self.run_tile(t, out_kv=(i == nm - 1))
        S.add("sp", lambda e: [e.dma_start(out=d["convp"], in_=self.zpre_p[:, :, :, :, :])],
              reads=[("zpre_p", 0), ("zpre_p", 1)], writes=[], dma=self.dkey("convp"))
        if do_s:
            t = samp
            S.tag = "sample0.cache"
            S.add("pool", lambda e: [e.dma_start(out=self.KT[:, c, :], in_=d["ckT"][:, c, :]) for c in range(8)],
                  reads=[], writes=[("KTc", c) for c in range(8)], dma=self.dkey("ckT"))
            S.add("pool", lambda e: [e.dma_start(out=self.Vb[:, sl, :, 0:64],
                                                 in_=d["cv"][:, sl, :].rearrange("p (h e) -> p h e", e=64))
                                     for sl in range(8)],
                  reads=[], writes=[("V", sl) for sl in range(8)], dma=self.dkey("cv"))
            S.add("dve", lambda e: e.memset(self.Vb[:, :, :, 64:65].rearrange("p a b c -> p (a b) c"), 1.0),
                  reads=[], writes=[("V1", sl) for sl in range(8)])
            for l in (2, 3):
                self.cur_l = l
                self.load_lnp(l, 0)
                self.attention_sample(t, l)
                self.load_lnp(l, 1)
                self.mlp([t], l)
            self.store_y(t)

    def build(self):
        nc = self.nc
        self.declare_dram()
        es = ExitStack()
        self.es = es
        with es:
            self.alloc(es)
            self.emit_all()
            plan = self.wrec
            self.S = Sched()
            self.P = PsumRR(range(6))
            self.PT = PsumRR((6, 7))
            self.P8 = PsumRR(range(8))
            self.cnt = 0
            self.wplan = plan
            self.wscr = {}
            self.wuses = {}
            for k in plan:
                self.wuses[k] = self.wuses.get(k, 0) + 1
            self.wcur = 0
            self.wissued = 0
            self.dma_keys = {}
            self.emit_all()
            S = self.S
            S.finalize()
            ops = S.ops
            self._esem = {e: es.enter_context(nc.semaphore("sem_" + e)) for e in ("pe", "act", "dve", "pool")}
            self._dsem = {k: es.enter_context(nc.semaphore("dsem%d" % i)) for i, k in enumerate(self.dma_keys)}
            per_eng = {e: [] for e in Sched.ENGS}
            for i, op in enumerate(ops):
                per_eng[op.eng].append(i)
            self._ops = ops
            with nc.Block() as block:
                self._emit_block(block, per_eng)
        return nc

    def _emit_block(self, block, per_eng):
        ops = self._ops
        esem, dsem = self._esem, self._dsem
        class Fake:
            def dma_start(self, **kw):
                return 1
        fake = Fake()
        cnt = {e: 0 for e in esem}
        dcnt = {k: 0 for k in dsem}
        for op in ops:
            if op.dma is not None:
                n = len(op.fn(fake))
                dcnt[op.dma] += 16 * n
                op.ev = (dsem[op.dma], dcnt[op.dma])
            elif op.need:
                cnt[op.eng] += 1
                op.ev = (esem[op.eng], cnt[op.eng])
        final_d = dict(dcnt)

        def run(engname, e):
            waited = {}
            for i in per_eng[engname]:
                op = ops[i]
                for dd in op.rdeps:
                    sem, val = ops[dd].ev
                    key = id(sem)
                    if waited.get(key, 0) < val:
                        e.wait_ge(sem, val)
                        waited[key] = val
                if op.dma is not None:
                    for ins in op.fn(e):
                        ins.then_inc(op.ev[0], 16)
                else:
                    ins = op.fn(e)
                    if op.need:
                        ins.then_inc(op.ev[0], 1)
            if engname == "sp":
                for k, v in final_d.items():
                    if v > 0 and waited.get(id(dsem[k]), 0) < v:
                        e.wait_ge(dsem[k], v)

        @block.tensor
        def _(e):
            run("pe", e)

        @block.scalar
        def _(e):
            run("act", e)

        @block.vector
        def _(e):
            run("dve", e)

        @block.gpsimd
        def _(e):
            run("pool", e)

        @block.sync
        def _(e):
            run("sp", e)


def _bias_tables(rel_bias_b):
    nb = rel_bias_b.shape[0]
    jj = np.arange(128)[:, None]
    ii = np.arange(256)[None, :]
    idx = np.clip(ii - jj, -128, 128) + 128
    bv = np.empty((nb, 128, 16, 256), np.float32)
    for l in range(nb):
        tb = rel_bias_b[l]
        g = tb[idx]
        bv[l] = np.transpose(g, (0, 2, 1))
    bv[:, 64:128, :, 0:64] = NEG
    j2 = (np.arange(64) % 32)[:, None]
    i2 = np.arange(32)[None, :]
    idx2 = np.clip(i2 - j2, -128, 128) + 128
    bvs = np.empty((nb, 64, 16, 32), np.float32)
    for l in range(nb):
        bvs[l] = np.transpose(rel_bias_b[l][idx2], (0, 2, 1))
    chi = np.broadcast_to(np.transpose(rel_bias_b[:, 256, :], (0, 1))[None], (128, nb, 16)).astype(np.float32)
    return bv, bvs, np.ascontiguousarray(chi)


_NC_CACHE = {}


def _get_nc(cfg_key=()):
    if cfg_key not in _NC_CACHE:
        b = Builder(dict(cfg_key))
        _NC_CACHE[cfg_key] = b.build()
    return _NC_CACHE[cfg_key]


def kernel(x_prompt, x_sample, cache_conv, cache_k, cache_v, ln_mix_g, ln_mix_b, ln_ffn_g,
           ln_ffn_b, w_up, w_down, w_in_a, conv_w_a, w_out_a, w_k, w_v, w_q_b, w_o_b, rel_bias_b, _cfg=()):
    f = lambda a: np.ascontiguousarray(np.asarray(a, dtype=np.float32))
    x_prompt, x_sample, cache_conv, cache_k, cache_v = map(f, (x_prompt, x_sample, cache_conv, cache_k, cache_v))
    rel_bias_b = f(rel_bias_b)
    xp = x_prompt[0]
    lnp = f(np.stack([ln_mix_g, ln_mix_b, ln_ffn_g, ln_ffn_b], axis=1))
    convw = f(np.transpose(f(conv_w_a).reshape(2, 3, 8, 128), (3, 0, 2, 1)))
    lnT = f(np.transpose(lnp.reshape(4, 4, 8, 128), (3, 0, 1, 2)))
    bv, bvs, chi = _bias_tables(rel_bias_b)
    ident = np.eye(128, dtype=np.float32)
    shared = dict(lnp=lnp, lnT=lnT, convw=convw, bv=bv, bvs=bvs, chi=chi, ident=ident,
                  w_up=f(w_up), w_down=f(w_down), w_in_a=f(w_in_a), w_out_a=f(w_out_a),
                  w_k=f(w_k), w_v=f(w_v), w_q_b=f(w_q_b), w_o_b=f(w_o_b))
    in_maps = []
    for c in range(NCORES):
        s = c * TPC
        xh = np.zeros((XROWS, D), np.float32)
        lo = s - HALO
        a = max(lo, 0)
        xh[a - lo:] = xp[a:s + TPC]
        xs = np.zeros((64, D), np.float32)
        xs[0:16] = x_sample[2 * c]
        xs[32:48] = x_sample[2 * c + 1]
        hv = np.full((128, 1), 0.0 if c == 0 else 1.0, np.float32)
        cc = cache_conv[:, 2 * c:2 * c + 2].reshape(2, 2, 2, 8, 128)
        cconv = f(np.transpose(cc, (4, 0, 1, 3, 2)))
        ck = cache_k[2 * c:2 * c + 2].reshape(2, 512, 8, 128)
        ckT = f(np.transpose(ck, (3, 2, 0, 1)).reshape(128, 8, 1024))
        cvv = cache_v[2 * c:2 * c + 2].reshape(2, 4, 128, 1024)
        cv = f(np.transpose(cvv, (2, 0, 1, 3)).reshape(128, 8, 1024))
        m = dict(shared)
        m.update(xh=xh, xs=xs, hv=hv, cconv=cconv, ckT=ckT, cv=cv)
        in_maps.append(m)
    nc = _get_nc(_cfg)
    ncr = dict(_cfg).get("ncores", NCORES)
    res = run_bass_kernel_spmd(nc, in_maps[:ncr], core_ids=list(range(ncr)))
    R = list(res.results)
    while len(R) < NCORES:
        R.append(R[0])
    y_prompt = np.concatenate([R[c]["y"] for c in range(NCORES)], axis=0)[None]
    y_sample = np.empty((16, 16, D), np.float32)
    conv_sample = np.empty((2, 16, 2, D), np.float32)
    k_sample = np.empty((16, 16, 16, 64), np.float32)
    v_sample = np.empty((16, 16, 16, 64), np.float32)
    for c in range(NCORES):
        r = R[c]
        for sg in range(2):
            b = 2 * c + sg
            y_sample[b] = r["ys"][sg * 32:sg * 32 + 16]
            v_sample[b] = r["vs"][sg * 32:sg * 32 + 16].reshape(16, 16, 64)
            kk = r["ksT"][:, :, sg * 32:sg * 32 + 16]
            k_sample[b] = np.transpose(kk, (2, 1, 0)).reshape(16, 16, 64)
            cs = r["convs"][:, :, sg]
            conv_sample[:, b] = np.transpose(cs, (1, 3, 2, 0)).reshape(2, 2, D)
    last = R[NCORES - 1]
    cp = last["convp"][:, :, 0]
    conv_prompt = np.ascontiguousarray(np.transpose(cp, (1, 3, 2, 0)).reshape(2, 1, 2, D))
    k_prompt = np.ascontiguousarray(np.transpose(last["kT"], (2, 1, 0)).reshape(1, 512, 16, 64))
    v_prompt = np.ascontiguousarray(last["v"].reshape(1, 512, 16, 64))
    return (y_prompt, y_sample, conv_prompt, k_prompt, v_prompt, conv_sample, k_sample, v_sample)
```

```python
import numpy as np
from contextlib import ExitStack
import concourse.bass as bass
import concourse.mybir as mybir
from concourse.bass_utils import run_bass_kernel_spmd

F32 = mybir.dt.float32
BF16 = mybir.dt.bfloat16
AF = mybir.ActivationFunctionType
ALU = mybir.AluOpType

D = 1024
NCORES = 8
TPC = 2048
NMAIN = 4
HALO = 516
XROWS = HALO + TPC
ALPHA = 8.0 ** 0.25
EPS = 1e-5
NEG = -30000.0
NS = 3
SLOT = 8192
WA = 520


class Op:
    __slots__ = ("eng", "fn", "deps", "dma", "rdeps", "ev", "need", "tag")

    def __init__(self, eng, fn, deps, dma):
        self.eng, self.fn, self.deps, self.dma = eng, fn, deps, dma
        self.rdeps, self.ev, self.need = [], None, False


class Sched:
    ENGS = ("pe", "act", "dve", "pool", "sp")

    def __init__(self):
        self.ops = []
        self.lastw = {}
        self.readers = {}
        self.tag = ""

    def add(self, eng, fn, reads=(), writes=(), dma=None):
        i = len(self.ops)
        deps = set()
        for r in reads:
            w = self.lastw.get(r)
            if w is not None:
                deps.add(w)
        for r in writes:
            w = self.lastw.get(r)
            if w is not None:
                deps.add(w)
            for j in self.readers.get(r, {}).values():
                deps.add(j)
        chan = ("dma", dma) if dma is not None else eng
        for r in reads:
            self.readers.setdefault(r, {})[chan] = i
        for r in writes:
            self.lastw[r] = i
            self.readers[r] = {}
        deps.discard(i)
        op = Op(eng, fn, deps, dma)
        op.tag = self.tag
        self.ops.append(op)
        return i

    def chan(self, op):
        return ("dma", op.dma) if op.dma is not None else op.eng

    def finalize(self):
        ops = self.ops
        for op in ops:
            by = {}
            for d in op.deps:
                c = self.chan(ops[d])
                if c == "pe" and op.eng == "pe" and op.dma is None:
                    continue
                if by.get(c, -1) < d:
                    by[c] = d
            op.rdeps = sorted(by.values())
            for d in op.rdeps:
                ops[d].need = True


class PsumRR:
    def __init__(self, banks):
        self.banks = list(banks)
        self.i = 0

    def next(self):
        b = self.banks[self.i % len(self.banks)]
        self.i += 1
        return b


class Tile:
    pass


def mk_tile(kind, idx=0):
    t = Tile()
    t.kind = kind
    t.idx = idx
    if kind == "halo":
        t.W = 516
        t.cbs = [(0, 258, 0, 0), (258, 258, 0, 258)]
        t.groups = [(0, 4), (4, 128), (132, 128), (260, 128), (388, 128)]
        t.segs = [(0, 516, 516)]
        t.row0 = 0
        t.par = 0
        t.kcbs = [(4, 256), (260, 256)]
        t.kgroups = [1, 2, 3, 4]
    elif kind == "main":
        t.W = 512
        t.cbs = [(0, 512, 0, 0)]
        t.groups = [(0, 128), (128, 128), (256, 128), (384, 128)]
        t.segs = [(0, 512, 512)]
        t.row0 = HALO + idx * 512
        t.par = (idx + 1) % 2
        t.kcbs = [(0, 512)]
        t.kgroups = [0, 1, 2, 3]
    else:
        t.W = 64
        t.cbs = [(0, 32, 0, 0), (32, 32, 1, 0)]
        t.groups = [(0, 64)]
        t.segs = [(0, 32, 16), (32, 32, 16)]
        t.row0 = 0
        t.par = 0
        t.kcbs = [(0, 64)]
        t.kgroups = [0]
    return t


def groups_of(t, c0, n):
    return [gi for gi, (g0, sz) in enumerate(t.groups) if g0 < c0 + n and c0 < g0 + sz]


def xt_keys(t, gis):
    return [(t.kp + "xT", x) for x in gis] + [(t.kp + "xTd", x) for x in gis]


class Builder:
    def __init__(self, cfg):
        self.cfg = cfg
        self.nc = bass.Bass("TRN2", target_bir_lowering=False)
        self.S = Sched()
        self.P = PsumRR(range(6))
        self.PT = PsumRR((6, 7))
        self.P8 = PsumRR(range(8))
        self.cnt = 0
        self.wplan = None
        self.wrec = []
        self.wissued = 0
        self.wcur = 0
        self.dma_keys = {}
        self.xqmap = {}
        self.pdef = []

    def declare_dram(self):
        nc = self.nc

        def din(name, shape):
            return nc.dram_tensor(name, list(shape), F32, kind="ExternalInput").ap()

        def dout(name, shape):
            return nc.dram_tensor(name, list(shape), F32, kind="ExternalOutput").ap()

        d = {}
        d["xh"] = din("xh", [XROWS, D])
        d["xs"] = din("xs", [64, D])
        d["hv"] = din("hv", [128, 1])
        d["cconv"] = din("cconv", [128, 2, 2, 8, 2])
        d["ckT"] = din("ckT", [128, 8, 1024])
        d["cv"] = din("cv", [128, 8, 1024])
        d["lnp"] = din("lnp", [4, 4, D])
        d["lnT"] = din("lnT", [128, 4, 4, 8])
        d["convw"] = din("convw", [128, 2, 8, 3])
        d["bv"] = din("bv", [2, 128, 16, 256])
        d["bvs"] = din("bvs", [2, 64, 16, 32])
        d["chi"] = din("chi", [128, 2, 16])
        d["ident"] = din("ident", [128, 128])
        d["w_up"] = din("w_up", [4, D, 4 * D])
        d["w_down"] = din("w_down", [4, 4 * D, D])
        d["w_in_a"] = din("w_in_a", [2, D, 3 * D])
        d["w_out_a"] = din("w_out_a", [2, D, D])
        d["w_k"] = din("w_k", [D, D])
        d["w_v"] = din("w_v", [D, D])
        d["w_q_b"] = din("w_q_b", [2, D, D])
        d["w_o_b"] = din("w_o_b", [2, D, D])
        d["y"] = dout("y", [TPC, D])
        d["ys"] = dout("ys", [64, D])
        d["convp"] = dout("convp", [128, 2, 2, 8, 2])
        d["kT"] = dout("kT", [128, 8, 512])
        d["v"] = dout("v", [512, D])
        d["convs"] = dout("convs", [128, 2, 2, 8, 2])
        d["ksT"] = dout("ksT", [128, 8, 64])
        d["vs"] = dout("vs", [64, D])
        self.d = d

    def alloc(self, es):
        nc = self.nc

        def sb(name, shape, dt=F32):
            return es.enter_context(nc.sbuf_tensor("sb_" + name, list(shape), dt))

        self.m_xres = sb("xres", [128, 5, D])
        self.m_xT = sb("xT", [128, 8, WA], BF16)
        self.m_uT = sb("uT", [128, 8, WA], BF16)
        self.big = sb("big", [128, 32 * WA], BF16)
        self.m_hT = self.big[:, :].rearrange("p (j w) -> p j w", w=WA)
        o = 0
        self.QT = sb("QT", [128, 8, WA], BF16)
        self.BV = self.big[:, o:o + 16 * 256 * 2].bitcast(F32).rearrange("p (h i) -> p h i", i=256)
        o += 16 * 256 * 2
        self.ET = self.big[:, o:o + 3 * 640].rearrange("p (q i) -> p q i", i=640)
        o += 3 * 640
        self.T1 = self.big[:, o:o + 3 * 256 * 2].bitcast(F32).rearrange("p (q i) -> p q i", i=256)
        o += 3 * 256 * 2
        self.On = self.big[:, o:o + 1024 * 2].bitcast(F32)
        self.Onb = self.big[:, o:o + 1024]
        o += 1024 * 2
        assert o <= 32 * WA
        self.BVs = sb("BVs", [64, 16, 32])
        self.rd2 = sb("rd2", [128, 2])
        self.lnT = sb("lnT", [128, 4, 4, 8])
        self.zt = sb("zt", [128, 2, 524])
        self.csb = sb("csb", [128, 2, 512])
        self.ysb = sb("ysb", [128, 2, 512])
        self.kst = self.csb
        self.vst = self.ysb
        self.sq = sb("sq", [128, 2, 512], BF16)
        self.KT = sb("KT", [128, 8, 1024], BF16)
        self.Vb = sb("Vb", [128, 8, 16, 66], BF16)
        self.KTs = sb("KTs", [128, 8, 64], BF16)
        self.Vs = sb("Vs", [64, 16, 66], BF16)
        self.lnp = sb("lnp", [128, 2, D])
        self.xnb = sb("xnb", [128, 2, D], BF16)
        self.identb = sb("identb", [128, 128], BF16)
        self.st5 = sb("st5", [128, 6, 12])
        self.mv5 = sb("mv5", [128, 6, 2])
        self.sd5 = sb("sd5", [128, 6, 1])
        self.rs5 = sb("rs5", [128, 6, 1])
        self.nm5 = sb("nm5", [128, 6, 1])
        self.s_xres = sb("s_xres", [128, 1, D])
        self.s_xT = sb("s_xT", [128, 8, 64], BF16)
        self.s_uT = sb("s_uT", [128, 8, 64], BF16)
        self.s_hT = sb("s_hT", [128, 32, 64], BF16)
        self.zpre_p = sb("zpre_p", [128, 2, 2, 8, 2])
        self.zpre_s = sb("zpre_s", [128, 2, 2, 8, 2])
        self.convw = sb("convw", [128, 2, 8, 3])
        self.chi = sb("chi", [128, 2, 16])
        self.ident = sb("ident", [128, 128])
        self.hvt = sb("hvt", [128, 1])
        self.st = sb("st", [128, 2, 12])
        self.mv = sb("mv", [128, 2, 2])
        self.sd = sb("sd", [128, 2, 1])
        self.rs = sb("rs", [128, 2, 1])
        self.rden = sb("rden", [128, 16])
        self.fz = sb("fz", [128, 2])
        self.epst = sb("epst", [128, 1])
        self.ones16 = sb("ones16", [128, 16])
        self.wring = sb("wring", [128, NS, SLOT], BF16)
        self.ps = es.enter_context(nc.psum_tensor("ps", [128, 8, 512], F32))

    def nxt(self):
        self.cnt += 1
        return self.cnt % 2

    def dkey(self, key):
        self.dma_keys[key] = True
        return key

    def mm(self, out, lhsT, rhs, start, stop, reads, writes):
        self.S.add("pe", lambda e, o=out, l=lhsT, r=rhs, s=start, t=stop:
                   e.matmul(o, l, r, start=s, stop=t), reads, writes)

    def fence(self):
        self.S.add("dve", lambda e, a=self.fz[:, 0:2]: e.memset(a, 0.0), reads=[], writes=["ALIAS", "fz"])

    def wsrc(self, key):
        d = self.d
        kind = key[0]
        if kind == "win":
            _, l, jp = key
            v = d["w_in_a"][l].rearrange("(kc p) (w f) -> p kc w f", p=128, w=3)
            return [(v[:, :, wi, jp * 256:(jp + 1) * 256], (8, 256), wi * 8 * 256) for wi in range(3)], (8 * 3 * 256)
        if kind == "wup":
            _, l, jp = key
            v = d["w_up"][l].rearrange("(kc p) f -> p kc f", p=128)
            return [(v[:, :, jp * 512:(jp + 1) * 512], (8, 512), 0)], 8 * 512
        if kind == "wdown":
            _, l, half, kh = key
            v = d["w_down"][l].rearrange("(kc p) f -> p kc f", p=128)
            return [(v[:, kh * 16:(kh + 1) * 16, half * 512:(half + 1) * 512], (16, 512), 0)], 16 * 512
        if kind in ("wout", "wo", "wv"):
            _, l, half, kh = key
            src = {"wout": d["w_out_a"], "wo": d["w_o_b"]}.get(kind)
            m = d["w_v"] if kind == "wv" else src[l]
            v = m.rearrange("(kc p) f -> p kc f", p=128)
            return [(v[:, :, half * 512:(half + 1) * 512], (8, 512), 0)], 8 * 512
        if kind in ("wq", "wk"):
            _, l = key
            m = d["w_k"] if kind == "wk" else d["w_q_b"][l]
            v = m.rearrange("(kc p) f -> p kc f", p=128)
            return [(v[:, :, 0:512], (8, 512), 0), (v[:, :, 512:1024], (8, 512), 8 * 512)], 8 * 1024
        raise KeyError(key)

    def wissue(self, i):
        key = self.wplan[i]
        s = i % NS
        parts, tot = self.wsrc(key)
        if key in self.wscr:
            scr = self.wscr[key]
            self.S.add("sp", lambda e, o=self.wring[:, s, 0:tot], a=scr: [e.dma_start(out=o, in_=a)],
                       reads=[("scr", key)], writes=[("w", s)], dma=self.dkey(("w", s)))
            return
        outs = []
        for src, (a, b), off in parts:
            dst = self.wring[:, s, off:off + a * b].rearrange("p (a b) -> p a b", b=b)
            outs.append((dst, src))

        def fn(e, outs=outs):
            return [e.dma_start(out=o, in_=i_) for o, i_ in outs]
        self.S.add("pool", fn, reads=[], writes=[("w", s)], dma=self.dkey(("wc", s)))
        if self.cfg.get("scratch", True) and self.wuses.get(key, 0) > 1:
            scr = self.nc.dram_tensor("scr%d" % len(self.wscr), [128, tot], BF16, kind="Internal").ap()
            self.wscr[key] = scr
            self.S.add("sp", lambda e, o=scr, a=self.wring[:, s, 0:tot]: [e.dma_start(out=o, in_=a)],
                       reads=[("w", s)], writes=[("scr", key)], dma=self.dkey(("wb", s)))

    def flush_pool(self):
        ops, self.pdef = self.pdef, []
        for fn, rd, wr in ops:
            self.S.add("pool", fn, reads=rd, writes=wr)

    def wget(self, key, nheld=1):
        i = self.wcur
        self.wcur += 1
        if self.wplan is None:
            self.wrec.append(key)
            self.flush_pool()
            return 0
        assert self.wplan[i] == key, (self.wplan[i], key)
        while self.wissued < min(len(self.wplan), i + NS - (nheld - 1)):
            self.wissue(self.wissued)
            self.wissued += 1
        self.flush_pool()
        return i % NS

    def wview(self, s, kind):
        r = self.wring[:, s, :]
        if kind == "win":
            return r[:, 0:8 * 3 * 256].rearrange("p (w k f) -> p k w f", w=3, k=8)
        if kind == "k512":
            return r[:, 0:8 * 512].rearrange("p (k f) -> p k f", f=512)
        if kind == "k16":
            return r[:, 0:16 * 512].rearrange("p (k f) -> p k f", f=512)
        if kind == "full":
            return r[:, 0:8 * 1024].rearrange("p (h k f) -> p k h f", h=2, k=8)
        raise KeyError(kind)

    def to_feat(self, t, gi, src, dst, dkeyname, src_reads, extra_reads=(), gb=None):
        c0, sz = t.groups[gi]
        S = self.S
        for c4 in range(2):
            bank = self.PT.next()
            for i in range(4):
                c = c4 * 4 + i
                S.add("pe", lambda e, o=self.ps[:, bank, i * 128:i * 128 + sz],
                      a=src[:, c * 128:(c + 1) * 128], idn=self.ident[:sz, :sz]:
                      e.transpose(o, a, idn),
                      reads=list(src_reads) + ["ident"], writes=[("ps", bank)])
            if gb is not None:
                l_, li_ = gb
                for i in range(4):
                    c = c4 * 4 + i
                    S.add("act", lambda e, o=dst[:, c, c0:c0 + sz], a=self.ps[:, bank, i * 128:i * 128 + sz],
                          g_=self.lnT[:, l_, li_ * 2, c:c + 1], b_=self.lnT[:, l_, li_ * 2 + 1, c:c + 1]:
                          e.activation(out=o, in_=a, func=AF.Identity, bias=b_, scale=g_),
                          reads=[("ps", bank), "lnT"] + list(extra_reads), writes=[(t.kp + dkeyname, gi)])
                continue
            pv = self.ps[:, bank, :].rearrange("p (a b) -> p a b", b=128)[:, :, 0:sz]
            S.add("act", lambda e, o=dst[:, c4 * 4:(c4 + 1) * 4, c0:c0 + sz], a=pv:
                  e.activation(out=o, in_=a, func=AF.Copy),
                  reads=[("ps", bank)] + list(extra_reads), writes=[(t.kp + dkeyname, gi)])

    def load_x(self, t):
        self.S.tag = "%s%d.loadx" % (t.kind, t.idx)
        S = self.S
        src = self.d["xs"] if t.kind == "sample" else self.d["xh"]
        for gi, (c0, sz) in enumerate(t.groups):
            S.add("sp", lambda e, o=t.xres[:sz, gi, :], a=src[t.row0 + c0:t.row0 + c0 + sz, :]:
                  [e.dma_start(out=o, in_=a)],
                  reads=[], writes=[(t.kp + "xres", gi)], dma=self.dkey((t.kp + "x", gi)))
            self.to_feat(t, gi, t.xres[:sz, gi, :], t.xT, "xT", [(t.kp + "xres", gi)])

    def load_lnp(self, l, which):
        self.flush_pool()
        def fn(e, l=l, which=which):
            return [e.dma_start(out=self.lnp[:, v, :], in_=self.d["lnp"][l, which * 2 + v].partition_broadcast(128))
                    for v in range(2)]
        self.S.add("sp", fn, reads=[], writes=["lnp"], dma=self.dkey("lnp"))

    def ln(self, t, gi, lnidx):
        S = self.S
        c0, sz = t.groups[gi]
        q = self.nxt()
        xr = t.xres[:sz, gi, :]
        st, mv, sd, rs = self.st[:sz, q, :], self.mv[:sz, q, :], self.sd[:sz, q, :], self.rs[:sz, q, :]
        X = (t.kp + "xres", gi)
        S.add("dve", lambda e: e.bn_stats(st[:, 0:6], xr[:, 0:512]), reads=[X], writes=[("st", q)])
        S.add("dve", lambda e: e.bn_stats(st[:, 6:12], xr[:, 512:1024]), reads=[X, ("st", q)], writes=[("st", q)])
        S.add("dve", lambda e: e.bn_aggr(mv, st), reads=[("st", q)], writes=[("mv", q)])
        S.add("act", lambda e: e.activation(out=sd, in_=mv[:, 1:2], func=AF.Sqrt, bias=self.epst[:sz, :], scale=1.0),
              reads=[("mv", q), "epst"], writes=[("sd", q)])
        S.add("dve", lambda e: e.reciprocal(rs, sd), reads=[("sd", q)], writes=[("rs", q)])
        S.add("dve", lambda e: e.tensor_scalar(out=xr, in0=xr, scalar1=mv[:, 0:1], scalar2=rs,
                                               op0=ALU.subtract, op1=ALU.mult),
              reads=[X, ("mv", q), ("rs", q)], writes=[X])

    def ln_stats(self, t, gi, lnidx):
        S = self.S
        c0, sz = t.groups[gi]
        q = gi + t.lnoff
        xr = t.xres[:sz, gi, :]
        st, mv, sd, rs = self.st5[:sz, q, :], self.mv5[:sz, q, :], self.sd5[:sz, q, :], self.rs5[:sz, q, :]
        self.xq = getattr(self, "xq", 0) + 1
        xi = self.xq % 2
        self.xqmap[(t.kp, gi)] = xi
        xb = self.xnb[:sz, xi, :]
        X = (t.kp + "xres", gi)
        S.add("dve", lambda e: e.bn_stats(st[:, 0:6], xr[:, 0:512]), reads=[X], writes=[("st5", q)])
        S.add("dve", lambda e: e.bn_stats(st[:, 6:12], xr[:, 512:1024]), reads=[X, ("st5", q)], writes=[("st5", q)])
        S.add("dve", lambda e: e.bn_aggr(mv, st), reads=[("st5", q)], writes=[("mv5", q)])
        S.add("act", lambda e: e.activation(out=sd, in_=mv[:, 1:2], func=AF.Sqrt, bias=self.epst[:sz, :], scale=1.0),
              reads=[("mv5", q), "epst"], writes=[("sd5", q)])
        S.add("dve", lambda e: e.reciprocal(rs, sd), reads=[("sd5", q)], writes=[("rs5", q)])
        nm = self.nm5[:sz, q, :]
        S.add("dve", lambda e: e.tensor_scalar(out=nm, in0=mv[:, 0:1], scalar1=-1.0, scalar2=rs, op0=ALU.mult, op1=ALU.mult),
              reads=[("mv5", q), ("rs5", q)], writes=[("nm5", q)])
        S.add("act", lambda e: e.activation(out=xb, in_=xr, func=AF.Identity, bias=nm, scale=rs),
              reads=[X, ("nm5", q), ("rs5", q)], writes=[("xnb", xi)])
        self.pdef.append((lambda e: e.tensor_scalar(out=xr, in0=xr, scalar1=rs, scalar2=nm, op0=ALU.mult, op1=ALU.add),
                          [X, ("nm5", q), ("rs5", q)], [X]))
        self.pdef.append((lambda e: e.tensor_tensor(out=xr, in0=xr, in1=self.lnp[:sz, 0, :], op=ALU.mult),
                          [X, "lnp"], [X]))
        self.pdef.append((lambda e: e.tensor_tensor(out=xr, in0=xr, in1=self.lnp[:sz, 1, :], op=ALU.add),
                          [X, "lnp"], [X]))

    def ln_out(self, t, gi, lnidx):
        S = self.S
        c0, sz = t.groups[gi]
        l_ = self.cur_l
        xi = self.xqmap[(t.kp, gi)]
        xb = self.xnb[:sz, xi, :]
        for c4 in range(2):
            bank = self.PT.next()
            psb = self.ps[:, bank, :].bitcast(BF16)
            for i in range(4):
                c = c4 * 4 + i
                S.add("pe", lambda e, o=psb[:, i * 128:i * 128 + sz], a=xb[:, c * 128:(c + 1) * 128],
                      idn=self.identb[:sz, :sz]: e.transpose(o, a, idn),
                      reads=[("xnb", xi), "identb"], writes=[("ps", bank)])
            for i in range(4):
                c = c4 * 4 + i
                o = t.xT[:, c, c0:c0 + sz]
                a = psb[:, i * 128:i * 128 + sz]
                g_ = self.lnT[:, l_, lnidx * 2, c:c + 1]
                b_ = self.lnT[:, l_, lnidx * 2 + 1, c:c + 1]
                if c4 == 0:
                    S.add("act", lambda e, o=o, a=a, g_=g_, b_=b_: e.activation(out=o, in_=a, func=AF.Identity, bias=b_, scale=g_),
                          reads=[("ps", bank), "lnT"], writes=[(t.kp + "xT", gi)])
                else:
                    S.add("dve", lambda e, o=o, a=a, g_=g_, b_=b_: e.tensor_scalar(out=o, in0=a, scalar1=g_, scalar2=b_,
                                                                                 op0=ALU.mult, op1=ALU.add),
                          reads=[("ps", bank), "lnT"], writes=[(t.kp + "xTd", gi)])

    def ln_gb(self, t, gi, lnidx):
        S = self.S
        c0, sz = t.groups[gi]
        xr = t.xres[:sz, gi, :]
        X = (t.kp + "xres", gi)
        eng = self.cfg.get("gb_eng", "pool")
        S.add(eng, lambda e: e.tensor_tensor(out=xr, in0=xr, in1=self.lnp[:sz, 0, :], op=ALU.mult),
              reads=[X, "lnp"], writes=[X])
        S.add(eng, lambda e: e.tensor_tensor(out=xr, in0=xr, in1=self.lnp[:sz, 1, :], op=ALU.add),
              reads=[X, "lnp"], writes=[X])

    def proj_tok_resid(self, tiles, bufname, nk, wkeyf, lnidx, alias=False):
        S = self.S
        self.flush_pool()
        S.tag = S.tag.rsplit(".", 1)[0] + (".down" if nk == 32 else ".oproj")
        kper = min(nk, 16)
        nkh = nk // kper
        ar = ["ALIAS"] if alias else []
        vk = "k16" if kper == 16 else "k512"
        tg = [(t, gi) for t in tiles for gi in range(len(t.groups))]

        def resid(t, gi, half, b):
            c0, sz = t.groups[gi]
            xr = t.xres[:sz, gi, half * 512:(half + 1) * 512]
            S.add("dve", lambda e, xr=xr, p=self.ps[:sz, b, :]:
                  e.scalar_tensor_tensor(out=xr, in0=xr, scalar=ALPHA, in1=p, op0=ALU.mult, op1=ALU.add),
                  reads=[(t.kp + "xres", gi), ("ps", b)], writes=[(t.kp + "xres", gi)])

        def mms(t, gi, b, s, wv, kh):
            c0, sz = t.groups[gi]
            actT = getattr(t, bufname)
            ard = [(t.kp + bufname, x) for x in range(len(t.cbs))] + ar
            for kk in range(kper):
                k = kh * kper + kk
                self.mm(self.ps[:sz, b, :], actT[:, k, c0:c0 + sz], wv[:, kk, :], k == 0, k == nk - 1,
                        reads=[("w", s)] + ard, writes=[("ps", b)])
        pending = None
        if nkh == 1:
            ss = [self.wget(wkeyf(0, 0)), self.wget(wkeyf(1, 0), nheld=2)]
            wvs = [self.wview(s, vk) for s in ss]
            for (t, gi) in tg:
                bs = []
                for half in range(2):
                    b = self.P.next()
                    bs.append(b)
                    mms(t, gi, b, ss[half], wvs[half], 0)
                for half in range(2):
                    resid(t, gi, half, bs[half])
                self.ln_stats(t, gi, lnidx)
                if pending is not None:
                    self.ln_out(pending[0], pending[1], lnidx)
                pending = (t, gi)
            self.ln_out(pending[0], pending[1], lnidx)
            return
        banks = {}
        assert len(tg) <= 6
        for kh in range(nkh):
            s = self.wget(wkeyf(0, kh))
            wv = self.wview(s, vk)
            for i, (t, gi) in enumerate(tg):
                if kh == 0:
                    banks[i] = self.P.next()
                mms(t, gi, banks[i], s, wv, kh)
        for i, (t, gi) in enumerate(tg):
            resid(t, gi, 0, banks[i])
        ss = [self.wget(wkeyf(1, kh), nheld=kh + 1) for kh in range(nkh)]
        wvs = [self.wview(s, vk) for s in ss]
        for (t, gi) in tg:
            b = self.P.next()
            for kh in range(nkh):
                mms(t, gi, b, ss[kh], wvs[kh], kh)
            resid(t, gi, 1, b)
            self.ln_stats(t, gi, lnidx)
            if pending is not None:
                self.ln_out(pending[0], pending[1], lnidx)
            pending = (t, gi)
        self.ln_out(pending[0], pending[1], lnidx)

    def mixer_A(self, tiles, l):
        self.S.tag = "%s%d.L%d.mixer" % (tiles[0].kind, tiles[0].idx, l)
        S = self.S
        for jp in range(4):
            s = self.wget(("win", l, jp))
            wv = self.wview(s, "win")
            for jj in range(2):
              for t in tiles:
                zpre = self.zpre_s if t.kind == "sample" else self.zpre_p
                zk = "zpre_s" if t.kind == "sample" else "zpre_p"
                L = t.segs[0][1]
                j = jp * 2 + jj
                self.zcnt = getattr(self, 'zcnt', 0) + 1
                zq = self.zcnt % 2
                Z = ("zt", zq)
                for si, (sc0, L_, Lr) in enumerate(t.segs):
                    zoff = si * (L + 2)
                    S.add("dve", lambda e, o=self.zt[:, zq, zoff:zoff + 2], a=zpre[:, l, si, j, :]:
                          e.tensor_copy(out=o, in_=a), reads=[(zk, l)], writes=[Z])
                one_blk = t.kind == "sample"
                if one_blk:
                    bk = [self.P8.next() for _ in range(3)]
                    xr = xt_keys(t, groups_of(t, 0, t.W))
                    for wi in range(3):
                        for k in range(8):
                            self.mm(self.ps[:, bk[wi], 0:t.W], wv[:, k, wi, jj * 128:(jj + 1) * 128],
                                    t.xT[:, k, 0:t.W], k == 0, k == 7,
                                    reads=[("w", s)] + xr, writes=[("ps", bk[wi])])
                for cbi, (c0, n, si, off) in enumerate(t.cbs):
                    po = c0 if one_blk else 0
                    if not one_blk:
                        bk = [self.P8.next() for _ in range(3)]
                        xr = xt_keys(t, groups_of(t, c0, n))
                        for wi in range(3):
                            for k in range(8):
                                self.mm(self.ps[:, bk[wi], 0:n], wv[:, k, wi, jj * 128:(jj + 1) * 128],
                                        t.xT[:, k, c0:c0 + n], k == 0, k == 7,
                                        reads=[("w", s)] + xr, writes=[("ps", bk[wi])])
                    q = self.nxt()
                    zo = si * (L + 2) + 2 + off
                    pb_, pc_, ph_ = (self.ps[:, bk[i], po:po + n] for i in range(3))
                    cs, ys = self.csb[:, q, 0:n], self.ysb[:, q, 0:n]
                    zt = self.zt
                    cw = self.convw
                    S.add("act", lambda e, cs=cs, pc_=pc_: e.activation(out=cs, in_=pc_, func=AF.Copy),
                          reads=[("ps", bk[1])], writes=[("csb", q)])
                    S.add("dve", lambda e, o=zt[:, zq, zo:zo + n], cs=cs, ph_=ph_:
                          e.tensor_tensor(out=o, in0=cs, in1=ph_, op=ALU.mult),
                          reads=[("csb", q), ("ps", bk[2])], writes=[Z])
                    S.add("act", lambda e, ys=ys, a=zt[:, zq, zo:zo + n], w=cw[:, l, j, 2:3]:
                          e.activation(out=ys, in_=a, func=AF.Identity, scale=w),
                          reads=[Z, "convw"], writes=[("ysb", q)])
                    S.add("dve", lambda e, ys=ys, a=zt[:, zq, zo - 1:zo - 1 + n], w=cw[:, l, j, 1:2]:
                          e.scalar_tensor_tensor(out=ys, in0=a, scalar=w, in1=ys, op0=ALU.mult, op1=ALU.add),
                          reads=[Z, ("ysb", q), "convw"], writes=[("ysb", q)])
                    S.add("dve", lambda e, ys=ys, a=zt[:, zq, zo - 2:zo - 2 + n], w=cw[:, l, j, 0:1]:
                          e.scalar_tensor_tensor(out=ys, in0=a, scalar=w, in1=ys, op0=ALU.mult, op1=ALU.add),
                          reads=[Z, ("ysb", q), "convw"], writes=[("ysb", q)])
                    S.add("dve", lambda e, o=t.uT[:, j, c0:c0 + n], ys=ys, pb_=pb_:
                          e.tensor_tensor(out=o, in0=ys, in1=pb_, op=ALU.mult),
                          reads=[("ysb", q), ("ps", bk[0])], writes=[(t.kp + "uT", cbi)])
                for si, (sc0, L_, Lr) in enumerate(t.segs):
                    zoff = si * (L + 2)
                    src = self.zt[:, zq, zoff + Lr:zoff + Lr + 2]
                    dst = zpre[:, l, si, j, :]
                    if t.kind == "halo":
                        S.add("dve", lambda e, o=dst, a=src: e.tensor_scalar(
                            out=o, in0=a, scalar1=self.hvt[:, 0:1], scalar2=None, op0=ALU.mult),
                            reads=[Z, "hvt"], writes=[(zk, l)])
                    else:
                        S.add("dve", lambda e, o=dst, a=src: e.tensor_copy(out=o, in_=a),
                              reads=[Z], writes=[(zk, l)])
        self.proj_tok_resid(tiles, "uT", 8, lambda half, kh: ("wout", l, half, kh), 0)

    def mlp(self, tiles, l):
        self.S.tag = "%s%d.L%d.mlp_up" % (tiles[0].kind, tiles[0].idx, l)
        S = self.S
        self.fence()
        for jp in range(8):
            s = self.wget(("wup", l, jp))
            wv = self.wview(s, "k512")
            for jj in range(4):
                j = jp * 4 + jj
                for t in tiles:
                    for cbi, (c0, n, si, off) in enumerate([(0, t.W, 0, 0)] if t.kind == "sample" else t.cbs):
                        b = self.P8.next()
                        xr = xt_keys(t, groups_of(t, c0, n))
                        for k in range(8):
                            self.mm(self.ps[:, b, 0:n], wv[:, k, jj * 128:(jj + 1) * 128], t.xT[:, k, c0:c0 + n],
                                    k == 0, k == 7, reads=[("w", s)] + xr, writes=[("ps", b)])
                        q = self.nxt()
                        sq = self.sq[:, q, 0:n]
                        p = self.ps[:, b, 0:n]
                        S.add("act", lambda e, sq=sq, p=p: e.activation(out=sq, in_=p, func=AF.Square),
                              reads=[("ps", b)], writes=[("sq", q)])
                        S.add("dve", lambda e, o=t.hT[:, j, c0:c0 + n], sq=sq, p=p:
                              e.scalar_tensor_tensor(out=o, in0=p, scalar=0.0, in1=sq, op0=ALU.is_gt, op1=ALU.mult),
                              reads=[("ps", b), ("sq", q), "ALIAS"], writes=[(t.kp + "hT", cbi)])
        self.proj_tok_resid(tiles, "hT", 32, lambda half, kh: ("wdown", l, half, kh), 1, alias=True)

    def proj_feat(self, t, wkey, dst, dcol_of, cbs, scale, dkeyname, stage=None, alias=False):
        self.proj_feat_multi(wkey, [(t, dst, dcol_of, cbs, dkeyname, stage)], scale, alias)

    def proj_feat_multi(self, wkey, specs, scale, alias=False):
        S = self.S
        s = self.wget(wkey)
        wv = self.wview(s, "full")
        ar = ["ALIAS"] if alias else []
        for c in range(8):
            for (t, dst, dcol_of, cbs, dkeyname, stage) in specs:
                for (c0, n) in cbs:
                    b = self.P8.next()
                    xr = xt_keys(t, groups_of(t, c0, n))
                    for k in range(8):
                        self.mm(self.ps[:, b, 0:n], wv[:, k, c // 4, (c % 4) * 128:(c % 4 + 1) * 128],
                                t.xT[:, k, c0:c0 + n], k == 0, k == 7, reads=[("w", s)] + xr, writes=[("ps", b)])
                    dc = dcol_of(c0)
                    p = self.ps[:, b, 0:n]
                    S.add("act", lambda e, o=dst[:, c, dc:dc + n], p=p: e.activation(out=o, in_=p, func=AF.Copy, scale=scale),
                          reads=[("ps", b)] + ar, writes=[(dkeyname, c)])
                    if stage is not None and self.cfg.get("kstage", True):
                        q = 0
                        S.add("dve", lambda e, o=self.kst[:, q, 0:n], p=p: e.tensor_copy(out=o, in_=p),
                              reads=[("ps", b), (dkeyname, c)], writes=[("csb", q)])
                        oc = dc - stage[1]
                        S.add("sp", lambda e, o=stage[0][:, c, oc:oc + n], a=self.kst[:, q, 0:n]: [e.dma_start(out=o, in_=a)],
                              reads=[("csb", q)], writes=[], dma=self.dkey(("csb", q)))

    def proj_kv(self, tiles, out_kv):
        self.S.tag = "%s%d.kv" % (tiles[0].kind, tiles[0].idx)
        S = self.S
        d = self.d
        if not self.cfg.get("kvout", True):
            out_kv = False
        specs = []
        for t in tiles:
            if t.kind == "sample":
                specs.append((t, self.KTs, lambda c0: c0, t.kcbs, "KTs", (d["ksT"], 0)))
            else:
                base = t.par * 512 - t.kcbs[0][0]
                specs.append((t, self.KT, lambda c0, base=base: base + c0, t.kcbs, "KTc",
                              (d["kT"], t.par * 512) if out_kv else None))
        self.proj_feat_multi(("wk", 0), specs, 1.0)
        for half in range(2 if self.cfg.get("vproj", True) else 0):
            s = self.wget(("wv", 0, half, 0))
            wv = self.wview(s, "k512")
            for t, n_, gi in [(t, n_, gi) for t in tiles for n_, gi in enumerate(t.kgroups)]:
                c0, sz = t.groups[gi]
                b = self.P.next()
                for k in range(8):
                    self.mm(self.ps[:sz, b, :], t.xT[:, k, c0:c0 + sz], wv[:, k, :], k == 0, k == 7,
                            reads=[("w", s), (t.kp + "xT", gi), (t.kp + "xTd", gi)], writes=[("ps", b)])
                pv = self.ps[:sz, b, :].rearrange("p (h e) -> p h e", e=64)
                if t.kind == "sample":
                    dst = self.Vs[:sz, half * 8:(half + 1) * 8, 0:64]
                    vk = ("Vs",)
                else:
                    slot = t.par * 4 + n_
                    dst = self.Vb[:sz, slot, half * 8:(half + 1) * 8, 0:64]
                    vk = ("V", slot)
                if t.kind == "halo":
                    S.add("act", lambda e, o=dst, a=pv, sz=sz: e.activation(out=o, in_=a, func=AF.Identity, scale=self.hvt[:sz, 0:1]),
                          reads=[("ps", b), "hvt"], writes=[vk])
                else:
                    S.add("act", lambda e, o=dst, a=pv: e.activation(out=o, in_=a, func=AF.Copy),
                          reads=[("ps", b)], writes=[vk])
                if half == 0:
                    if t.kind == "sample":
                        S.add("dve", lambda e, o=self.Vs[:sz, :, 64:65]: e.memset(o, 1.0), reads=[], writes=[("Vs1",)])
                    elif t.kind == "halo":
                        S.add("dve", lambda e, o=self.Vb[:sz, slot, :, 64:65], sz=sz:
                              e.tensor_scalar(out=o, in0=self.ones16[:sz, :].unsqueeze(2), scalar1=self.hvt[:sz, 0:1],
                                              scalar2=None, op0=ALU.mult),
                              reads=["hvt", "ones16"], writes=[("V1", slot)])
                    else:
                        S.add("dve", lambda e, o=self.Vb[:sz, slot, :, 64:65]: e.memset(o, 1.0), reads=[], writes=[("V1", slot)])
                if (out_kv or t.kind == "sample") and self.cfg.get("vstage", True):
                    q = 0
                    S.add("dve", lambda e, o=self.vst[:sz, q, :], p=self.ps[:sz, b, :]: e.tensor_copy(out=o, in_=p),
                          reads=[("ps", b), vk], writes=[("ysb", q)])
                    od = d["vs"] if t.kind == "sample" else d["v"]
                    r0 = c0
                    S.add("sp", lambda e, o=od[r0:r0 + sz, half * 512:(half + 1) * 512], a=self.vst[:sz, q, :]:
                          [e.dma_start(out=o, in_=a)], reads=[("ysb", q)], writes=[], dma=self.dkey(("ysb", q)))

    def slot_of(self, t, kg):
        return t.par * 4 + kg if kg >= 0 else (1 - t.par) * 4 + kg + 4

    def att_finish(self, t, gi, sz):
        S = self.S
        for bi, (h0, nh) in enumerate(((0, 7), (7, 7), (14, 2))):
            ov = self.ps[:sz, 4 + bi, 0:nh * 65].rearrange("p (h e) -> p h e", e=65)
            S.add("dve", lambda e, o=self.rden[:sz, h0:h0 + nh], a=ov[:, :, 64]: e.reciprocal(o, a),
                  reads=[("ps", 4 + bi)], writes=[("rden", bi)])
            S.add("dve", lambda e, o=self.On[:sz, h0 * 64:(h0 + nh) * 64].rearrange("p (h e) -> p h e", e=64),
                  a=ov[:, :, 0:64], r=self.rden[:sz, h0:h0 + nh].unsqueeze(2).broadcast_to([sz, nh, 64]):
                  e.tensor_tensor(out=o, in0=a, in1=r, op=ALU.mult),
                  reads=[("ps", 4 + bi), ("rden", bi), "ALIAS"], writes=[("On", c) for c in range(8)])
        c0 = t.groups[gi][0]
        for c4 in range(2):
            bank = 7
            for i in range(4):
                c = c4 * 4 + i
                S.add("pe", lambda e, o=self.ps[:, bank, i * 128:i * 128 + sz],
                      a=self.On[:sz, c * 128:(c + 1) * 128], idn=self.ident[:sz, :sz]: e.transpose(o, a, idn),
                      reads=[("On", c) for c in range(8)] + ["ident", "ALIAS"], writes=[("ps", bank)])
            pv = self.ps[:, bank, :].rearrange("p (a b) -> p a b", b=128)[:, :, 0:sz]
            S.add("act", lambda e, o=t.uT[:, c4 * 4:(c4 + 1) * 4, c0:c0 + sz], a=pv: e.activation(out=o, in_=a, func=AF.Copy),
                  reads=[("ps", bank)], writes=[(t.kp + "uT", 0), (t.kp + "uT", 1)])

    def attention(self, t, l):
        self.S.tag = "%s%d.L%d.att" % (t.kind, t.idx, l)
        S = self.S
        lb = l - 2
        self.fence()
        S.add("sp", lambda e: [e.dma_start(out=self.BV[:, 0:8, :], in_=self.d["bv"][lb, :, 0:8, :]),
                               e.dma_start(out=self.BV[:, 8:16, :], in_=self.d["bv"][lb, :, 8:16, :])],
              reads=["ALIAS"], writes=["BV"], dma=self.dkey("BV"))
        self.proj_feat(t, ("wq", lb), self.QT, lambda c0: c0, [(c0, n) for (c0, n, _, _) in t.cbs], 0.125, "QT", alias=True)
        steps = [(m, h) for m in range(4) for h in range(16)]
        LA = 2

        def qk(i):
            m, h = steps[i]
            q = i % 3
            c, pb = h // 2, (h % 2) * 64
            A, B = 2 * q, 2 * q + 1
            for dd in range(5):
                slot = self.slot_of(t, m - dd)
                o = self.ps[:, A, dd * 128:(dd + 1) * 128] if dd < 2 else self.ps[:, B, (dd - 2) * 128:(dd - 1) * 128]
                bk = A if dd < 2 else B
                self.mm(o, self.KT[pb:pb + 64, c, slot * 128:(slot + 1) * 128],
                        self.QT[pb:pb + 64, c, m * 128:(m + 1) * 128], True, True,
                        reads=[("KTc", c), ("QT", c), "ALIAS"], writes=[("ps", bk)])
            S.add("dve", lambda e, o=self.T1[:, q, :], a=self.ps[:, A, 0:256], bvv=self.BV[:, h, :]:
                  e.tensor_tensor(out=o, in0=a, in1=bvv, op=ALU.add),
                  reads=[("ps", A), "BV", "ALIAS"], writes=[("T1", q)])
            S.add("act", lambda e, o=self.ET[:, q, 0:256], a=self.T1[:, q, :]: e.activation(out=o, in_=a, func=AF.Exp),
                  reads=[("T1", q), "ALIAS"], writes=[("ET", q)])
            S.add("act", lambda e, o=self.ET[:, q, 256:640], a=self.ps[:, B, 0:384], bb=self.chi[:, lb, h:h + 1]:
                  e.activation(out=o, in_=a, func=AF.Exp, bias=bb, scale=1.0),
                  reads=[("ps", B), "chi", "ALIAS"], writes=[("ET2", q)])

        def pv(i):
            m, h = steps[i]
            q = i % 3
            r = i % 2
            ob = 6 + r
            for dd in range(4):
                slot = self.slot_of(t, m - dd)
                self.mm(self.ps[:, ob, 0:65], self.ET[:, q, dd * 128:(dd + 1) * 128],
                        self.Vb[:, slot, h, 0:65], dd == 0, False,
                        reads=[("ET", q), ("ET2", q), ("V", slot), ("V1", slot), "ALIAS"], writes=[("ps", ob)])
            slot = self.slot_of(t, m - 4)
            rd = [("ET", q), ("ET2", q), ("V", slot), ("V1", slot), "ALIAS"]
            self.mm(self.ps[0:64, ob, 0:65], self.ET[0:64, q, 512:576], self.Vb[0:64, slot, h, 0:65], False, False,
                    reads=rd, writes=[("ps", ob)])
            self.mm(self.ps[:, ob, 0:65], self.ET[64:128, q, 512:640], self.Vb[64:128, slot, h, 0:65], False, True,
                    reads=rd, writes=[("ps", ob)])
            S.add("dve", lambda e, o=self.rd2[:, r:r + 1], a=self.ps[:, ob, 64:65]: e.reciprocal(o, a),
                  reads=[("ps", ob)], writes=[("rd2", r)])
            S.add("dve", lambda e, o=self.Onb[:, h * 64:(h + 1) * 64], a=self.ps[:, ob, 0:64], rr=self.rd2[:, r:r + 1]:
                  e.tensor_scalar(out=o, in0=a, scalar1=rr, scalar2=None, op0=ALU.mult),
                  reads=[("ps", ob), ("rd2", r), "ALIAS"], writes=[("On", h // 2)])

        def fin(m):
            for c4 in range(2):
                bank = self.PT.next()
                psb = self.ps[:, bank, :].bitcast(BF16)
                for i in range(4):
                    c = c4 * 4 + i
                    S.add("pe", lambda e, o=psb[:, i * 128:(i + 1) * 128],
                          a=self.Onb[:, c * 128:(c + 1) * 128], idn=self.identb[:, :]: e.transpose(o, a, idn),
                          reads=[("On", c), "identb", "ALIAS"], writes=[("ps", bank)])
                pvw = psb[:, 0:512].rearrange("p (a b) -> p a b", b=128)
                S.add("act", lambda e, o=t.uT[:, c4 * 4:(c4 + 1) * 4, m * 128:(m + 1) * 128], a=pvw:
                      e.activation(out=o, in_=a, func=AF.Copy),
                      reads=[("ps", bank)], writes=[(t.kp + "uT", 0), (t.kp + "uT", 1)])
        for i in range(LA):
            qk(i)
        for i in range(len(steps)):
            if i + LA < len(steps):
                qk(i + LA)
            pv(i)
            if steps[i][1] == 15:
                fin(steps[i][0])
        self.proj_tok_resid([t], "uT", 8, lambda half, kh: ("wo", lb, half, kh), 0)

    def attention_sample(self, t, l):
        self.S.tag = "%s%d.L%d.att" % (t.kind, t.idx, l)
        S = self.S
        lb = l - 2
        self.fence()
        S.add("sp", lambda e: [e.dma_start(out=self.BV[:, 0:8, :], in_=self.d["bv"][lb, :, 0:8, :]),
                               e.dma_start(out=self.BV[:, 8:16, :], in_=self.d["bv"][lb, :, 8:16, :]),
                               e.dma_start(out=self.BVs[0:64, :, :], in_=self.d["bvs"][lb])],
              reads=["ALIAS"], writes=["BV"], dma=self.dkey("BV"))
        self.proj_feat(t, ("wq", lb), self.QT, lambda c0: c0, [(0, 64)], 0.125, "QT", alias=True)
        steps = [(s, h) for s in range(2) for h in range(16)]

        def qk(s, h, q):
            c, pb = h // 2, (h % 2) * 64
            A, B = 2 * q, 2 * q + 1
            p0 = s * 32
            rq = self.QT[pb:pb + 64, c, p0:p0 + 32]
            rd = [("KTc", c), ("KTs", c), ("QT", c), "ALIAS"]
            self.mm(self.ps[p0:p0 + 16, A, 0:32], self.KTs[pb:pb + 64, c, p0:p0 + 16], rq, True, True, rd, [("ps", A)])
            sl3 = s * 4 + 3
            self.mm(self.ps[:, A, 32:64], self.KT[pb:pb + 64, c, sl3 * 128:(sl3 + 1) * 128], rq, True, True, rd, [("ps", A)])
            for kg in range(3):
                sl = s * 4 + kg
                self.mm(self.ps[:, B, kg * 32:(kg + 1) * 32], self.KT[pb:pb + 64, c, sl * 128:(sl + 1) * 128], rq,
                        True, True, rd, [("ps", B)])
            S.add("dve", lambda e, o=self.T1[p0:p0 + 16, q, 0:32], a=self.ps[p0:p0 + 16, A, 0:32], bvv=self.BVs[p0:p0 + 16, h, :]:
                  e.tensor_tensor(out=o, in0=a, in1=bvv, op=ALU.add),
                  reads=[("ps", A), "BV", "ALIAS"], writes=[("T1", q)])
            S.add("dve", lambda e, o=self.T1[:, q, 32:64], a=self.ps[:, A, 32:64], bvv=self.BV[:, h, 128:160]:
                  e.tensor_tensor(out=o, in0=a, in1=bvv, op=ALU.add),
                  reads=[("ps", A), "BV", "ALIAS", ("T1", q)], writes=[("T1", q)])
            S.add("act", lambda e, o=self.ET[p0:p0 + 16, q, 0:32], a=self.T1[p0:p0 + 16, q, 0:32]: e.activation(out=o, in_=a, func=AF.Exp),
                  reads=[("T1", q), "ALIAS"], writes=[("ET", q)])
            S.add("act", lambda e, o=self.ET[:, q, 32:64], a=self.T1[:, q, 32:64]: e.activation(out=o, in_=a, func=AF.Exp),
                  reads=[("T1", q), "ALIAS", ("ET", q)], writes=[("ET", q)])
            S.add("act", lambda e, o=self.ET[:, q, 64:160], a=self.ps[:, B, 0:96], bb=self.chi[:, lb, h:h + 1]:
                  e.activation(out=o, in_=a, func=AF.Exp, bias=bb, scale=1.0),
                  reads=[("ps", B), "chi", "ALIAS"], writes=[("ET2", q)])

        def pv(s, h, q):
            ob, oc = 4 + h // 7, (h % 7) * 65
            p0 = s * 32
            o = self.ps[p0:p0 + 32, ob, oc:oc + 65]
            rd = [("ET", q), ("ET2", q), ("Vs",), ("Vs1",), "ALIAS"] + [("V", s * 4 + k) for k in range(4)]
            self.mm(o, self.ET[p0:p0 + 16, q, 0:32], self.Vs[p0:p0 + 16, h, 0:65], True, False, rd, [("ps", ob)])
            self.mm(o, self.ET[:, q, 32:64], self.Vb[:, s * 4 + 3, h, 0:65], False, False, rd, [("ps", ob)])
            for kg in range(3):
                self.mm(o, self.ET[:, q, 64 + kg * 32:64 + (kg + 1) * 32], self.Vb[:, s * 4 + kg, h, 0:65], False, kg == 2,
                        rd, [("ps", ob)])
        qk(0, 0, 0)
        for i, (s, h) in enumerate(steps):
            if i + 1 < len(steps):
                qk(steps[i + 1][0], steps[i + 1][1], (i + 1) % 2)
            pv(s, h, i % 2)
        self.att_finish(t, 0, 64)
        self.proj_tok_resid([t], "uT", 8, lambda half, kh: ("wo", lb, half, kh), 0)

    def prologue(self):
        S = self.S
        d = self.d
        S.add("dve", lambda e: e.memset(self.epst[:, :], EPS), reads=[], writes=["epst"])
        S.add("dve", lambda e: e.memset(self.ones16[:, :], 1.0), reads=[], writes=["ones16"])
        S.add("dve", lambda e: e.memset(self.zpre_p[:, :, :, :, :].rearrange("p a b c d -> p (a b c d)"), 0.0),
              reads=[], writes=[("zpre_p", 0), ("zpre_p", 1)])
        for nm, tl in (("convw", self.convw), ("chi", self.chi), ("ident", self.ident), ("hv", self.hvt), ("lnT", self.lnT)):
            S.add("sp", lambda e, o=tl, a=d[nm]: [e.dma_start(out=o[tuple(slice(None) for _ in o.shape)], in_=a)],
                  reads=[], writes=[nm if nm != "hv" else "hvt"], dma=self.dkey(("c", nm)))

    def prologue2(self):
        self.S.add("dve", lambda e: e.tensor_copy(out=self.identb[:, :], in_=self.ident[:, :]), reads=["ident"], writes=["identb"])

    def bind(self, t):
        if t.kind == "sample":
            t.xres, t.xT, t.uT, t.hT, t.kp, t.lnoff = self.s_xres, self.s_xT, self.s_uT, self.s_hT, "s_", 5
        else:
            t.xres, t.xT, t.uT, t.hT, t.kp, t.lnoff = self.m_xres, self.m_xT, self.m_uT, self.m_hT, "", 0
        return t

    def run_A(self, tiles):
        for t in tiles:
            self.load_x(t)
        for l in range(2):
            self.cur_l = l
            self.load_lnp(l, 0)
            self.mixer_A(tiles, l)
            self.load_lnp(l, 1)
            self.mlp(tiles, l)
        self.cur_l = 2
        self.proj_kv(tiles, False)

    def run_tile(self, t, out_kv=False):
        S = self.S
        cfg = self.cfg
        self.load_x(t)
        nl = cfg.get("nlayers", 4)
        for l in range(nl):
            self.cur_l = l
            self.load_lnp(l, 0)
            if l < 2:
                if cfg.get("mixer", True):
                    self.mixer_A([t], l)
            else:
                if l == 2:
                    self.proj_kv([t], out_kv)
                if not cfg.get("att", True):
                    pass
                else:
                    self.attention(t, l)
            if cfg.get("mlp", True):
                self.load_lnp(l, 1)
                self.mlp([t], l)
        self.store_y(t)

    def store_y(self, t):
        self.flush_pool()
        S = self.S
        od = self.d["ys"] if t.kind == "sample" else self.d["y"]
        r0 = 0 if t.kind == "sample" else t.idx * 512
        for gi, (c0, sz) in enumerate(t.groups):
            S.add("sp", lambda e, o=od[r0 + c0:r0 + c0 + sz, :], a=t.xres[:sz, gi, :]: [e.dma_start(out=o, in_=a)],
                  reads=[(t.kp + "xres", gi)], writes=[], dma=self.dkey((t.kp + "y", gi)))

    def emit_all(self):
        S = self.S
        d = self.d
        cfg = self.cfg
        self.prologue()
        self.prologue2()
        nm = cfg.get("nmain", NMAIN)
        halo = self.bind(mk_tile("halo"))
        samp = self.bind(mk_tile("sample"))
        do_s = cfg.get("sample", True)
        if do_s:
            S.add("sp", lambda e: [e.dma_start(out=self.zpre_s[:, :, :, :, :], in_=d["cconv"])],
                  reads=[], writes=[("zpre_s", 0), ("zpre_s", 1)], dma=self.dkey("cconv"))
        self.run_A([halo, samp] if do_s else [halo])
        if do_s:
            S.add("sp", lambda e: [e.dma_start(out=d["convs"], in_=self.zpre_s[:, :, :, :, :])],
                  reads=[("zpre_s", 0), ("zpre_s", 1)], writes=[], dma=self.dkey("convs"))
        for i in range(nm):
            t = self.bind(mk_tile("main", i))
            self.run_tile(t, out_kv=(i == nm - 1))
        S.add("sp", lambda e: [e.dma_start(out=d["convp"], in_=self.zpre_p[:, :, :, :, :])],
              reads=[("zpre_p", 0), ("zpre_p", 1)], writes=[], dma=self.dkey("convp"))
        if do_s:
            t = samp
            S.tag = "sample0.cache"
            S.add("pool", lambda e: [e.dma_start(out=self.KT[:, c, :], in_=d["ckT"][:, c, :]) for c in range(8)],
                  reads=[], writes=[("KTc", c) for c in range(8)], dma=self.dkey("ckT"))
            S.add("pool", lambda e: [e.dma_start(out=self.Vb[:, sl, :, 0:64],
                                                 in_=d["cv"][:, sl, :].rearrange("p (h e) -> p h e", e=64))
                                     for sl in range(8)],
                  reads=[], writes=[("V", sl) for sl in range(8)], dma=self.dkey("cv"))
            S.add("dve", lambda e: e.memset(self.Vb[:, :, :, 64:65].rearrange("p a b c -> p (a b) c"), 1.0),
                  reads=[], writes=[("V1", sl) for sl in range(8)])
            for l in (2, 3):
                self.cur_l = l
                self.load_lnp(l, 0)
                self.attention_sample(t, l)
                self.load_lnp(l, 1)
                self.mlp([t], l)
            self.store_y(t)

    def build(self):
        nc = self.nc
        self.declare_dram()
        es = ExitStack()
        self.es = es
        with es:
            self.alloc(es)
            self.emit_all()
            plan = self.wrec
            self.S = Sched()
            self.P = PsumRR(range(6))
            self.PT = PsumRR((6, 7))
            self.P8 = PsumRR(range(8))
            self.cnt = 0
            self.wplan = plan
            self.wscr = {}
            self.wuses = {}
            for k in plan:
                self.wuses[k] = self.wuses.get(k, 0) + 1
            self.wcur = 0
            self.wissued = 0
            self.dma_keys = {}
            self.emit_all()
            S = self.S
            S.finalize()
            ops = S.ops
            self._esem = {e: es.enter_context(nc.semaphore("sem_" + e)) for e in ("pe", "act", "dve", "pool")}
            self._dsem = {k: es.enter_context(nc.semaphore("dsem%d" % i)) for i, k in enumerate(self.dma_keys)}
            per_eng = {e: [] for e in Sched.ENGS}
            for i, op in enumerate(ops):
                per_eng[op.eng].append(i)
            self._ops = ops
            with nc.Block() as block:
                self._emit_block(block, per_eng)
        return nc

    def _emit_block(self, block, per_eng):
        ops = self._ops
        esem, dsem = self._esem, self._dsem
        class Fake:
            def dma_start(self, **kw):
                return 1
        fake = Fake()
        cnt = {e: 0 for e in esem}
        dcnt = {k: 0 for k in dsem}
        for op in ops:
            if op.dma is not None:
                n = len(op.fn(fake))
                dcnt[op.dma] += 16 * n
                op.ev = (dsem[op.dma], dcnt[op.dma])
            elif op.need:
                cnt[op.eng] += 1
                op.ev = (esem[op.eng], cnt[op.eng])
        final_d = dict(dcnt)

        def run(engname, e):
            waited = {}
            for i in per_eng[engname]:
                op = ops[i]
                for dd in op.rdeps:
                    sem, val = ops[dd].ev
                    key = id(sem)
                    if waited.get(key, 0) < val:
                        e.wait_ge(sem, val)
                        waited[key] = val
                if op.dma is not None:
                    for ins in op.fn(e):
                        ins.then_inc(op.ev[0], 16)
                else:
                    ins = op.fn(e)
                    if op.need:
                        ins.then_inc(op.ev[0], 1)
            if engname == "sp":
                for k, v in final_d.items():
                    if v > 0 and waited.get(id(dsem[k]), 0) < v:
                        e.wait_ge(dsem[k], v)

        @block.tensor
        def _(e):
            run("pe", e)

        @block.scalar
        def _(e):
            run("act", e)

        @block.vector
        def _(e):
            run("dve", e)

        @block.gpsimd
        def _(e):
            run("pool", e)

        @block.sync
        def _(e):
            run("sp", e)


def _bias_tables(rel_bias_b):
    nb = rel_bias_b.shape[0]
    jj = np.arange(128)[:, None]
    ii = np.arange(256)[None, :]
    idx = np.clip(ii - jj, -128, 128) + 128
    bv = np.empty((nb, 128, 16, 256), np.float32)
    for l in range(nb):
        tb = rel_bias_b[l]
        g = tb[idx]
        bv[l] = np.transpose(g, (0, 2, 1))
    bv[:, 64:128, :, 0:64] = NEG
    j2 = (np.arange(64) % 32)[:, None]
    i2 = np.arange(32)[None, :]
    idx2 = np.clip(i2 - j2, -128, 128) + 128
    bvs = np.empty((nb, 64, 16, 32), np.float32)
    for l in range(nb):
        bvs[l] = np.transpose(rel_bias_b[l][idx2], (0, 2, 1))
    chi = np.broadcast_to(np.transpose(rel_bias_b[:, 256, :], (0, 1))[None], (128, nb, 16)).astype(np.float32)
    return bv, bvs, np.ascontiguousarray(chi)


_NC_CACHE = {}


def _get_nc(cfg_key=()):
    if cfg_key not in _NC_CACHE:
        b = Builder(dict(cfg_key))
        _NC_CACHE[cfg_key] = b.build()
    return _NC_CACHE[cfg_key]


def kernel(x_prompt, x_sample, cache_conv, cache_k, cache_v, ln_mix_g, ln_mix_b, ln_ffn_g,
           ln_ffn_b, w_up, w_down, w_in_a, conv_w_a, w_out_a, w_k, w_v, w_q_b, w_o_b, rel_bias_b, _cfg=()):
    f = lambda a: np.ascontiguousarray(np.asarray(a, dtype=np.float32))
    x_prompt, x_sample, cache_conv, cache_k, cache_v = map(f, (x_prompt, x_sample, cache_conv, cache_k, cache_v))
    rel_bias_b = f(rel_bias_b)
    xp = x_prompt[0]
    lnp = f(np.stack([ln_mix_g, ln_mix_b, ln_ffn_g, ln_ffn_b], axis=1))
    convw = f(np.transpose(f(conv_w_a).reshape(2, 3, 8, 128), (3, 0, 2, 1)))
    lnT = f(np.transpose(lnp.reshape(4, 4, 8, 128), (3, 0, 1, 2)))
    bv, bvs, chi = _bias_tables(rel_bias_b)
    ident = np.eye(128, dtype=np.float32)
    shared = dict(lnp=lnp, lnT=lnT, convw=convw, bv=bv, bvs=bvs, chi=chi, ident=ident,
                  w_up=f(w_up), w_down=f(w_down), w_in_a=f(w_in_a), w_out_a=f(w_out_a),
                  w_k=f(w_k), w_v=f(w_v), w_q_b=f(w_q_b), w_o_b=f(w_o_b))
    in_maps = []
    for c in range(NCORES):
        s = c * TPC
        xh = np.zeros((XROWS, D), np.float32)
        lo = s - HALO
        a = max(lo, 0)
        xh[a - lo:] = xp[a:s + TPC]
        xs = np.zeros((64, D), np.float32)
        xs[0:16] = x_sample[2 * c]
        xs[32:48] = x_sample[2 * c + 1]
        hv = np.full((128, 1), 0.0 if c == 0 else 1.0, np.float32)
        cc = cache_conv[:, 2 * c:2 * c + 2].reshape(2, 2, 2, 8, 128)
        cconv = f(np.transpose(cc, (4, 0, 1, 3, 2)))
        ck = cache_k[2 * c:2 * c + 2].reshape(2, 512, 8, 128)
        ckT = f(np.transpose(ck, (3, 2, 0, 1)).reshape(128, 8, 1024))
        cvv = cache_v[2 * c:2 * c + 2].reshape(2, 4, 128, 1024)
        cv = f(np.transpose(cvv, (2, 0, 1, 3)).reshape(128, 8, 1024))
        m = dict(shared)
        m.update(xh=xh, xs=xs, hv=hv, cconv=cconv, ckT=ckT, cv=cv)
        in_maps.append(m)
    nc = _get_nc(_cfg)
    ncr = dict(_cfg).get("ncores", NCORES)
    res = run_bass_kernel_spmd(nc, in_maps[:ncr], core_ids=list(range(ncr)))
    R = list(res.results)
    while len(R) < NCORES:
        R.append(R[0])
    y_prompt = np.concatenate([R[c]["y"] for c in range(NCORES)], axis=0)[None]
    y_sample = np.empty((16, 16, D), np.float32)
    conv_sample = np.empty((2, 16, 2, D), np.float32)
    k_sample = np.empty((16, 16, 16, 64), np.float32)
    v_sample = np.empty((16, 16, 16, 64), np.float32)
    for c in range(NCORES):
        r = R[c]
        for sg in range(2):
            b = 2 * c + sg
            y_sample[b] = r["ys"][sg * 32:sg * 32 + 16]
            v_sample[b] = r["vs"][sg * 32:sg * 32 + 16].reshape(16, 16, 64)
            kk = r["ksT"][:, :, sg * 32:sg * 32 + 16]
            k_sample[b] = np.transpose(kk, (2, 1, 0)).reshape(16, 16, 64)
            cs = r["convs"][:, :, sg]
            conv_sample[:, b] = np.transpose(cs, (1, 3, 2, 0)).reshape(2, 2, D)
    last = R[NCORES - 1]
    cp = last["convp"][:, :, 0]
    conv_prompt = np.ascontiguousarray(np.transpose(cp, (1, 3, 2, 0)).reshape(2, 1, 2, D))
    k_prompt = np.ascontiguousarray(np.transpose(last["kT"], (2, 1, 0)).reshape(1, 512, 16, 64))
    v_prompt = np.ascontiguousarray(last["v"].reshape(1, 512, 16, 64))
    return (y_prompt, y_sample, conv_prompt, k_prompt, v_prompt, conv_sample, k_sample, v_sample)
```
